# Optimizing a Trainium2 kernel written in Bass

```python
import jax, jax.numpy as jnp
from jax import lax
import numpy as np

D_MODEL = 1024
BATCH = 16
SEQ = 2048
DEPTH = 1
DEC_BATCH = 8
DEC_SEQ = 8192
PAST_LEN = 128

HG_HEADS = 4
HG_DK = 128
HG_DV = 128
HG_KW = HG_HEADS * HG_DK
HG_VW = HG_HEADS * HG_DV
HG_CHUNK = 64
MLA_HEADS = 8
Q_LORA = 384
KV_LORA = 256
QK_NOPE = 64
QK_ROPE = 32
V_DIM = 64
MLA_W = MLA_HEADS * V_DIM
ROPE_THETA = 10000.0
Q_BLOCK = 128
ATTN_SCALE = (QK_NOPE + QK_ROPE) ** -0.5
PEER_HEADS = 8
N_KEYS = 128
N_EXPERTS = N_KEYS * N_KEYS
PEER_DKEY = 256
PEER_HALF = PEER_DKEY // 2
PEER_TOPK = 16
TOKEN_BLOCK = 128
EPS = 1e-6
IN_SPLITS = (HG_KW, HG_KW, HG_KW, HG_VW, HG_VW, Q_LORA, KV_LORA, QK_ROPE, D_MODEL, D_MODEL)
IN_COLS = sum(IN_SPLITS)

kernel_name = "hybrid_hgrn2_mla_peer_encoder"


def _rmsnorm(x, gain):
    xf = x.astype(jnp.float32)
    y = xf * lax.rsqrt(jnp.mean(xf * xf, axis=-1, keepdims=True) + EPS)
    return (y * gain.astype(jnp.float32)).astype(x.dtype)


def _rope(x):
    s = x.shape[-2]
    half = QK_ROPE // 2
    inv = 1.0 / (ROPE_THETA ** (jnp.arange(half, dtype=jnp.float32) / half))
    ang = jnp.arange(s, dtype=jnp.float32)[:, None] * inv[None, :]
    cos, sin = jnp.cos(ang), jnp.sin(ang)
    xf = x.astype(jnp.float32)
    x1, x2 = xf[..., :half], xf[..., half:]
    return jnp.concatenate([x1 * cos - x2 * sin, x1 * sin + x2 * cos], axis=-1).astype(x.dtype)


def _hgrn2_scan(q, k, v, logf):
    b, h, s, dk = q.shape
    dv = v.shape[-1]
    nc = s // HG_CHUNK

    def to_chunks(t):
        return jnp.moveaxis(t.reshape(b, h, nc, HG_CHUNK, t.shape[-1]), 2, 0)

    lower = jnp.tril(jnp.ones((HG_CHUNK, HG_CHUNK), dtype=bool))[:, :, None]

    def step(state, inp):
        qi, ki, vi, li = inp
        a = jnp.cumsum(li, axis=-2)
        o_inter = jnp.einsum('bhck,bhkv->bhcv', qi * jnp.exp(a), state)
        diff = a[:, :, :, None, :] - a[:, :, None, :, :]
        decay = jnp.exp(jnp.where(lower, diff, -jnp.inf))
        scores = jnp.einsum('bhtk,bhtsk,bhsk->bhts', qi, decay, ki)
        o_intra = jnp.einsum('bhts,bhsv->bhtv', scores, vi)
        a_last = a[:, :, -1:, :]
        k_dec = ki * jnp.exp(a_last - a)
        new_state = jnp.exp(a_last[:, :, 0, :])[..., None] * state + jnp.einsum('bhck,bhcv->bhkv', k_dec, vi)
        return new_state, o_inter + o_intra

    s0 = jnp.zeros((b, h, dk, dv), jnp.float32)
    _, o = lax.scan(step, s0, (to_chunks(q), to_chunks(k), to_chunks(v), to_chunks(logf)))
    return jnp.moveaxis(o, 0, 2).reshape(b, h, s, dv)


def _hgrn2_branch(q_raw, ff_raw, fb_raw, i_raw, g_raw, lb_fwd, lb_bwd, out_gain):
    b, s, _ = q_raw.shape

    def heads(t, d):
        return t.astype(jnp.float32).reshape(b, s, HG_HEADS, d).transpose(0, 2, 1, 3)

    q = heads(jax.nn.silu(q_raw.astype(jnp.float32)), HG_DK)
    v = heads(i_raw, HG_DV)

    def gates(f_raw, lb):
        f = lb + (1.0 - lb) * jax.nn.sigmoid(f_raw.astype(jnp.float32))
        return heads(1.0 - f, HG_DK), heads(jnp.log(f), HG_DK)

    k_f, lf_f = gates(ff_raw, lb_fwd)
    k_b, lf_b = gates(fb_raw, lb_bwd)
    flip = lambda t: jnp.flip(t, axis=2)
    o = _hgrn2_scan(q, k_f, v, lf_f) + flip(_hgrn2_scan(flip(q), flip(k_b), flip(v), flip(lf_b)))
    o = o.transpose(0, 2, 1, 3)
    o = o * lax.rsqrt(jnp.mean(o * o, axis=-1, keepdims=True) + EPS)
    o = o * out_gain.astype(jnp.float32).reshape(HG_HEADS, HG_DV)
    o = o.reshape(b, s, HG_VW) * jax.nn.silu(g_raw.astype(jnp.float32))
    return o.astype(q_raw.dtype)


def _mla_branch(cq_raw, ckv_raw, kpe_raw, q_gain, kv_gain, w_uq, w_ukv):
    b, s, _ = cq_raw.shape
    q = (_rmsnorm(cq_raw, q_gain) @ w_uq).reshape(b, s, MLA_HEADS, QK_NOPE + QK_ROPE).transpose(0, 2, 1, 3)
    q_nope, q_pe = q[..., :QK_NOPE], _rope(q[..., QK_NOPE:])
    kv = (_rmsnorm(ckv_raw, kv_gain) @ w_ukv).reshape(b, s, MLA_HEADS, QK_NOPE + V_DIM).transpose(0, 2, 1, 3)
    k_nope, v = kv[..., :QK_NOPE], kv[..., QK_NOPE:]
    k_pe = _rope(kpe_raw)
    nb = s // Q_BLOCK

    def blocks(t):
        return jnp.moveaxis(t.reshape(b, MLA_HEADS, nb, Q_BLOCK, t.shape[-1]), 2, 0)

    def attend(blk):
        qn, qp = blk
        sc = jnp.einsum('bhqd,bhkd->bhqk', qn, k_nope) + jnp.einsum('bhqr,bkr->bhqk', qp, k_pe)
        p = jax.nn.softmax(sc.astype(jnp.float32) * ATTN_SCALE, axis=-1).astype(v.dtype)
        return jnp.einsum('bhqk,bhkd->bhqd', p, v)

    o = lax.map(attend, (blocks(q_nope), blocks(q_pe)))
    o = jnp.moveaxis(o, 0, 2).reshape(b, MLA_HEADS, s, V_DIM)
    return o.transpose(0, 2, 1, 3).reshape(b, s, MLA_W)


def _peer(z, w_pq, sub_keys, u_tab, v_tab):
    b, s, d = z.shape
    zb = z.reshape((b * s) // TOKEN_BLOCK, TOKEN_BLOCK, d)

    def block(zt):
        q = (zt @ w_pq).reshape(TOKEN_BLOCK, PEER_HEADS, 2, PEER_HALF)
        sc = jnp.einsum('thpc,phnc->thpn', q, sub_keys).astype(jnp.float32)
        vals, idx = lax.top_k(sc, PEER_TOPK)
        cand = vals[:, :, 0, :, None] + vals[:, :, 1, None, :]
        cand_idx = idx[:, :, 0, :, None] * N_KEYS + idx[:, :, 1, None, :]
        top, pos = lax.top_k(cand.reshape(TOKEN_BLOCK, PEER_HEADS, PEER_TOPK * PEER_TOPK), PEER_TOPK)
        eidx = jnp.take_along_axis(cand_idx.reshape(TOKEN_BLOCK, PEER_HEADS, PEER_TOPK * PEER_TOPK), pos, axis=-1)
        g = jax.nn.softmax(top, axis=-1)
        u = jnp.take(u_tab, eidx, axis=0)
        ve = jnp.take(v_tab, eidx, axis=0)
        act = jax.nn.gelu(jnp.einsum('td,thkd->thk', zt, u).astype(jnp.float32), approximate=False) * g
        return jnp.einsum('thk,thkd->td', act.astype(zt.dtype), ve)

    return lax.map(block, zb).reshape(b, s, d)


def _encode(x, lb_logits, norm_mix, w_in, hg_out_norm, w_br_h, q_a_norm, w_uq, kv_a_norm, w_ukv,
            w_br_a, w_out, norm_ffn, peer_wq, peer_sub_keys, peer_u, peer_v, norm_final):
    lb_all = jnp.cumsum(jax.nn.softmax(lb_logits.astype(jnp.float32), axis=0), axis=0)
    split_points = [int(p) for p in np.cumsum(IN_SPLITS)[:-1]]
    for l in range(DEPTH):
        h = _rmsnorm(x, norm_mix[l])
        (q_h, f_fw, f_bw, i_h, g_h, c_q, c_kv, k_pe, gl_h, gl_a) = jnp.split(h @ w_in[l], split_points, axis=-1)
        y_h = _hgrn2_branch(q_h, f_fw, f_bw, i_h, g_h, lb_all[l, 0], lb_all[l, 1], hg_out_norm[l]) @ w_br_h[l]
        y_a = _mla_branch(c_q, c_kv, k_pe, q_a_norm[l], kv_a_norm[l], w_uq[l], w_ukv[l]) @ w_br_a[l]
        merged = jax.nn.sigmoid(gl_h) * y_h + jax.nn.sigmoid(gl_a) * y_a
        x = x + merged @ w_out[l]
        x = x + _peer(_rmsnorm(x, norm_ffn[l]), peer_wq[l], peer_sub_keys[l], peer_u[l], peer_v[l])
    return _rmsnorm(x, norm_final)


def setup_inputs(seed: int = 0) -> dict:
    key = jax.random.key(seed)
    ks = jax.random.split(key, 19)
    f32 = jnp.float32

    def nrm(k, shape, scale):
        return jax.random.normal(k, shape, f32) * scale

    def gain(k, shape):
        return 1.0 + 0.01 * jax.random.normal(k, shape, f32)

    return {
        "x_prompt": nrm(ks[0], (BATCH, SEQ, D_MODEL), 1.0),
        "x_sample": nrm(ks[1], (DEC_BATCH, DEC_SEQ, D_MODEL), 1.0),
        "lb_logits": nrm(ks[2], (DEPTH + 1, 2, HG_KW), 0.5),
        "norm_mix": gain(ks[3], (DEPTH, D_MODEL)),
        "w_in": nrm(ks[4], (DEPTH, D_MODEL, IN_COLS), D_MODEL ** -0.5),
        "hg_out_norm": gain(ks[5], (DEPTH, HG_VW)),
        "w_br_h": nrm(ks[6], (DEPTH, HG_VW, D_MODEL), HG_VW ** -0.5),
        "q_a_norm": gain(ks[7], (DEPTH, Q_LORA)),
        "w_uq": nrm(ks[8], (DEPTH, Q_LORA, MLA_HEADS * (QK_NOPE + QK_ROPE)), Q_LORA ** -0.5),
        "kv_a_norm": gain(ks[9], (DEPTH, KV_LORA)),
        "w_ukv": nrm(ks[10], (DEPTH, KV_LORA, MLA_HEADS * (QK_NOPE + V_DIM)), KV_LORA ** -0.5),
        "w_br_a": nrm(ks[11], (DEPTH, MLA_W, D_MODEL), MLA_W ** -0.5),
        "w_out": nrm(ks[12], (DEPTH, D_MODEL, D_MODEL), D_MODEL ** -0.5),
        "norm_ffn": gain(ks[13], (DEPTH, D_MODEL)),
        "peer_wq": nrm(ks[14], (DEPTH, D_MODEL, PEER_HEADS * PEER_DKEY), D_MODEL ** -0.5),
        "peer_sub_keys": nrm(ks[15], (DEPTH, 2, PEER_HEADS, N_KEYS, PEER_HALF), PEER_HALF ** -0.5),
        "peer_u": nrm(ks[16], (DEPTH, N_EXPERTS, D_MODEL), D_MODEL ** -0.5),
        "peer_v": nrm(ks[17], (DEPTH, N_EXPERTS, D_MODEL), (PEER_HEADS * PEER_TOPK) ** -0.5),
        "norm_final": gain(ks[18], (D_MODEL,)),
    }


def reference(x_prompt, x_sample, lb_logits, norm_mix, w_in, hg_out_norm, w_br_h, q_a_norm, w_uq,
              kv_a_norm, w_ukv, w_br_a, w_out, norm_ffn, peer_wq, peer_sub_keys, peer_u, peer_v, norm_final):
    y_prompt = _encode(x_prompt, lb_logits, norm_mix, w_in, hg_out_norm, w_br_h, q_a_norm, w_uq, kv_a_norm,
                       w_ukv, w_br_a, w_out, norm_ffn, peer_wq, peer_sub_keys, peer_u, peer_v, norm_final)
    y_sample = _encode(x_sample, lb_logits, norm_mix, w_in, hg_out_norm, w_br_h, q_a_norm, w_uq, kv_a_norm,
                       w_ukv, w_br_a, w_out, norm_ffn, peer_wq, peer_sub_keys, peer_u, peer_v, norm_final)
    return (y_prompt, y_sample)
```

```python
import contextlib
import numpy as np
import concourse.bass as bass
import concourse.mybir as mybir
from concourse.bass_utils import run_bass_kernel_spmd

F32 = mybir.dt.float32
BF16 = mybir.dt.bfloat16
I32 = mybir.dt.int32
U32 = mybir.dt.uint32
ALU = mybir.AluOpType
AF = mybir.ActivationFunctionType
AX = mybir.AxisListType

EPS = 1e-6
EPOCH = 30000
DLIMIT = 30000
ENGS = ("pe", "act", "dve", "pool", "sp")


class Buf:
    _n = 0

    def __init__(self, name=""):
        Buf._n += 1
        self.id = Buf._n
        self.name = name
        self.w = None
        self.r = []
        self.psum = False


class T:
    def __init__(self, t, buf):
        self.t = t
        self.b = buf

    def __getitem__(self, k):
        return self.t[k]


class Ring:
    def __init__(self, tiles):
        self.tiles = tiles
        self.i = 0

    def next(self):
        t = self.tiles[self.i % len(self.tiles)]
        self.i += 1
        return t


class Sched:
    def __init__(self, nc, gstack):
        self.nc = nc
        self.gstack = gstack
        self.stack = gstack
        self.q = {e: [] for e in ENGS}
        self.count = {e: 0 for e in ENGS}
        self.seen = {e: {} for e in ENGS}
        self.sems = {}
        self.all_tokens = {}
        self.n_sb = 0
        self.dmasem = {}
        self.free_dsems = []
        self.n_dsem = 0

    def sb(self, shape, dtype, name="t"):
        self.n_sb += 1
        t = self.stack.enter_context(self.nc.sbuf_tensor(f"{name}_{self.n_sb}", list(shape), dtype))
        return T(t, Buf(name))

    def ring(self, n, shape, dtype, name="r"):
        return Ring([self.sb(shape, dtype, name) for _ in range(n)])

    def ps(self, shape, dtype, name="ps"):
        self.n_sb += 1
        t = self.stack.enter_context(self.nc.psum_tensor(f"{name}_{self.n_sb}", list(shape), dtype))
        b = Buf(name)
        b.psum = True
        return T(t, b)

    def psring(self, n, name="ps"):
        return Ring([self.ps([128, 512], F32, name) for _ in range(n)])

    def _dma_tok(self, buf):
        ent = self.dmasem.get(buf.id)
        if ent is None or ent[1] + 16 > DLIMIT:
            if self.free_dsems:
                j, base = self.free_dsems.pop()
            else:
                j, base = self.n_dsem, 0
                self.n_dsem += 1
            ent = [j, base]
            self.dmasem[buf.id] = ent
        ent[1] += 16
        return (("dsem", ent[0]), ent[1])

    def op(self, eng, fn, reads=(), writes=(), dma=None, noreg=()):
        reads = [x.b if isinstance(x, T) else x for x in reads]
        writes = [x.b if isinstance(x, T) else x for x in writes]
        noreg = [x.b if isinstance(x, T) else x for x in noreg]
        writes = writes + [r for r in reads if r.psum and r not in writes]
        reads = [r for r in reads if not r.psum]
        need = {}

        def add(tok):
            if tok is None:
                return
            k, v = tok
            if need.get(k, 0) < v:
                need[k] = v

        for r in reads:
            add(r.w)
        for r in noreg:
            add(r.w)
        for w in writes:
            add(w.w)
            for t in w.r:
                add(t)
        waits = []
        seen = self.seen[eng]
        for k, v in need.items():
            if eng == "pe" and k[0] == "eng" and k[1] == "pe":
                continue
            if seen.get(k, 0) >= v:
                continue
            seen[k] = v
            waits.append((k, v))
        if dma is not None:
            dma = dma.b if isinstance(dma, T) else dma
            tok = self._dma_tok(dma)
            inc = (tok[0], 16)
        else:
            idx = self.count[eng]
            self.count[eng] += 1
            key = ("eng", eng, idx // EPOCH)
            tok = (key, idx % EPOCH + 1)
            inc = (key, 1)
        key = tok[0]
        self.all_tokens[key] = max(self.all_tokens.get(key, 0), tok[1])
        self.q[eng].append((fn, waits, inc))
        for r in reads:
            r.r.append(tok)
        for w in writes:
            w.w = tok
            w.r = []
        return tok

    def barrier(self):
        for e in ENGS:
            waits = []
            seen = self.seen[e]
            for k, v in self.all_tokens.items():
                if seen.get(k, 0) >= v:
                    continue
                seen[k] = v
                waits.append((k, v))
            if waits:
                self.q[e].append((None, waits, None))
        for bid, (j, cnt) in self.dmasem.items():
            if cnt < DLIMIT - 4000:
                self.free_dsems.append((j, cnt))
        self.dmasem = {}

    @contextlib.contextmanager
    def phase(self):
        with contextlib.ExitStack() as st:
            self.stack = st
            yield
            self.barrier()
        self.stack = self.gstack

    def emit(self):
        nc = self.nc
        sems = {}
        for k in self.all_tokens.keys():
            nm = "s_" + "_".join(str(x) for x in k)
            sems[k] = self.gstack.enter_context(nc.semaphore(nm))
        q = self.q

        def run(engobj, lst):
            for fn, waits, inc in lst:
                for k, v in waits:
                    engobj.wait_ge(sems[k], v)
                if fn is not None:
                    ins = fn(engobj)
                    ins.then_inc(sems[inc[0]], inc[1])

        with nc.Block() as block:
            @block.tensor
            def _(e):
                run(e, q["pe"])

            @block.scalar
            def _(e):
                run(e, q["act"])

            @block.vector
            def _(e):
                run(e, q["dve"])

            @block.gpsimd
            def _(e):
                run(e, q["pool"])

            @block.sync
            def _(e):
                run(e, q["sp"])


def dma(S, out, in_, reads=(), writes=(), owner=None, eng="sp", slow=False):
    if slow:
        S.op(eng, lambda e: e.dma_start(out=out, in_=in_, allow_slow_non_contiguous=True), reads, writes, dma=owner)
    else:
        S.op(eng, lambda e: e.dma_start(out=out, in_=in_), reads, writes, dma=owner)


def mm(S, out, lhsT, rhs, start, stop, reads, writes):
    S.op("pe", lambda e: e.matmul(out, lhsT=lhsT, rhs=rhs, start=start, stop=stop), reads, writes)


def tr(S, out, in_, ident, reads, writes):
    S.op("pe", lambda e: e.transpose(out=out, in_=in_, identity=ident), reads, writes)


def act(S, out, in_, func, reads, writes, scale=None, bias=None, accum=None):
    kw = {}
    if scale is not None:
        kw["scale"] = scale
    if bias is not None:
        kw["bias"] = bias
    if accum is not None:
        kw["accum_out"] = accum
    S.op("act", lambda e: e.activation(out=out, in_=in_, func=func, **kw), reads, writes)


def tt(S, eng, out, in0, in1, op, reads, writes):
    S.op(eng, lambda e: e.tensor_tensor(out=out, in0=in0, in1=in1, op=op), reads, writes)


def ts(S, eng, out, in0, s1, s2, op0, op1, reads, writes):
    if op1 is None:
        S.op(eng, lambda e: e.tensor_scalar(out=out, in0=in0, scalar1=s1, scalar2=None, op0=op0), reads, writes)
    else:
        S.op(eng, lambda e: e.tensor_scalar(out=out, in0=in0, scalar1=s1, scalar2=s2, op0=op0, op1=op1), reads, writes)


def stt(S, out, in0, scalar, in1, op0, op1, reads, writes, accum=None, noreg=()):
    if accum is None:
        S.op("dve", lambda e: e.scalar_tensor_tensor(out=out, in0=in0, scalar=scalar, in1=in1, op0=op0, op1=op1), reads, writes, noreg=noreg)
    else:
        S.op("dve", lambda e: e.scalar_tensor_tensor(out=out, in0=in0, scalar=scalar, in1=in1, op0=op0, op1=op1, accum_out=accum), reads, writes, noreg=noreg)


def cp(S, eng, out, in_, reads, writes):
    if eng == "act":
        S.op("act", lambda e: e.activation(out=out, in_=in_, func=AF.Copy), reads, writes)
    else:
        S.op(eng, lambda e: e.tensor_copy(out=out, in_=in_), reads, writes)


def memset(S, eng, ap, val, writes):
    S.op(eng, lambda e: e.memset(ap, val), (), writes)


def rstd_from_ss(S, rstd, ss, n, reads_t):
    act(S, rstd[:], ss[:], AF.Ln, [ss], [rstd], scale=1.0 / n, bias=EPS)
    act(S, rstd[:], rstd[:], AF.Exp, [rstd], [rstd], scale=-0.5)


def build(seq_lens, n_exp=16384, upto=9, sec=99):
    TT = sum(seq_lens)
    assert all(L % 512 == 0 for L in seq_lens)
    NT = TT // 512
    NCH = TT // 64
    nc = bass.Bass("TRN2", target_bir_lowering=False)

    def din(name, shape, dt=F32):
        return nc.dram_tensor(name, list(shape), dt, kind="ExternalInput").ap()

    def dscr(name, shape, dt):
        return nc.dram_tensor(name, list(shape), dt, kind="Internal").ap()

    x = din("x", [TT, 1024])
    w_in = din("w_in", [1024, 5312])
    lbl = din("lbl", [2, 2, 512])
    g_mix = din("g_mix", [1024])
    g_hg = din("g_hg", [512])
    w_brh = din("w_brh", [512, 1024])
    g_qa = din("g_qa", [384])
    wq_n = din("wq_n", [384, 512])
    wq_p = din("wq_p", [384, 256])
    wq_s = din("wq_s", [384, 256])
    g_kva = din("g_kva", [256])
    wk_n = din("wk_n", [256, 512])
    wv = din("wv", [256, 512])
    w_bra = din("w_bra", [512, 1024])
    w_out = din("w_out", [1024, 1024])
    g_ffn = din("g_ffn", [1024])
    w_pq = din("w_pq", [1024, 2048])
    skeys = din("skeys", [16, 128, 128])
    pu = din("pu", [n_exp, 1024])
    pv = din("pv", [n_exp, 1024])
    g_fin = din("g_fin", [1024])
    cos4 = din("cos4", [128, 8192])
    sin4 = din("sin4", [128, 8192])
    ident_d = din("ident", [128, 128])
    masks_d = din("masks", [2, 64, 64])
    iota_d = din("iota16", [128, 16])
    y = nc.dram_tensor("y", [TT, 1024], F32, kind="ExternalOutput").ap()

    QTs = [dscr(f"QT{d}", [512, TT], BF16) for d in range(2)]
    KTs = [dscr(f"KT{d}", [512, TT], BF16) for d in range(2)]
    KDs = [dscr(f"KD{d}", [TT, 512], BF16) for d in range(2)]
    DECs = [dscr(f"DEC{d}", [512, NCH], F32) for d in range(2)]
    VH = dscr("VH", [TT, 512], BF16)
    GH = dscr("GH", [TT, 512], BF16)
    AQ = dscr("AQ", [768, TT], BF16)
    AKN = dscr("AKN", [512, TT], BF16)
    AKP = dscr("AKP", [32, TT], BF16)
    AV = dscr("AV", [TT, 520], BF16)
    SGH = dscr("SGH", [1024, TT], BF16)
    SGA = dscr("SGA", [1024, TT], BF16)
    OF = dscr("OF", [TT, 512], F32)
    OT = dscr("OT", [512, TT], BF16)
    AT = dscr("AT", [512, TT], BF16)
    X1 = dscr("X1", [TT, 1024], F32)
    UV = dscr("UV", [n_exp, 2048], BF16)

    with contextlib.ExitStack() as gstack:
        S = Sched(nc, gstack)
        identb = S.sb([128, 128], BF16, "identb")
        onesb = S.sb([128, 128], BF16, "onesb")
        onesf = S.sb([128, 128], F32, "onesf")
        stage0 = S.sb([128, 128], F32, "stage0")
        dma(S, stage0[:], ident_d[:, :], (), [stage0], stage0)
        cp(S, "dve", identb[:], stage0[:], [stage0], [identb])
        memset(S, "pool", onesb[:], 1.0, [onesb])
        memset(S, "pool", onesf[:], 1.0, [onesf])

        def load_w_bf16(dst, dst_ap, src_ap, shape, stg_ring, i):
            P, N = shape
            for n0 in range(0, N, 1024):
                n1 = min(N, n0 + 1024)
                stg = stg_ring.next()
                v = stg[0:P, 0:n1 - n0]
                dma(S, v, src_ap[:, n0:n1], (), [stg], stg)
                cp(S, ("dve", "pool", "act")[(i + n0 // 1024) % 3], dst_ap[:, n0:n1], v, [stg], [dst])

        with S.phase():
            stg_ring = S.ring(3, [128, 1024], F32, "stg")
            W = S.sb([128, 8, 5312], BF16, "W")
            w_in_v = w_in.rearrange("(c p) n -> p c n", p=128)
            k = 0
            for c in range(8):
                load_w_bf16(W, W[:, c, :], w_in_v[:, c, :], [128, 5312], stg_ring, k)
                k += 1
            Wqn = S.sb([128, 3, 512], BF16, "Wqn")
            Wqp = S.sb([128, 3, 256], BF16, "Wqp")
            Wqs = S.sb([128, 3, 256], BF16, "Wqs")
            Wkn = S.sb([128, 2, 512], BF16, "Wkn")
            Wv = S.sb([128, 2, 512], BF16, "Wv")
            for (dst, src, nchunk, ncol) in ((Wqn, wq_n, 3, 512), (Wqp, wq_p, 3, 256), (Wqs, wq_s, 3, 256),
                                             (Wkn, wk_n, 2, 512), (Wv, wv, 2, 512)):
                for c in range(nchunk):
                    load_w_bf16(dst, dst[:, c, :], src[c * 128:(c + 1) * 128, :], [128, ncol], stg_ring, k)
                    k += 1
            gmix = S.sb([128, 8], F32, "gmix")
            dma(S, gmix[:], g_mix.rearrange("(c p) -> p c", p=128), (), [gmix], gmix, slow=True)
            gqa = S.sb([128, 3], F32, "gqa")
            dma(S, gqa[:], g_qa.rearrange("(c p) -> p c", p=128), (), [gqa], gqa, slow=True)
            gkva = S.sb([128, 2], F32, "gkva")
            dma(S, gkva[:], g_kva.rearrange("(c p) -> p c", p=128), (), [gkva], gkva, slow=True)
            l0 = S.sb([128, 2, 4], F32, "l0")
            l1 = S.sb([128, 2, 4], F32, "l1")
            lb = S.sb([128, 2, 4], F32, "lb")
            oml = S.sb([128, 2, 4], F32, "oml")
            dma(S, l0[:], lbl[0].rearrange("d (h p) -> p d h", p=128), (), [l0], l0, slow=True)
            dma(S, l1[:], lbl[1].rearrange("d (h p) -> p d h", p=128), (), [l1], l1, slow=True)
            tt(S, "dve", l0[:], l0[:], l1[:], ALU.subtract, [l0, l1], [l0])
            act(S, lb[:], l0[:], AF.Sigmoid, [l0], [lb])
            ts(S, "dve", oml[:], lb[:], -1.0, 1.0, ALU.mult, ALU.add, [lb], [oml])
            noml = S.sb([128, 2, 4], F32, "noml")
            ts(S, "dve", noml[:], lb[:], 1.0, -1.0, ALU.mult, ALU.add, [lb], [noml])

            xring = S.ring(2, [128, 1024], F32, "x")
            ssr = S.ring(2, [128, 1], F32, "ss")
            rsr = S.ring(2, [128, 1], F32, "rstd")
            xnr = S.ring(2, [128, 1024], BF16, "xn")
            hTr = S.ring(1, [128, 8, 512], BF16, "hT")
            psT = S.psring(2, "psT")
            psM = S.psring(4, "psM")
            psX = S.psring(2, "psX")
            f32r = S.ring(6, [128, 512], F32, "f32r")
            sqr = S.ring(2, [128, 512], F32, "sq")
            Ar = S.ring(2, [128, 512], F32, "A")
            logr = S.ring(2, [128, 512], F32, "logf")
            omfr = S.ring(2, [128, 512], F32, "omf")
            aendr = S.ring(2, [128, 9], F32, "aend")
            decr = S.ring(2, [128, 8], F32, "dec")
            bfr = S.ring(6, [128, 512], BF16, "bfr")
            kdtr = S.ring(2, [128, 512], BF16, "kdT")
            kdall = [S.ring(1, [128, 4, 512], BF16, "kdall") for _ in range(2)]
            vtr = S.ring(1, [128, 4, 512], BF16, "vt")
            gtr = S.ring(1, [128, 4, 512], BF16, "gt")
            cqg = S.sb([128, 3, 512], F32, "cqg")
            sqc = S.sb([128, 3, 512], BF16, "sqc")
            cqn = S.sb([128, 3, 512], BF16, "cqn")
            rsb = S.sb([128, 512], F32, "rsb")
            vaugr = S.ring(1, [128, 4, 520], BF16, "vaug")
            for t_ in vaugr.tiles:
                memset(S, "pool", t_[:], 1.0, [t_])
            cosr = S.ring(1, [128, 512], F32, "cos")
            sinr = S.ring(1, [128, 512], F32, "sin")
            ones512 = S.sb([128, 512], F32, "ones512")
            memset(S, "pool", ones512[:], 1.0, [ones512])

            def fm_group(hT, col0, ncol):
                ps = psM.next()
                for c in range(8):
                    mm(S, ps[0:ncol, :], W[:, c, col0:col0 + ncol], hT[:, c, :], c == 0, c == 7, [W, hT], [ps])
                return ps

            def tm_group(hT, s, col0):
                ps = psM.next()
                for c in range(8):
                    mm(S, ps[:, :], hT[:, c, s * 128:(s + 1) * 128], W[:, c, col0:col0 + 512], c == 0, c == 7, [W, hT], [ps])
                return ps

            tile_pos = []
            for L in seq_lens:
                for p0 in range(0, L, 512):
                    tile_pos.append(p0)

            for n in range(NT if sec > 0 else 0):
                t0 = n * 512
                p0 = tile_pos[n]
                hT = hTr.next()
                for s in range(4):
                    xt = xring.next()
                    dma(S, xt[:], x[t0 + s * 128:t0 + (s + 1) * 128, :], (), [xt], xt)
                    ss = ssr.next()
                    rs = rsr.next()
                    xn = xnr.next()
                    act(S, xn[:], xt[:], AF.Square, [xt], [xn, ss], accum=ss[:, 0:1])
                    rstd_from_ss(S, rs, ss, 1024.0, None)
                    act(S, xn[:], xt[:], AF.Copy, [xt, rs], [xn], scale=rs[:, 0:1])
                    pt = psT.next()
                    ptb = pt[:].bitcast(BF16)
                    for c in range(8):
                        tr(S, ptb[:, c * 128:(c + 1) * 128], xn[:, c * 128:(c + 1) * 128], identb[:], [xn, identb], [pt])
                    tt(S, "dve", hT[:, :, s * 128:(s + 1) * 128], ptb.rearrange("p (c t) -> p c t", t=128),
                       gmix[:].unsqueeze(2).to_broadcast([128, 8, 128]), ALU.mult, [pt, gmix], [hT])
                if sec < 2:
                    continue
                cosT = cosr.next()
                sinT = sinr.next()
                dma(S, cosT[:], cos4[:, p0:p0 + 512], (), [cosT], cosT)
                dma(S, sinT[:], sin4[:, p0:p0 + 512], (), [sinT], sinT)
                kda = [kdall[0].next(), kdall[1].next()]
                pend1 = []
                for hc in range(4):
                    psq = fm_group(hT, hc * 128, 128)
                    sq = sqr.next()
                    act(S, sq[:], psq[:], AF.Exp, [psq], [sq], scale=-1.0)
                    ts(S, "dve", sq[:], sq[:], 1.0, None, ALU.add, None, [sq], [sq])
                    S.op("dve", lambda e, sq=sq: e.reciprocal(out=sq[:], in_=sq[:]), [sq], [sq])
                    tt(S, "dve", sq[:], psq[:], sq[:], ALU.mult, [psq, sq], [sq])
                    for d in range(2):
                        psf = fm_group(hT, 512 + d * 512 + hc * 128, 128)
                        for fn_ in pend1:
                            fn_()
                        pend1.clear()
                        sig = f32r.next()
                        nsig = f32r.next()
                        act(S, sig[:], psf[:], AF.Exp, [psf], [sig], scale=-1.0)
                        ts(S, "dve", sig[:], sig[:], 1.0, None, ALU.add, None, [sig], [sig])
                        S.op("dve", lambda e, sig=sig: e.reciprocal(out=sig[:], in_=sig[:]), [sig], [sig])
                        logf = logr.next()
                        act(S, logf[:], sig[:], AF.Ln, [sig, oml, lb], [logf], scale=oml[:, d, hc:hc + 1], bias=lb[:, d, hc:hc + 1])
                        omf = omfr.next()
                        ts(S, "dve", omf[:], sig[:], noml[:, d, hc:hc + 1], oml[:, d, hc:hc + 1], ALU.mult, ALU.add, [sig, oml, noml], [omf])
                        A = Ar.next()
                        S.op("dve", lambda e, A=A, logf=logf: e.tensor_tensor_scan(out=A[:], data0=ones512[:], data1=logf[:], initial=0.0,
                                                                                  op0=ALU.mult, op1=ALU.add), [ones512, logf], [A])
                        A3 = A[:].rearrange("p (c j) -> p c j", j=64)
                        aend = aendr.next()
                        memset(S, "pool", aend[:, 0:1], 0.0, [aend])
                        cp(S, "dve", aend[:, 1:9], A3[:, :, 63], [A], [aend])
                        dec = decr.next()
                        tt(S, "dve", dec[:], aend[:, 1:9], aend[:, 0:8], ALU.subtract, [aend], [dec])
                        act(S, dec[:], dec[:], AF.Exp, [dec], [dec])
                        dma(S, DECs[d][hc * 128:(hc + 1) * 128, n * 8:(n + 1) * 8], dec[:], [dec], (), dec)
                        lo_b = aend[:, 0:8].unsqueeze(2).to_broadcast([128, 8, 64])
                        hi_b = aend[:, 1:9].unsqueeze(2).to_broadcast([128, 8, 64])
                        a = f32r.next()
                        a3 = a[:].rearrange("p (c j) -> p c j", j=64)
                        t1 = f32r.next()
                        t13 = t1[:].rearrange("p (c j) -> p c j", j=64)
                        if d == 0:
                            tt(S, "dve", a3, A3, lo_b, ALU.subtract, [A, aend], [a])
                            tt(S, "dve", t13, hi_b, A3, ALU.subtract, [A, aend], [t1])
                        else:
                            tt(S, "dve", t13, hi_b, A3, ALU.subtract, [A, aend], [t1])
                            tt(S, "pool", a[:], t1[:], logf[:], ALU.add, [t1, logf], [a])
                            tt(S, "pool", t1[:], A[:], logf[:], ALU.subtract, [A, logf], [t1])
                            tt(S, "dve", t13, t13, lo_b, ALU.subtract, [t1, aend], [t1])
                        ea = f32r.next()
                        ena = f32r.next()
                        act(S, ea[:], a[:], AF.Exp, [a], [ea])
                        act(S, ena[:], a[:], AF.Exp, [a], [ena], scale=-1.0)
                        act(S, t1[:], t1[:], AF.Exp, [t1], [t1])
                        qt = bfr.next()
                        kt = bfr.next()
                        tt(S, "dve", qt[:], sq[:], ea[:], ALU.mult, [sq, ea], [qt])
                        tt(S, "pool", kt[:], omf[:], ena[:], ALU.mult, [omf, ena], [kt])
                        dma(S, QTs[d][hc * 128:(hc + 1) * 128, t0:t0 + 512], qt[:], [qt], (), qt)
                        dma(S, KTs[d][hc * 128:(hc + 1) * 128, t0:t0 + 512], kt[:], [kt], (), kt)
                        kdT = kdtr.next()
                        tt(S, "dve", kdT[:], omf[:], t1[:], ALU.mult, [omf, t1], [kdT])
                        def fin_(kdT=kdT, d=d, hc=hc):
                            pt = psT.next()
                            ptb = pt[:].bitcast(BF16)
                            for s in range(4):
                                tr(S, ptb[:, s * 128:(s + 1) * 128], kdT[:, s * 128:(s + 1) * 128], identb[:], [kdT, identb], [pt])
                            cp(S, "act", kda[d][:, :, hc * 128:(hc + 1) * 128], ptb[:, 0:512].rearrange("p (s k) -> p s k", k=128), [pt], [kda[d]])
                        pend1.append(fin_)
                for fn_ in pend1:
                    fn_()
                pend1.clear()
                if sec < 3:
                    continue
                for d in range(2):
                    dma(S, KDs[d][t0:t0 + 512, :].rearrange("(s p) f -> p s f", p=128), kda[d][:], [kda[d]], (), kda[d])
                vt = vtr.next()
                gt = gtr.next()
                for s in range(4):
                    psv = tm_group(hT, s, 1536)
                    cp(S, "dve", vt[:, s, :], psv[:], [psv], [vt])
                    psg = tm_group(hT, s, 2048)
                    act(S, gt[:, s, :], psg[:], AF.Silu, [psg], [gt])
                dma(S, VH[t0:t0 + 512, :].rearrange("(s p) f -> p s f", p=128), vt[:], [vt], (), vt)
                dma(S, GH[t0:t0 + 512, :].rearrange("(s p) f -> p s f", p=128), gt[:], [gt], (), gt)

                if sec < 4:
                    continue
                def lora_norm(col0, nchunk, gain, D):
                    for c in range(nchunk):
                        ps = fm_group(hT, col0 + c * 128, 128)
                        act(S, sqc[:, c, :], ps[:], AF.Square, [ps], [sqc])
                        ts(S, "dve", cqg[:, c, :], ps[:], gain[:, c:c + 1], None, ALU.mult, None, [ps, gain], [cqg])
                    pss = psX.next()
                    for c in range(nchunk):
                        mm(S, pss[:, :], onesb[:], sqc[:, c, :], c == 0, c == nchunk - 1, [onesb, sqc], [pss])
                    act(S, rsb[:], pss[:], AF.Ln, [pss], [rsb], scale=1.0 / D, bias=EPS)
                    act(S, rsb[:], rsb[:], AF.Exp, [rsb], [rsb], scale=-0.5)
                    for c in range(nchunk):
                        tt(S, ("dve", "pool")[c % 2], cqn[:, c, :], cqg[:, c, :], rsb[:], ALU.mult, [cqg, rsb], [cqn])

                def up_fm(Wt, nchunk, col0, ncol):
                    ps = psM.next()
                    for c in range(nchunk):
                        mm(S, ps[0:ncol, :], Wt[:, c, col0:col0 + ncol], cqn[:, c, :], c == 0, c == nchunk - 1, [Wt, cqn], [ps])
                    return ps

                lora_norm(2560, 3, gqa, 384.0)
                for g in range(4):
                    ps = up_fm(Wqn, 3, g * 128, 128)
                    o = bfr.next()
                    cp(S, ("dve", "act")[g % 2], o[:], ps[:], [ps], [o])
                    for hh in range(2):
                        h = 2 * g + hh
                        dma(S, AQ[h * 96:h * 96 + 64, t0:t0 + 512], o[hh * 64:(hh + 1) * 64, :], [o], (), o)
                for g in range(2):
                    pp = up_fm(Wqp, 3, g * 128, 128)
                    pw = up_fm(Wqs, 3, g * 128, 128)
                    r1 = f32r.next()
                    r2 = f32r.next()
                    tt(S, "dve", r1[:], pp[:], cosT[:], ALU.mult, [pp, cosT], [r1])
                    tt(S, "dve", r2[:], pw[:], sinT[:], ALU.mult, [pw, sinT], [r2])
                    o = bfr.next()
                    tt(S, "pool", o[:], r1[:], r2[:], ALU.add, [r1, r2], [o])
                    for hh in range(4):
                        h = 4 * g + hh
                        dma(S, AQ[h * 96 + 64:h * 96 + 96, t0:t0 + 512], o[hh * 32:(hh + 1) * 32, :], [o], (), o)
                if sec < 5:
                    continue
                lora_norm(2944, 2, gkva, 256.0)
                for g in range(4):
                    ps = up_fm(Wkn, 2, g * 128, 128)
                    o = bfr.next()
                    cp(S, ("dve", "act")[g % 2], o[:], ps[:], [ps], [o])
                    dma(S, AKN[g * 128:(g + 1) * 128, t0:t0 + 512], o[:], [o], (), o)
                vaug = vaugr.next()
                for s in range(4):
                    ps = psM.next()
                    for c in range(2):
                        mm(S, ps[:, :], cqn[:, c, s * 128:(s + 1) * 128], Wv[:, c, :], c == 0, c == 1, [Wv, cqn], [ps])
                    cp(S, ("dve", "act")[s % 2], vaug[:, s, :].rearrange("p (h e) -> p h e", e=65)[:, :, 0:64],
                       ps[:].rearrange("p (h e) -> p h e", e=64), [ps], [vaug])
                dma(S, AV[t0:t0 + 512, :].rearrange("(s p) f -> p s f", p=128), vaug[:], [vaug], (), vaug)
                pk = fm_group(hT, 3200, 32)
                pks = fm_group(hT, 5280, 32)
                r1 = f32r.next()
                r2 = f32r.next()
                tt(S, "dve", r1[0:32, :], pk[0:32, :], cosT[0:32, :], ALU.mult, [pk, cosT], [r1])
                tt(S, "dve", r2[0:32, :], pks[0:32, :], sinT[0:32, :], ALU.mult, [pks, sinT], [r2])
                o = bfr.next()
                tt(S, "pool", o[0:32, :], r1[0:32, :], r2[0:32, :], ALU.add, [r1, r2], [o])
                dma(S, AKP[:, t0:t0 + 512], o[0:32, :], [o], (), o)
                if sec < 6:
                    continue
                for gi in range(2):
                    dst = (SGH, SGA)[gi]
                    for c in range(8):
                        ps = fm_group(hT, 3232 + gi * 1024 + c * 128, 128)
                        sg = bfr.next()
                        act(S, sg[:], ps[:], AF.Sigmoid, [ps], [sg])
                        dma(S, dst[c * 128:(c + 1) * 128, t0:t0 + 512], sg[:], [sg], (), sg)

        for d in (range(2) if upto >= 2 else ()):
            with S.phase():
                msk = S.sb([64, 64], F32, "msk")
                dma(S, msk[:], masks_d[d], (), [msk], msk)
                qTr = S.ring(2, [128, 4, 512], BF16, "qT")
                kTr = S.ring(2, [128, 4, 512], BF16, "kT")
                kdr = S.ring(2, [64, 8, 512], BF16, "kd")
                vr = S.ring(2, [64, 8, 512], BF16, "v")
                dcr = S.ring(2, [128, 4, 8], F32, "dc")
                Sf = S.sb([128, 4, 128], F32, "Sf")
                Sb = S.sb([128, 4, 128], BF16, "Sb")
                smr = S.ring(3, [64, 4, 64], BF16, "sm")
                otr = S.ring(2, [64, 8, 512], F32, "ot")
                ps_s = S.psring(2, "ps_s")
                ps_o = S.psring(2, "ps_o")
                ps_u = S.psring(2, "ps_u")
                if d == 1:
                    ofr = S.ring(2, [64, 8, 512], F32, "of")
                    ghr = S.ring(2, [64, 8, 512], BF16, "gh")
                    sqt = S.sb([64, 8, 512], F32, "sqt")
                    msr = S.ring(2, [64, 32], F32, "ms")
                    onb = S.ring(2, [64, 8, 512], BF16, "onb")
                    otT = S.ring(2, [128, 4, 512], BF16, "otT")
                    ghg = S.sb([64, 512], F32, "ghg")
                    dma(S, ghg[:], g_hg.partition_broadcast(64), (), [ghg], ghg)
                    psT2 = S.psring(2, "psT2")
                seq0 = 0
                for L in seq_lens:
                    ntile = L // 512
                    memset(S, "pool", Sf[:], 0.0, [Sf])
                    memset(S, "pool", Sb[:], 0.0, [Sb])
                    order = range(ntile) if d == 0 else range(ntile - 1, -1, -1)
                    for ti in order:
                        t0 = seq0 + ti * 512
                        n = t0 // 512
                        qT = qTr.next()
                        kT = kTr.next()
                        kd = kdr.next()
                        v = vr.next()
                        dc = dcr.next()
                        dma(S, qT[:], QTs[d][:, t0:t0 + 512].rearrange("(h k) t -> k h t", k=128), (), [qT], qT)
                        dma(S, kT[:], KTs[d][:, t0:t0 + 512].rearrange("(h k) t -> k h t", k=128), (), [kT], kT)
                        dma(S, kd[:], KDs[d][t0:t0 + 512, :].rearrange("(c j) f -> j c f", j=64), (), [kd], kd)
                        dma(S, v[:], VH[t0:t0 + 512, :].rearrange("(c j) f -> j c f", j=64), (), [v], v)
                        dma(S, dc[:], DECs[d][:, n * 8:(n + 1) * 8].rearrange("(h k) c -> k h c", k=128), (), [dc], dc)
                        ot = otr.next()
                        corder = list(range(8)) if d == 0 else list(range(7, -1, -1))

                        def scores(c, kT=kT, qT=qT):
                            cs = slice(c * 64, (c + 1) * 64)
                            pss = ps_s.next()
                            for h in range(4):
                                mm(S, pss[0:64, h * 64:(h + 1) * 64], kT[:, h, cs], qT[:, h, cs], True, True, [kT, qT], [pss])
                            sm = smr.next()
                            tt(S, "dve", sm[:], pss[0:64, 0:256].rearrange("p (h t) -> p h t", t=64),
                               msk[:].unsqueeze(1).to_broadcast([64, 4, 64]), ALU.mult, [pss, msk], [sm])
                            return sm

                        sm_next = scores(corder[0])
                        for ci, c in enumerate(corder):
                            cs = slice(c * 64, (c + 1) * 64)
                            sm = sm_next
                            if ci + 1 < 8:
                                sm_next = scores(corder[ci + 1])
                            psu = ps_u.next()
                            for h in range(4):
                                hs = slice(h * 128, (h + 1) * 128)
                                mm(S, psu[:, hs], kd[:, c, hs], v[:, c, hs], True, True, [kd, v], [psu])
                            pso = ps_o.next()
                            for h in range(4):
                                hs = slice(h * 128, (h + 1) * 128)
                                mm(S, pso[0:64, hs], sm[:, h, :], v[:, c, hs], True, False, [sm, v], [pso])
                                mm(S, pso[0:64, hs], qT[:, h, cs], Sb[:, h, :], False, True, [qT, Sb], [pso])
                            cp(S, "act", ot[:, c, :], pso[0:64, :], [pso], [ot])
                            tt(S, "dve", Sf[:], Sf[:], dc[:, :, c].unsqueeze(2).to_broadcast([128, 4, 128]), ALU.mult, [Sf, dc], [Sf])
                            tt(S, "dve", Sf[:], Sf[:], psu[:].rearrange("p (h e) -> p h e", e=128), ALU.add, [Sf, psu], [Sf])
                            cp(S, "act", Sb[:], Sf[:], [Sf], [Sb])
                        if d == 0:
                            dma(S, OF[t0:t0 + 512, :].rearrange("(c j) f -> j c f", j=64), ot[:], [ot], (), ot)
                        else:
                            of = ofr.next()
                            gh = ghr.next()
                            dma(S, of[:], OF[t0:t0 + 512, :].rearrange("(c j) f -> j c f", j=64), (), [of], of)
                            dma(S, gh[:], GH[t0:t0 + 512, :].rearrange("(c j) f -> j c f", j=64), (), [gh], gh)
                            tt(S, "pool", ot[:], ot[:], of[:], ALU.add, [ot, of], [ot])
                            tt(S, "dve", sqt[:], ot[:], ot[:], ALU.mult, [ot], [sqt])
                            ms = msr.next()
                            S.op("dve", lambda e, ms=ms: e.tensor_reduce(out=ms[:], in_=sqt[:].rearrange("p c (h e) -> p (c h) e", e=128),
                                                                       axis=AX.X, op=ALU.add), [sqt], [ms])
                            act(S, ms[:], ms[:], AF.Ln, [ms], [ms], scale=1.0 / 128, bias=EPS)
                            act(S, ms[:], ms[:], AF.Exp, [ms], [ms], scale=-0.5)
                            ot4 = ot[:].rearrange("p c (h e) -> p (c h) e", e=128)
                            tt(S, "dve", ot4, ot4, ms[:].unsqueeze(2).to_broadcast([64, 32, 128]), ALU.mult, [ot, ms], [ot])
                            tt(S, "pool", ot[:], ot[:], ghg[:].unsqueeze(1).to_broadcast([64, 8, 512]), ALU.mult, [ot, ghg], [ot])
                            ob = onb.next()
                            tt(S, "dve", ob[:], ot[:], gh[:], ALU.mult, [ot, gh], [ob])
                            oT = otT.next()
                            for hc in range(4):
                                pt = psT2.next()
                                ptb = pt[:].bitcast(BF16)
                                for c in range(8):
                                    tr(S, ptb[:, c * 64:(c + 1) * 64], ob[:, c, hc * 128:(hc + 1) * 128], identb[0:64, 0:64], [ob, identb], [pt])
                                cp(S, "act", oT[:, hc, :], ptb[:, 0:512], [pt], [oT])
                            dma(S, OT[:, t0:t0 + 512].rearrange("(c p) t -> p c t", p=128), oT[:], [oT], (), oT)
                    seq0 += L

        for _ in ((1,) if upto >= 3 else ()):
          with S.phase():
            Lmax = max(seq_lens)
            KTr = S.ring(2, [96, Lmax], BF16, "KT")
            Vall = S.sb([128, Lmax // 128, 520], BF16, "Vall")
            QTr = S.ring(3, [96, 512], BF16, "QTt")
            pTr = S.ring(4, [128, 512], BF16, "pT")
            ps_s = S.psring(4, "a_s")
            ps_o = S.psring(2, "a_o")
            ps_b = S.psring(2, "a_b")
            osr = S.ring(2, [65, 512], F32, "osb")
            atr = S.ring(2, [64, 512], BF16, "at")
            scale = float(96 ** -0.5)
            lur = S.ring(2, [128, 1024], F32, "lu")
            lvr = S.ring(2, [128, 1024], F32, "lv")
            uvo = S.ring(2, [128, 2048], BF16, "uvo")
            conv_left = list(range(n_exp // 128)) if upto >= 5 else []

            def conv_some(k):
                for _ in range(k):
                    if not conv_left:
                        return
                    r = conv_left.pop(0)
                    lu = lur.next(); lv = lvr.next(); o = uvo.next()
                    dma(S, lu[:], pu[r * 128:(r + 1) * 128, :], (), [lu], lu)
                    dma(S, lv[:], pv[r * 128:(r + 1) * 128, :], (), [lv], lv)
                    cp(S, "dve", o[:, 0:1024], lu[:], [lu], [o])
                    cp(S, "pool", o[:, 1024:2048], lv[:], [lv], [o])
                    dma(S, UV[r * 128:(r + 1) * 128, :], o[:], [o], (), o)
            items = []
            seq0 = 0
            for si, L in enumerate(seq_lens):
                for h in range(8):
                    for qi in range(L // 512):
                        items.append((si, seq0, L, h, qi))
                seq0 += L
            loaded = {}

            def prefetch(i):
                si, seq0, L, h, qi = items[i]
                if (si, h) not in loaded:
                    KT = KTr.next()
                    dma(S, KT[0:64, 0:L], AKN[h * 64:(h + 1) * 64, seq0:seq0 + L], (), [KT], KT)
                    dma(S, KT[64:96, 0:L], AKP[:, seq0:seq0 + L], (), [KT], KT)
                    loaded[(si, h)] = KT
                QTt = QTr.next()
                q0 = seq0 + qi * 512
                dma(S, QTt[:], AQ[h * 96:(h + 1) * 96, q0:q0 + 512], (), [QTt], QTt)
                loaded[i] = QTt

            prefetch(0)
            cur_seq = -1
            for i in range(len(items)):
                si, seq0, L, h, qi = items[i]
                nk = L // 128
                if si != cur_seq:
                    dma(S, Vall[:, 0:nk, :], AV[seq0:seq0 + L, :].rearrange("(n j) f -> j n f", j=128), (), [Vall], Vall)
                    cur_seq = si
                if i + 1 < len(items):
                    prefetch(i + 1)
                conv_some(-(-(n_exp // 128) // len(items)))
                KT = loaded[(si, h)]
                QTt = loaded.pop(i)
                q0 = seq0 + qi * 512
                pso = ps_o.next()
                pend = []
                LOOK = 2

                def pvmm(j, pT, pso=pso, h=h, nk=nk):
                    mm(S, pso[0:65, :], Vall[:, j, h * 65:(h + 1) * 65], pT[:], j == 0, j == nk - 1, [Vall, pT], [pso])

                for j in range(nk):
                    pss = ps_s.next()
                    mm(S, pss[:, :], KT[0:96, j * 128:(j + 1) * 128], QTt[:], True, True, [KT, QTt], [pss])
                    pT = pTr.next()
                    act(S, pT[:], pss[:], AF.Exp, [pss], [pT], scale=scale)
                    pend.append((j, pT))
                    if len(pend) > LOOK:
                        pvmm(*pend.pop(0))
                while pend:
                    pvmm(*pend.pop(0))
                osb = osr.next()
                cp(S, "dve", osb[:], pso[0:65, :], [pso], [osb])
                S.op("dve", lambda e, osb=osb: e.reciprocal(out=osb[64:65, :], in_=osb[64:65, :]), [osb], [osb])
                psb = ps_b.next()
                mm(S, psb[0:64, :], onesf[64:65, 0:64], osb[64:65, :], True, True, [onesf, osb], [psb])
                at = atr.next()
                tt(S, "dve", at[:], osb[0:64, :], psb[0:64, :], ALU.mult, [osb, psb], [at])
                dma(S, AT[h * 64:(h + 1) * 64, q0:q0 + 512], at[:], [at], (), at)
            conv_some(len(conv_left))

        for _ in ((1,) if upto >= 4 else ()):
          with S.phase():
            stg_ring = S.ring(3, [128, 1024], F32, "stg")
            Wbh = S.sb([128, 4, 1024], BF16, "Wbh")
            Wba = S.sb([64, 8, 1024], BF16, "Wba")
            Wo = S.sb([128, 8, 1024], BF16, "Wo")
            k = 0
            for c in range(4):
                load_w_bf16(Wbh, Wbh[:, c, :], w_brh[c * 128:(c + 1) * 128, :], [128, 1024], stg_ring, k); k += 1
            for h in range(8):
                load_w_bf16(Wba, Wba[:, h, :], w_bra[h * 64:(h + 1) * 64, :], [64, 1024], stg_ring, k); k += 1
            for c in range(8):
                load_w_bf16(Wo, Wo[:, c, :], w_out[c * 128:(c + 1) * 128, :], [128, 1024], stg_ring, k); k += 1
            OTr = S.ring(2, [128, 4, 512], BF16, "OTt")
            ATr = S.ring(2, [64, 8, 512], BF16, "ATt")
            sghr = S.ring(2, [128, 8, 512], BF16, "sgh")
            sgar = S.ring(2, [128, 8, 512], BF16, "sga")
            mgr = S.ring(2, [128, 8, 512], BF16, "mg")
            t1r = S.ring(3, [128, 512], F32, "t1")
            t2r = S.ring(3, [128, 512], F32, "t2")
            xr = S.ring(3, [128, 1024], F32, "x4")
            x1r = S.ring(3, [128, 1024], F32, "x1")
            psh = S.psring(2, "psh")
            psa = S.psring(2, "psa")
            psd = S.psring(4, "psd")
            for n in range(NT):
                t0 = n * 512
                OTt = OTr.next(); ATt = ATr.next(); sgh = sghr.next(); sga = sgar.next()
                dma(S, OTt[:], OT[:, t0:t0 + 512].rearrange("(c p) t -> p c t", p=128), (), [OTt], OTt)
                dma(S, ATt[:], AT[:, t0:t0 + 512].rearrange("(h e) t -> e h t", e=64), (), [ATt], ATt)
                dma(S, sgh[:], SGH[:, t0:t0 + 512].rearrange("(c p) t -> p c t", p=128), (), [sgh], sgh)
                dma(S, sga[:], SGA[:, t0:t0 + 512].rearrange("(c p) t -> p c t", p=128), (), [sga], sga)
                mg = mgr.next()
                for m in range(8):
                    ph = psh.next()
                    for c in range(4):
                        mm(S, ph[:, :], Wbh[:, c, m * 128:(m + 1) * 128], OTt[:, c, :], c == 0, c == 3, [Wbh, OTt], [ph])
                    pa = psa.next()
                    for h in range(8):
                        mm(S, pa[:, :], Wba[:, h, m * 128:(m + 1) * 128], ATt[:, h, :], h == 0, h == 7, [Wba, ATt], [pa])
                    t1 = t1r.next(); t2 = t2r.next()
                    tt(S, "dve", t1[:], ph[:], sgh[:, m, :], ALU.mult, [ph, sgh], [t1])
                    tt(S, "dve", t2[:], pa[:], sga[:, m, :], ALU.mult, [pa, sga], [t2])
                    tt(S, "pool", mg[:, m, :], t1[:], t2[:], ALU.add, [t1, t2], [mg])
                for s in range(4):
                    xt = xr.next()
                    dma(S, xt[:], x[t0 + s * 128:t0 + (s + 1) * 128, :], (), [xt], xt)
                    x1 = x1r.next()
                    for hf in range(2):
                        pd = psd.next()
                        for c in range(8):
                            mm(S, pd[:, :], mg[:, c, s * 128:(s + 1) * 128], Wo[:, c, hf * 512:(hf + 1) * 512], c == 0, c == 7, [mg, Wo], [pd])
                        tt(S, "dve", x1[:, hf * 512:(hf + 1) * 512], xt[:, hf * 512:(hf + 1) * 512], pd[:], ALU.add, [xt, pd], [x1])
                    dma(S, X1[t0 + s * 128:t0 + (s + 1) * 128, :], x1[:], [x1], (), x1)

        for _ in ((1,) if upto >= 5 else ()):
          with S.phase():
            stg_ring = S.ring(3, [128, 1024], F32, "stg")
            Wpq = S.sb([128, 8, 2048], BF16, "Wpq")
            k = 0
            for c in range(8):
                load_w_bf16(Wpq, Wpq[:, c, :], w_pq[c * 128:(c + 1) * 128, :], [128, 2048], stg_ring, k); k += 1
            skT = S.sb([128, 16, 128], BF16, "skT")
            skb = S.sb([128, 128], BF16, "skb")
            psT = S.psring(2, "psT")
            for g in range(16):
                stg = stg_ring.next()
                dma(S, stg[:, 0:128], skeys[g], (), [stg], stg)
                cp(S, "dve", skb[:], stg[:, 0:128], [stg], [skb])
                pt = psT.next()
                ptb = pt[:].bitcast(BF16)
                tr(S, ptb[:, 0:128], skb[:], identb[:], [skb, identb], [pt])
                cp(S, "act", skT[:, g, :], ptb[:, 0:128], [pt], [skT])
            gffn_c = S.sb([128, 8], F32, "gffn_c")
            dma(S, gffn_c[:], g_ffn.rearrange("(c p) -> p c", p=128), (), [gffn_c], gffn_c, slow=True)
            gffn_b = S.sb([128, 1024], F32, "gffn_b")
            dma(S, gffn_b[:], g_ffn.partition_broadcast(128), (), [gffn_b], gffn_b)
            gfin_b = S.sb([128, 1024], F32, "gfin_b")
            dma(S, gfin_b[:], g_fin.partition_broadcast(128), (), [gfin_b], gfin_b)
            iota16 = S.sb([128, 16], F32, "iota16")
            dma(S, iota16[:], iota_d[:, :], (), [iota16], iota16)

            x1r = S.ring(2, [128, 1024], F32, "x1")
            junk = S.sb([128, 1024], BF16, "junk")
            junkb = S.sb([128, 1024], BF16, "junkb")
            ssr = S.ring(2, [128, 1], F32, "ss")
            rsr = S.ring(2, [128, 1], F32, "rs")
            xnbr = S.ring(1, [128, 1024], BF16, "xnb")
            zr = S.ring(2, [128, 1024], BF16, "z")
            zTr = S.ring(1, [128, 8, 128], BF16, "zT")
            qTr = S.ring(1, [128, 16, 128], BF16, "qT")
            psq = S.psring(1, "psq")
            pssc = S.psring(1, "pssc")
            psacc = S.psring(4, "psacc")
            scr_ = S.ring(1, [128, 16, 128], F32, "sc")
            wkr = S.ring(2, [128, 128], F32, "wk")
            V1r = S.ring(2, [128, 16, 16], F32, "V1")
            I1r = S.ring(2, [128, 16, 16], U32, "I1")
            I1fr = S.ring(2, [128, 16, 16], F32, "I1f")
            candr = S.ring(1, [128, 8, 256], F32, "cand")
            wk2r = S.ring(2, [128, 256], F32, "wk2")
            tvr = S.ring(2, [128, 8, 16], F32, "tv")
            posr = S.ring(2, [128, 8, 16], U32, "pos")
            pir = S.ring(2, [128, 8, 16], U32, "pi")
            pjr = S.ring(2, [128, 8, 16], U32, "pj")
            fir = S.ring(2, [128, 8, 16], F32, "fi")
            fjr = S.ring(2, [128, 8, 16], F32, "fj")
            eqr = S.ring(1, [128, 8, 16, 16], BF16, "eq")
            eir = S.ring(2, [128, 8, 16], F32, "ei")
            ejr = S.ring(2, [128, 8, 16], F32, "ej")
            eidr = S.ring(2, [128, 128], I32, "eid")
            gwr = S.ring(2, [128, 8, 16], F32, "gw")
            zsr = S.ring(2, [128, 8], F32, "zs")
            dotr = S.ring(8, [128, 4], F32, "dots")
            actr = S.ring(8, [128, 4], F32, "actv")
            uvr = S.ring(20, [128, 2048], BF16, "uvg")
            dgr = S.ring(6, [128, 128], BF16, "dg")
            accr = S.ring(1, [128, 1024], F32, "acc")

            def prep(n, st):
                t0 = n * 128
                x1 = x1r.next()
                dma(S, x1[:], X1[t0:t0 + 128, :], (), [x1], x1)
                ss = ssr.next(); rs = rsr.next()
                act(S, junk[:], x1[:], AF.Square, [x1], [junk, ss], accum=ss[:, 0:1])
                rstd_from_ss(S, rs, ss, 1024.0, None)
                xnb = xnbr.next()
                act(S, xnb[:], x1[:], AF.Copy, [x1, rs], [xnb], scale=rs[:, 0:1])
                z = zr.next()
                stt(S, z[:], x1[:], rs[:, 0:1], gffn_b[:], ALU.mult, ALU.mult, [x1, rs, gffn_b], [z])
                pt = psT.next()
                ptb = pt[:].bitcast(BF16)
                for c in range(8):
                    tr(S, ptb[:, c * 128:(c + 1) * 128], xnb[:, c * 128:(c + 1) * 128], identb[:], [xnb, identb], [pt])
                zT = zTr.next()
                for c in range(8):
                    act(S, zT[:, c, :], ptb[:, c * 128:(c + 1) * 128], AF.Copy, [pt, gffn_c], [zT], scale=gffn_c[:, c:c + 1])
                yield
                qT = qTr.next()
                for gq in range(4):
                    pq = psq.next()
                    for gg in range(4):
                        g = gq * 4 + gg
                        for c in range(8):
                            mm(S, pq[:, gg * 128:(gg + 1) * 128], Wpq[:, c, g * 128:(g + 1) * 128], zT[:, c, :], c == 0, c == 7, [Wpq, zT], [pq])
                        yield
                    cp(S, "act", qT[:, gq * 4:(gq + 1) * 4, :], pq[:].rearrange("p (g t) -> p g t", t=128), [pq], [qT])
                sc = scr_.next()
                for gq in range(4):
                    pc = pssc.next()
                    for gg in range(4):
                        g = gq * 4 + gg
                        mm(S, pc[:, gg * 128:(gg + 1) * 128], qT[:, g, :], skT[:, g, :], True, True, [qT, skT], [pc])
                    cp(S, "act", sc[:, gq * 4:(gq + 1) * 4, :], pc[:].rearrange("p (g t) -> p g t", t=128), [pc], [sc])
                    yield
                yield
                V1 = V1r.next(); I1 = I1r.next()
                for g in range(16):
                    wk = wkr.next()
                    S.op("dve", lambda e, V1=V1, sc=sc, g=g: e.max(out=V1[:, g, 0:8], in_=sc[:, g, :]), [sc], [V1])
                    S.op("dve", lambda e, V1=V1, I1=I1, sc=sc, g=g: e.max_index(out=I1[:, g, 0:8], in_max=V1[:, g, 0:8], in_values=sc[:, g, :]), [sc, V1], [I1])
                    S.op("dve", lambda e, V1=V1, wk=wk, sc=sc, g=g: e.match_replace(out=wk[:], in_to_replace=V1[:, g, 0:8], in_values=sc[:, g, :], imm_value=-1e30), [sc, V1], [wk])
                    S.op("dve", lambda e, V1=V1, wk=wk, g=g: e.max(out=V1[:, g, 8:16], in_=wk[:]), [wk], [V1])
                    S.op("dve", lambda e, V1=V1, I1=I1, wk=wk, g=g: e.max_index(out=I1[:, g, 8:16], in_max=V1[:, g, 8:16], in_values=wk[:]), [wk, V1], [I1])
                    yield
                I1f = I1fr.next()
                cp(S, "dve", I1f[:], I1[:], [I1], [I1f])
                V1v = V1[:].rearrange("p (h two) k -> p h two k", two=2)
                I1v = I1f[:].rearrange("p (h two) k -> p h two k", two=2)
                cand = candr.next()
                tt(S, "dve", cand[:].rearrange("p h (i j) -> p h i j", j=16),
                   V1v[:, :, 0, :].unsqueeze(3).to_broadcast([128, 8, 16, 16]),
                   V1v[:, :, 1, :].unsqueeze(2).to_broadcast([128, 8, 16, 16]), ALU.add, [V1], [cand])
                yield
                tv = tvr.next(); pos = posr.next()
                for h in range(8):
                    wk2 = wk2r.next()
                    S.op("dve", lambda e, tv=tv, cand=cand, h=h: e.max(out=tv[:, h, 0:8], in_=cand[:, h, :]), [cand], [tv])
                    S.op("dve", lambda e, tv=tv, pos=pos, cand=cand, h=h: e.max_index(out=pos[:, h, 0:8], in_max=tv[:, h, 0:8], in_values=cand[:, h, :]), [cand, tv], [pos])
                    S.op("dve", lambda e, tv=tv, wk2=wk2, cand=cand, h=h: e.match_replace(out=wk2[:], in_to_replace=tv[:, h, 0:8], in_values=cand[:, h, :], imm_value=-1e30), [cand, tv], [wk2])
                    S.op("dve", lambda e, tv=tv, wk2=wk2, h=h: e.max(out=tv[:, h, 8:16], in_=wk2[:]), [wk2], [tv])
                    S.op("dve", lambda e, tv=tv, pos=pos, wk2=wk2, h=h: e.max_index(out=pos[:, h, 8:16], in_max=tv[:, h, 8:16], in_values=wk2[:]), [wk2, tv], [pos])
                    yield
                pi = pir.next(); pj = pjr.next(); fi = fir.next(); fj = fjr.next()
                S.op("dve", lambda e, pi=pi, pos=pos: e.tensor_single_scalar(out=pi[:], in_=pos[:], scalar=4, op=ALU.logical_shift_right), [pos], [pi])
                S.op("dve", lambda e, pj=pj, pos=pos: e.tensor_single_scalar(out=pj[:], in_=pos[:], scalar=15, op=ALU.bitwise_and), [pos], [pj])
                cp(S, "act", fi[:], pi[:], [pi], [fi])
                cp(S, "act", fj[:], pj[:], [pj], [fj])
                yield
                iob = iota16[:].unsqueeze(1).unsqueeze(1).to_broadcast([128, 8, 16, 16])
                ei = eir.next(); ej = ejr.next()
                for (ff, which, eo) in ((fi, 0, ei), (fj, 1, ej)):
                    eq = eqr.next()
                    tt(S, "dve", eq[:], ff[:].unsqueeze(3).to_broadcast([128, 8, 16, 16]), iob, ALU.is_equal, [ff, iota16], [eq])
                    yield
                    tt(S, "dve", eq[:], eq[:], I1v[:, :, which, :].unsqueeze(2).to_broadcast([128, 8, 16, 16]), ALU.mult, [eq, I1f], [eq])
                    yield
                    S.op("dve", lambda e, eo=eo, eq=eq: e.tensor_reduce(out=eo[:], in_=eq[:], axis=AX.X, op=ALU.add), [eq], [eo])
                    yield
                eid = eidr.next()
                stt(S, ei[:], ei[:], 128.0, ej[:], ALU.mult, ALU.add, [ei, ej], [ei])
                cp(S, "dve", eid[:], ei[:].rearrange("p h k -> p (h k)"), [ei], [eid])
                yield
                gw = gwr.next(); zs = zsr.next()
                tt(S, "dve", gw[:], tv[:], tv[:, :, 0:1].to_broadcast([128, 8, 16]), ALU.subtract, [tv], [gw])
                act(S, gw[:], gw[:], AF.Exp, [gw], [gw])
                S.op("dve", lambda e, zs=zs, gw=gw: e.tensor_reduce(out=zs[:], in_=gw[:], axis=AX.X, op=ALU.add), [gw], [zs])
                S.op("dve", lambda e, zs=zs: e.reciprocal(out=zs[:], in_=zs[:]), [zs], [zs])
                tt(S, "dve", gw[:], gw[:], zs[:].unsqueeze(2).to_broadcast([128, 8, 16]), ALU.mult, [gw, zs], [gw])
                st.update(x1=x1, z=z, eid=eid, gw=gw, t0=t0)
                yield

            def gather(st, nxt):
                x1 = st['x1']; z = st['z']; eid = st['eid']; gw = st['gw']; t0 = st['t0']
                gwf = gw[:].rearrange("p h k -> p (h k)")
                accA = psacc.next()
                accB = psacc.next()
                GB = 4
                for s0 in range(0, 128, GB):
                    rows = []
                    dots = dotr.next()
                    av = actr.next()
                    for sl in range(s0, s0 + GB):
                        uvg = uvr.next()
                        S.op("pool", lambda e, uvg=uvg, eid=eid, sl=sl: e.indirect_dma_start(
                            out=uvg[:], out_offset=None, in_=UV[:, :], in_offset=bass.IndirectOffsetOnAxis(ap=eid[:, sl:sl + 1], axis=0)),
                            [eid], [uvg], dma=uvg)
                        stt(S, junkb[:], uvg[:, 0:1024], 1.0, z[:], ALU.mult, ALU.mult, [z],
                            ([dots] if sl in (s0, s0 + GB - 1) else []), accum=dots[:, sl - s0:sl - s0 + 1], noreg=[uvg])
                        rows.append(uvg)
                    act(S, av[:], dots[:], AF.Gelu, [dots], [av])
                    for k_, sl in enumerate(range(s0, s0 + GB)):
                        uvg = rows[k_]
                        dg = dgr.next()
                        act(S, av[:, k_:k_ + 1], av[:, k_:k_ + 1], AF.Copy, [av, gw], [av], scale=gwf[:, sl:sl + 1])
                        act(S, dg[:], identb[:], AF.Copy, [identb, av], [dg], scale=av[:, k_:k_ + 1])
                        mm(S, accA[:, :], dg[:], uvg[:, 1024:1536], sl == 0, sl == 127, [dg, uvg], [accA])
                        mm(S, accB[:, :], dg[:], uvg[:, 1536:2048], sl == 0, sl == 127, [dg, uvg], [accB])
                    if nxt is not None:
                        next(nxt, None)
                        next(nxt, None)
                acc = accr.next()
                tt(S, "dve", acc[:, 0:512], x1[:, 0:512], accA[:, :], ALU.add, [x1, accA], [acc])
                tt(S, "dve", acc[:, 512:1024], x1[:, 512:1024], accB[:, :], ALU.add, [x1, accB], [acc])
                ss2 = ssr.next(); rs2 = rsr.next()
                act(S, junk[:], acc[:], AF.Square, [acc], [junk, ss2], accum=ss2[:, 0:1])
                rstd_from_ss(S, rs2, ss2, 1024.0, None)
                stt(S, acc[:], acc[:], rs2[:, 0:1], gfin_b[:], ALU.mult, ALU.mult, [acc, rs2, gfin_b], [acc])
                dma(S, y[t0:t0 + 128, :], acc[:], [acc], (), acc)


            ntile4 = TT // 128
            sts = [dict() for _ in range(ntile4)]
            for _ in prep(0, sts[0]):
                pass
            for n in range(ntile4):
                nxt = prep(n + 1, sts[n + 1]) if n + 1 < ntile4 else None
                gather(sts[n], nxt)
                if nxt is not None:
                    for _ in nxt:
                        pass

        S.barrier()
        S.emit()
    return nc


def _consts():
    half = 16
    inv = (1.0 / (10000.0 ** (np.arange(half, dtype=np.float32) / half))).astype(np.float32)
    ang = np.arange(8192, dtype=np.float32)[:, None] * inv[None, :]
    cos, sin = np.cos(ang).astype(np.float32).T, np.sin(ang).astype(np.float32).T
    cos32 = np.concatenate([cos, cos], 0)
    sin32 = np.concatenate([-sin, sin], 0)
    cos4 = np.ascontiguousarray(np.tile(cos32, (4, 1)))
    sin4 = np.ascontiguousarray(np.tile(sin32, (4, 1)))
    ident = np.eye(128, dtype=np.float32)
    s = np.arange(64)
    mU = (s[:, None] <= s[None, :]).astype(np.float32)
    mL = (s[:, None] >= s[None, :]).astype(np.float32)
    masks = np.stack([mU, mL], 0)
    iota16 = np.tile(np.arange(16, dtype=np.float32)[None, :], (128, 1))
    return dict(cos4=cos4, sin4=sin4, ident=ident, masks=masks, iota16=np.ascontiguousarray(iota16))


def _prep_weights(lb_logits, norm_mix, w_in, hg_out_norm, w_br_h, q_a_norm, w_uq, kv_a_norm, w_ukv,
                  w_br_a, w_out, norm_ffn, peer_wq, peer_sub_keys, peer_u, peer_v, norm_final):
    f = lambda a: np.ascontiguousarray(np.asarray(a, dtype=np.float32))
    w_in0 = f(w_in[0])
    kpe = w_in0[:, 3200:3232]
    kpe_sw = np.concatenate([kpe[:, 16:32], kpe[:, 0:16]], 1)
    w_in_ext = np.concatenate([w_in0, kpe_sw], 1)
    wuq = f(w_uq[0]).reshape(384, 8, 96)
    wq_n = wuq[:, :, 0:64].reshape(384, 512)
    wq_p = wuq[:, :, 64:96].reshape(384, 256)
    wq_s = np.concatenate([wuq[:, :, 80:96], wuq[:, :, 64:80]], 2).reshape(384, 256)
    wukv = f(w_ukv[0]).reshape(256, 8, 128)
    wk_n = wukv[:, :, 0:64].reshape(256, 512)
    wv = wukv[:, :, 64:128].reshape(256, 512)
    sk = f(peer_sub_keys[0])
    skeys = sk.transpose(1, 0, 2, 3).reshape(16, 128, 128)
    d = dict(w_in=w_in_ext, lbl=f(lb_logits), g_mix=f(norm_mix[0]), g_hg=f(hg_out_norm[0]), w_brh=f(w_br_h[0]),
             g_qa=f(q_a_norm[0]), wq_n=wq_n, wq_p=wq_p, wq_s=wq_s, g_kva=f(kv_a_norm[0]), wk_n=wk_n, wv=wv,
             w_bra=f(w_br_a[0]), w_out=f(w_out[0]), g_ffn=f(norm_ffn[0]), w_pq=f(peer_wq[0]), skeys=skeys,
             pu=f(peer_u[0]), pv=f(peer_v[0]), g_fin=f(norm_final))
    d = {k: np.ascontiguousarray(v) for k, v in d.items()}
    d.update(_consts())
    return d


def kernel(x_prompt, x_sample, lb_logits, norm_mix, w_in, hg_out_norm, w_br_h, q_a_norm, w_uq,
           kv_a_norm, w_ukv, w_br_a, w_out, norm_ffn, peer_wq, peer_sub_keys, peer_u, peer_v, norm_final):
    xp = np.asarray(x_prompt, dtype=np.float32)
    xs = np.asarray(x_sample, dtype=np.float32)
    B, SP, D = xp.shape
    BS, SS, _ = xs.shape
    n = 8
    ppc = B // n
    spc = BS // n
    seq_lens = [SP] * ppc + [SS] * spc
    wd = _prep_weights(lb_logits, norm_mix, w_in, hg_out_norm, w_br_h, q_a_norm, w_uq, kv_a_norm, w_ukv,
                       w_br_a, w_out, norm_ffn, peer_wq, peer_sub_keys, peer_u, peer_v, norm_final)
    nc = build(seq_lens)
    in_maps = []
    for c in range(n):
        parts = [xp[c * ppc + i] for i in range(ppc)] + [xs[c * spc + i] for i in range(spc)]
        m = dict(wd)
        m["x"] = np.ascontiguousarray(np.concatenate(parts, 0))
        in_maps.append(m)
    res = run_bass_kernel_spmd(nc, in_maps, core_ids=list(range(n)))
    yp = np.empty_like(xp)
    ys = np.empty_like(xs)
    for c in range(n):
        yc = res.results[c]["y"]
        o = 0
        for i in range(ppc):
            yp[c * ppc + i] = yc[o:o + SP]; o += SP
        for i in range(spc):
            ys[c * spc + i] = yc[o:o + SS]; o += SS
    return (yp, ys)
```

```python
import contextlib
import numpy as np
import concourse.bass as bass
import concourse.mybir as mybir
from concourse.bass_utils import run_bass_kernel_spmd

F32 = mybir.dt.float32
BF16 = mybir.dt.bfloat16
I32 = mybir.dt.int32
U32 = mybir.dt.uint32
ALU = mybir.AluOpType
AF = mybir.ActivationFunctionType
AX = mybir.AxisListType

EPS = 1e-6
EPOCH = 30000
DLIMIT = 30000
ENGS = ("pe", "act", "dve", "pool", "sp")


class Buf:
    _n = 0

    def __init__(self, name=""):
        Buf._n += 1
        self.id = Buf._n
        self.name = name
        self.w = None
        self.r = []
        self.psum = False


class T:
    def __init__(self, t, buf):
        self.t = t
        self.b = buf

    def __getitem__(self, k):
        return self.t[k]


class Ring:
    def __init__(self, tiles):
        self.tiles = tiles
        self.i = 0

    def next(self):
        t = self.tiles[self.i % len(self.tiles)]
        self.i += 1
        return t


class Sched:
    def __init__(self, nc, gstack):
        self.nc = nc
        self.gstack = gstack
        self.stack = gstack
        self.q = {e: [] for e in ENGS}
        self.count = {e: 0 for e in ENGS}
        self.seen = {e: {} for e in ENGS}
        self.sems = {}
        self.all_tokens = {}
        self.n_sb = 0
        self.dmasem = {}
        self.free_dsems = []
        self.n_dsem = 0

    def sb(self, shape, dtype, name="t"):
        self.n_sb += 1
        t = self.stack.enter_context(self.nc.sbuf_tensor(f"{name}_{self.n_sb}", list(shape), dtype))
        return T(t, Buf(name))

    def ring(self, n, shape, dtype, name="r"):
        return Ring([self.sb(shape, dtype, name) for _ in range(n)])

    def ps(self, shape, dtype, name="ps"):
        self.n_sb += 1
        t = self.stack.enter_context(self.nc.psum_tensor(f"{name}_{self.n_sb}", list(shape), dtype))
        b = Buf(name)
        b.psum = True
        return T(t, b)

    def psring(self, n, name="ps"):
        return Ring([self.ps([128, 512], F32, name) for _ in range(n)])

    def _dma_tok(self, buf):
        ent = self.dmasem.get(buf.id)
        if ent is None or ent[1] + 16 > DLIMIT:
            if self.free_dsems:
                j, base = self.free_dsems.pop()
            else:
                j, base = self.n_dsem, 0
                self.n_dsem += 1
            ent = [j, base]
            self.dmasem[buf.id] = ent
        ent[1] += 16
        return (("dsem", ent[0]), ent[1])

    def op(self, eng, fn, reads=(), writes=(), dma=None, noreg=()):
        reads = [x.b if isinstance(x, T) else x for x in reads]
        writes = [x.b if isinstance(x, T) else x for x in writes]
        noreg = [x.b if isinstance(x, T) else x for x in noreg]
        writes = writes + [r for r in reads if r.psum and r not in writes]
        reads = [r for r in reads if not r.psum]
        need = {}

        def add(tok):
            if tok is None:
                return
            k, v = tok
            if need.get(k, 0) < v:
                need[k] = v

        for r in reads:
            add(r.w)
        for r in noreg:
            add(r.w)
        for w in writes:
            add(w.w)
            for t in w.r:
                add(t)
        waits = []
        seen = self.seen[eng]
        for k, v in need.items():
            if eng == "pe" and k[0] == "eng" and k[1] == "pe":
                continue
            if seen.get(k, 0) >= v:
                continue
            seen[k] = v
            waits.append((k, v))
        if dma is not None:
            dma = dma.b if isinstance(dma, T) else dma
            tok = self._dma_tok(dma)
            inc = (tok[0], 16)
        else:
            idx = self.count[eng]
            self.count[eng] += 1
            key = ("eng", eng, idx // EPOCH)
            tok = (key, idx % EPOCH + 1)
            inc = (key, 1)
        key = tok[0]
        self.all_tokens[key] = max(self.all_tokens.get(key, 0), tok[1])
        self.q[eng].append((fn, waits, inc))
        for r in reads:
            r.r.append(tok)
        for w in writes:
            w.w = tok
            w.r = []
        return tok

    def barrier(self):
        for e in ENGS:
            waits = []
            seen = self.seen[e]
            for k, v in self.all_tokens.items():
                if seen.get(k, 0) >= v:
                    continue
                seen[k] = v
                waits.append((k, v))
            if waits:
                self.q[e].append((None, waits, None))
        for bid, (j, cnt) in self.dmasem.items():
            if cnt < DLIMIT - 4000:
                self.free_dsems.append((j, cnt))
        self.dmasem = {}

    @contextlib.contextmanager
    def phase(self):
        with contextlib.ExitStack() as st:
            self.stack = st
            yield
            self.barrier()
        self.stack = self.gstack

    def emit(self):
        nc = self.nc
        sems = {}
        for k in self.all_tokens.keys():
            nm = "s_" + "_".join(str(x) for x in k)
            sems[k] = self.gstack.enter_context(nc.semaphore(nm))
        q = self.q

        def run(engobj, lst):
            for fn, waits, inc in lst:
                for k, v in waits:
                    engobj.wait_ge(sems[k], v)
                if fn is not None:
                    ins = fn(engobj)
                    ins.then_inc(sems[inc[0]], inc[1])

        with nc.Block() as block:
            @block.tensor
            def _(e):
                run(e, q["pe"])

            @block.scalar
            def _(e):
                run(e, q["act"])

            @block.vector
            def _(e):
                run(e, q["dve"])

            @block.gpsimd
            def _(e):
                run(e, q["pool"])

            @block.sync
            def _(e):
                run(e, q["sp"])


def dma(S, out, in_, reads=(), writes=(), owner=None, eng="sp", slow=False):
    if slow:
        S.op(eng, lambda e: e.dma_start(out=out, in_=in_, allow_slow_non_contiguous=True), reads, writes, dma=owner)
    else:
        S.op(eng, lambda e: e.dma_start(out=out, in_=in_), reads, writes, dma=owner)


def mm(S, out, lhsT, rhs, start, stop, reads, writes):
    S.op("pe", lambda e: e.matmul(out, lhsT=lhsT, rhs=rhs, start=start, stop=stop), reads, writes)


def tr(S, out, in_, ident, reads, writes):
    S.op("pe", lambda e: e.transpose(out=out, in_=in_, identity=ident), reads, writes)


def act(S, out, in_, func, reads, writes, scale=None, bias=None, accum=None):
    kw = {}
    if scale is not None:
        kw["scale"] = scale
    if bias is not None:
        kw["bias"] = bias
    if accum is not None:
        kw["accum_out"] = accum
    S.op("act", lambda e: e.activation(out=out, in_=in_, func=func, **kw), reads, writes)


def tt(S, eng, out, in0, in1, op, reads, writes):
    S.op(eng, lambda e: e.tensor_tensor(out=out, in0=in0, in1=in1, op=op), reads, writes)


def ts(S, eng, out, in0, s1, s2, op0, op1, reads, writes):
    if op1 is None:
        S.op(eng, lambda e: e.tensor_scalar(out=out, in0=in0, scalar1=s1, scalar2=None, op0=op0), reads, writes)
    else:
        S.op(eng, lambda e: e.tensor_scalar(out=out, in0=in0, scalar1=s1, scalar2=s2, op0=op0, op1=op1), reads, writes)


def stt(S, out, in0, scalar, in1, op0, op1, reads, writes, accum=None, noreg=()):
    if accum is None:
        S.op("dve", lambda e: e.scalar_tensor_tensor(out=out, in0=in0, scalar=scalar, in1=in1, op0=op0, op1=op1), reads, writes, noreg=noreg)
    else:
        S.op("dve", lambda e: e.scalar_tensor_tensor(out=out, in0=in0, scalar=scalar, in1=in1, op0=op0, op1=op1, accum_out=accum), reads, writes, noreg=noreg)


def cp(S, eng, out, in_, reads, writes):
    if eng == "act":
        S.op("act", lambda e: e.activation(out=out, in_=in_, func=AF.Copy), reads, writes)
    else:
        S.op(eng, lambda e: e.tensor_copy(out=out, in_=in_), reads, writes)


def memset(S, eng, ap, val, writes):
    S.op(eng, lambda e: e.memset(ap, val), (), writes)


def rstd_from_ss(S, rstd, ss, n, reads_t):
    act(S, rstd[:], ss[:], AF.Sqrt, [ss], [rstd], scale=1.0 / n, bias=EPS)
    S.op("dve", lambda e: e.reciprocal(out=rstd[:], in_=rstd[:]), [rstd], [rstd])


def build(seq_lens, n_exp=16384, upto=9, sec=99):
    TT = sum(seq_lens)
    assert all(L % 512 == 0 for L in seq_lens)
    NT = TT // 512
    NCH = TT // 64
    nc = bass.Bass("TRN2", target_bir_lowering=False)

    def din(name, shape, dt=F32):
        return nc.dram_tensor(name, list(shape), dt, kind="ExternalInput").ap()

    def dscr(name, shape, dt):
        return nc.dram_tensor(name, list(shape), dt, kind="Internal").ap()

    x = din("x", [TT, 1024])
    w_in = din("w_in", [1024, 5312])
    lbl = din("lbl", [2, 2, 512])
    g_mix = din("g_mix", [1024])
    g_hg = din("g_hg", [512])
    w_brh = din("w_brh", [512, 1024])
    g_qa = din("g_qa", [384])
    wq_n = din("wq_n", [384, 512])
    wq_p = din("wq_p", [384, 256])
    wq_s = din("wq_s", [384, 256])
    g_kva = din("g_kva", [256])
    wk_n = din("wk_n", [256, 512])
    wv = din("wv", [256, 512])
    w_bra = din("w_bra", [512, 1024])
    w_out = din("w_out", [1024, 1024])
    g_ffn = din("g_ffn", [1024])
    w_pq = din("w_pq", [1024, 2048])
    skeys = din("skeys", [16, 128, 128])
    pu = din("pu", [n_exp, 1024])
    pv = din("pv", [n_exp, 1024])
    g_fin = din("g_fin", [1024])
    cos4 = din("cos4", [128, 8192])
    sin4 = din("sin4", [128, 8192])
    ident_d = din("ident", [128, 128])
    masks_d = din("masks", [2, 64, 64])
    iota_d = din("iota16", [128, 16])
    y = nc.dram_tensor("y", [TT, 1024], F32, kind="ExternalOutput").ap()

    QTs = [dscr(f"QT{d}", [512, TT], BF16) for d in range(2)]
    KTs = [dscr(f"KT{d}", [512, TT], BF16) for d in range(2)]
    KDs = [dscr(f"KD{d}", [TT, 512], BF16) for d in range(2)]
    DECs = [dscr(f"DEC{d}", [512, NCH], F32) for d in range(2)]
    VH = dscr("VH", [TT, 512], BF16)
    GH = dscr("GH", [TT, 512], BF16)
    AQ = dscr("AQ", [768, TT], BF16)
    AKN = dscr("AKN", [512, TT], BF16)
    AKP = dscr("AKP", [32, TT], BF16)
    AV = dscr("AV", [TT, 520], BF16)
    SGH = dscr("SGH", [1024, TT], BF16)
    SGA = dscr("SGA", [1024, TT], BF16)
    OF = dscr("OF", [TT, 512], F32)
    OT = dscr("OT", [512, TT], BF16)
    AT = dscr("AT", [512, TT], BF16)
    X1 = dscr("X1", [TT, 1024], F32)
    UV = dscr("UV", [n_exp, 2048], BF16)

    with contextlib.ExitStack() as gstack:
        S = Sched(nc, gstack)
        identb = S.sb([128, 128], BF16, "identb")
        onesb = S.sb([128, 128], BF16, "onesb")
        onesf = S.sb([128, 128], F32, "onesf")
        stage0 = S.sb([128, 128], F32, "stage0")
        dma(S, stage0[:], ident_d[:, :], (), [stage0], stage0)
        cp(S, "dve", identb[:], stage0[:], [stage0], [identb])
        memset(S, "pool", onesb[:], 1.0, [onesb])
        memset(S, "pool", onesf[:], 1.0, [onesf])

        def load_w_bf16(dst, dst_ap, src_ap, shape, stg_ring, i):
            P, N = shape
            for n0 in range(0, N, 1024):
                n1 = min(N, n0 + 1024)
                stg = stg_ring.next()
                v = stg[0:P, 0:n1 - n0]
                dma(S, v, src_ap[:, n0:n1], (), [stg], stg)
                cp(S, ("dve", "pool", "act")[(i + n0 // 1024) % 3], dst_ap[:, n0:n1], v, [stg], [dst])

        with S.phase():
            W = S.sb([128, 8, 5312], BF16, "W")
            w_in_v = w_in.rearrange("(c p) n -> p c n", p=128)
            Wqn = S.sb([128, 3, 512], BF16, "Wqn")
            Wqp = S.sb([128, 3, 256], BF16, "Wqp")
            Wqs = S.sb([128, 3, 256], BF16, "Wqs")
            Wkn = S.sb([128, 2, 512], BF16, "Wkn")
            Wv = S.sb([128, 2, 512], BF16, "Wv")
            def _load_p1_weights(stg_ring):
                k = 0
                for c in range(8):
                    load_w_bf16(W, W[:, c, :], w_in_v[:, c, :], [128, 5312], stg_ring, k)
                    k += 1
                for (dst, src, nchunk, ncol) in ((Wqn, wq_n, 3, 512), (Wqp, wq_p, 3, 256), (Wqs, wq_s, 3, 256),
                                                 (Wkn, wk_n, 2, 512), (Wv, wv, 2, 512)):
                    for c in range(nchunk):
                        load_w_bf16(dst, dst[:, c, :], src[c * 128:(c + 1) * 128, :], [128, ncol], stg_ring, k)
                        k += 1
            gmix = S.sb([128, 8], F32, "gmix")
            dma(S, gmix[:], g_mix.rearrange("(c p) -> p c", p=128), (), [gmix], gmix, slow=True)
            gqa = S.sb([128, 3], F32, "gqa")
            dma(S, gqa[:], g_qa.rearrange("(c p) -> p c", p=128), (), [gqa], gqa, slow=True)
            gkva = S.sb([128, 2], F32, "gkva")
            dma(S, gkva[:], g_kva.rearrange("(c p) -> p c", p=128), (), [gkva], gkva, slow=True)
            l0 = S.sb([128, 2, 4], F32, "l0")
            l1 = S.sb([128, 2, 4], F32, "l1")
            lb = S.sb([128, 2, 4], F32, "lb")
            oml = S.sb([128, 2, 4], F32, "oml")
            dma(S, l0[:], lbl[0].rearrange("d (h p) -> p d h", p=128), (), [l0], l0, slow=True)
            dma(S, l1[:], lbl[1].rearrange("d (h p) -> p d h", p=128), (), [l1], l1, slow=True)
            tt(S, "dve", l0[:], l0[:], l1[:], ALU.subtract, [l0, l1], [l0])
            act(S, lb[:], l0[:], AF.Sigmoid, [l0], [lb])
            ts(S, "dve", oml[:], lb[:], -1.0, 1.0, ALU.mult, ALU.add, [lb], [oml])

            xring = S.ring(2, [128, 1024], F32, "x")
            ssr = S.ring(2, [128, 1], F32, "ss")
            rsr = S.ring(2, [128, 1], F32, "rstd")
            xnr = S.ring(2, [128, 1024], BF16, "xn")
            hTr = S.ring(1, [128, 8, 512], BF16, "hT")
            psT = S.psring(2, "psT")
            psM = S.psring(4, "psM")
            psX = S.psring(2, "psX")
            f32r = S.ring(6, [128, 512], F32, "f32r")
            sqr = S.ring(2, [128, 512], F32, "sq")
            Ar = S.ring(2, [128, 512], F32, "A")
            logr = S.ring(2, [128, 512], F32, "logf")
            omfr = S.ring(2, [128, 512], F32, "omf")
            aendr = S.ring(2, [128, 9], F32, "aend")
            decr = S.ring(2, [128, 8], F32, "dec")
            bfr = S.ring(6, [128, 512], BF16, "bfr")
            kdtr = S.ring(2, [128, 512], BF16, "kdT")
            kdall = [S.ring(1, [128, 4, 512], BF16, "kdall") for _ in range(2)]
            vtr = S.ring(1, [128, 4, 512], BF16, "vt")
            gtr = S.ring(1, [128, 4, 512], BF16, "gt")
            cqg = S.sb([128, 3, 512], F32, "cqg")
            sqc = S.sb([128, 3, 512], BF16, "sqc")
            cqn = S.sb([128, 3, 512], BF16, "cqn")
            rsb = S.sb([128, 512], F32, "rsb")
            vaugr = S.ring(1, [128, 4, 520], BF16, "vaug")
            for t_ in vaugr.tiles:
                memset(S, "pool", t_[:], 1.0, [t_])
            cosr = S.ring(1, [128, 512], F32, "cos")
            sinr = S.ring(1, [128, 512], F32, "sin")
            ones512 = S.sb([128, 512], F32, "ones512")
            memset(S, "pool", ones512[:], 1.0, [ones512])
            with contextlib.ExitStack() as wst:
                _pst = S.stack
                S.stack = wst
                stg_ring = S.ring(3, [128, 1024], F32, "stg")
                S.stack = _pst
                _load_p1_weights(stg_ring)
            f32r2 = S.ring(6, [128, 512], F32, "f32r2")
            f32x = (f32r, f32r2)

            def fm_group(hT, col0, ncol):
                ps = psM.next()
                for c in range(8):
                    mm(S, ps[0:ncol, :], W[:, c, col0:col0 + ncol], hT[:, c, :], c == 0, c == 7, [W, hT], [ps])
                return ps

            def tm_group(hT, s, col0):
                ps = psM.next()
                for c in range(8):
                    mm(S, ps[:, :], hT[:, c, s * 128:(s + 1) * 128], W[:, c, col0:col0 + 512], c == 0, c == 7, [W, hT], [ps])
                return ps

            tile_pos = []
            for L in seq_lens:
                for p0 in range(0, L, 512):
                    tile_pos.append(p0)

            for n in range(NT if sec > 0 else 0):
                t0 = n * 512
                p0 = tile_pos[n]
                hT = hTr.next()
                for s in range(4):
                    xt = xring.next()
                    dma(S, xt[:], x[t0 + s * 128:t0 + (s + 1) * 128, :], (), [xt], xt)
                    ss = ssr.next()
                    rs = rsr.next()
                    xn = xnr.next()
                    act(S, xn[:], xt[:], AF.Square, [xt], [xn, ss], accum=ss[:, 0:1])
                    rstd_from_ss(S, rs, ss, 1024.0, None)
                    act(S, xn[:], xt[:], AF.Copy, [xt, rs], [xn], scale=rs[:, 0:1])
                    pt = psT.next()
                    ptb = pt[:].bitcast(BF16)
                    for c in range(8):
                        tr(S, ptb[:, c * 128:(c + 1) * 128], xn[:, c * 128:(c + 1) * 128], identb[:], [xn, identb], [pt])
                    tt(S, "dve", hT[:, :, s * 128:(s + 1) * 128], ptb.rearrange("p (c t) -> p c t", t=128),
                       gmix[:].unsqueeze(2).to_broadcast([128, 8, 128]), ALU.mult, [pt, gmix], [hT])
                if sec < 2:
                    continue
                cosT = cosr.next()
                sinT = sinr.next()
                dma(S, cosT[:], cos4[:, p0:p0 + 512], (), [cosT], cosT)
                dma(S, sinT[:], sin4[:, p0:p0 + 512], (), [sinT], sinT)
                kda = [kdall[0].next(), kdall[1].next()]
                pend1 = []
                for hc in range(4):
                    psq = fm_group(hT, hc * 128, 128)
                    sq = sqr.next()
                    act(S, sq[:], psq[:], AF.Silu, [psq], [sq])
                    def chain(d, sq=sq, hc=hc):
                        psf = fm_group(hT, 512 + d * 512 + hc * 128, 128)
                        sig = f32x[d].next()
                        nsig = f32x[d].next()
                        act(S, sig[:], psf[:], AF.Sigmoid, [psf], [sig])
                        act(S, nsig[:], psf[:], AF.Sigmoid, [psf], [nsig], scale=-1.0)
                        logf = logr.next()
                        act(S, logf[:], sig[:], AF.Ln, [sig, oml, lb], [logf], scale=oml[:, d, hc:hc + 1], bias=lb[:, d, hc:hc + 1])
                        omf = omfr.next()
                        ts(S, "dve", omf[:], nsig[:], oml[:, d, hc:hc + 1], None, ALU.mult, None, [nsig, oml], [omf])
                        yield
                        A = Ar.next()
                        S.op("dve", lambda e, A=A, logf=logf: e.tensor_tensor_scan(out=A[:], data0=ones512[:], data1=logf[:], initial=0.0,
                                                                                  op0=ALU.mult, op1=ALU.add), [ones512, logf], [A])
                        A3 = A[:].rearrange("p (c j) -> p c j", j=64)
                        aend = aendr.next()
                        memset(S, "pool", aend[:, 0:1], 0.0, [aend])
                        cp(S, "dve", aend[:, 1:9], A3[:, :, 63], [A], [aend])
                        dec = decr.next()
                        tt(S, "dve", dec[:], aend[:, 1:9], aend[:, 0:8], ALU.subtract, [aend], [dec])
                        act(S, dec[:], dec[:], AF.Exp, [dec], [dec])
                        dma(S, DECs[d][hc * 128:(hc + 1) * 128, n * 8:(n + 1) * 8], dec[:], [dec], (), dec)
                        yield
                        lo_b = aend[:, 0:8].unsqueeze(2).to_broadcast([128, 8, 64])
                        hi_b = aend[:, 1:9].unsqueeze(2).to_broadcast([128, 8, 64])
                        a = f32x[d].next()
                        a3 = a[:].rearrange("p (c j) -> p c j", j=64)
                        t1 = f32x[d].next()
                        t13 = t1[:].rearrange("p (c j) -> p c j", j=64)
                        if d == 0:
                            tt(S, "dve", a3, A3, lo_b, ALU.subtract, [A, aend], [a])
                            tt(S, "dve", t13, hi_b, A3, ALU.subtract, [A, aend], [t1])
                        else:
                            tt(S, "dve", t13, hi_b, A3, ALU.subtract, [A, aend], [t1])
                            tt(S, "pool", a[:], t1[:], logf[:], ALU.add, [t1, logf], [a])
                            tt(S, "pool", t1[:], A[:], logf[:], ALU.subtract, [A, logf], [t1])
                            tt(S, "dve", t13, t13, lo_b, ALU.subtract, [t1, aend], [t1])
                        yield
                        ea = f32x[d].next()
                        ena = f32x[d].next()
                        act(S, ea[:], a[:], AF.Exp, [a], [ea])
                        act(S, ena[:], a[:], AF.Exp, [a], [ena], scale=-1.0)
                        act(S, t1[:], t1[:], AF.Exp, [t1], [t1])
                        yield
                        qt = bfr.next()
                        kt = bfr.next()
                        tt(S, "dve", qt[:], sq[:], ea[:], ALU.mult, [sq, ea], [qt])
                        tt(S, "pool", kt[:], omf[:], ena[:], ALU.mult, [omf, ena], [kt])
                        dma(S, QTs[d][hc * 128:(hc + 1) * 128, t0:t0 + 512], qt[:], [qt], (), qt)
                        dma(S, KTs[d][hc * 128:(hc + 1) * 128, t0:t0 + 512], kt[:], [kt], (), kt)
                        kdT = kdtr.next()
                        tt(S, "dve", kdT[:], omf[:], t1[:], ALU.mult, [omf, t1], [kdT])
                        def fin_(kdT=kdT, d=d, hc=hc):
                            pt = psT.next()
                            ptb = pt[:].bitcast(BF16)
                            for s in range(4):
                                tr(S, ptb[:, s * 128:(s + 1) * 128], kdT[:, s * 128:(s + 1) * 128], identb[:], [kdT, identb], [pt])
                            cp(S, "act", kda[d][:, :, hc * 128:(hc + 1) * 128], ptb[:, 0:512].rearrange("p (s k) -> p s k", k=128), [pt], [kda[d]])
                        pend1.append(fin_)
                    gens = [chain(0), chain(1)]
                    first_ = True
                    while gens:
                        for g_ in list(gens):
                            try:
                                next(g_)
                            except StopIteration:
                                gens.remove(g_)
                        if first_:
                            for fn_ in pend1:
                                fn_()
                            pend1.clear()
                            first_ = False
                for fn_ in pend1:
                    fn_()
                pend1.clear()
                if sec < 3:
                    continue
                for d in range(2):
                    dma(S, KDs[d][t0:t0 + 512, :].rearrange("(s p) f -> p s f", p=128), kda[d][:], [kda[d]], (), kda[d])
                vt = vtr.next()
                gt = gtr.next()
                for s in range(4):
                    psv = tm_group(hT, s, 1536)
                    cp(S, "dve", vt[:, s, :], psv[:], [psv], [vt])
                    psg = tm_group(hT, s, 2048)
                    act(S, gt[:, s, :], psg[:], AF.Silu, [psg], [gt])
                dma(S, VH[t0:t0 + 512, :].rearrange("(s p) f -> p s f", p=128), vt[:], [vt], (), vt)
                dma(S, GH[t0:t0 + 512, :].rearrange("(s p) f -> p s f", p=128), gt[:], [gt], (), gt)

                if sec < 4:
                    continue
                def lora_norm(col0, nchunk, gain, D):
                    for c in range(nchunk):
                        ps = fm_group(hT, col0 + c * 128, 128)
                        act(S, sqc[:, c, :], ps[:], AF.Square, [ps], [sqc])
                        ts(S, "dve", cqg[:, c, :], ps[:], gain[:, c:c + 1], None, ALU.mult, None, [ps, gain], [cqg])
                    pss = psX.next()
                    for c in range(nchunk):
                        mm(S, pss[:, :], onesb[:], sqc[:, c, :], c == 0, c == nchunk - 1, [onesb, sqc], [pss])
                    act(S, rsb[:], pss[:], AF.Sqrt, [pss], [rsb], scale=1.0 / D, bias=EPS)
                    S.op("dve", lambda e: e.reciprocal(out=rsb[:], in_=rsb[:]), [rsb], [rsb])
                    for c in range(nchunk):
                        tt(S, ("dve", "pool")[c % 2], cqn[:, c, :], cqg[:, c, :], rsb[:], ALU.mult, [cqg, rsb], [cqn])

                def up_fm(Wt, nchunk, col0, ncol):
                    ps = psM.next()
                    for c in range(nchunk):
                        mm(S, ps[0:ncol, :], Wt[:, c, col0:col0 + ncol], cqn[:, c, :], c == 0, c == nchunk - 1, [Wt, cqn], [ps])
                    return ps

                lora_norm(2560, 3, gqa, 384.0)
                for g in range(4):
                    ps = up_fm(Wqn, 3, g * 128, 128)
                    o = bfr.next()
                    cp(S, ("dve", "act")[g % 2], o[:], ps[:], [ps], [o])
                    for hh in range(2):
                        h = 2 * g + hh
                        dma(S, AQ[h * 96:h * 96 + 64, t0:t0 + 512], o[hh * 64:(hh + 1) * 64, :], [o], (), o)
                for g in range(2):
                    pp = up_fm(Wqp, 3, g * 128, 128)
                    pw = up_fm(Wqs, 3, g * 128, 128)
                    r1 = f32r.next()
                    r2 = f32r.next()
                    tt(S, "dve", r1[:], pp[:], cosT[:], ALU.mult, [pp, cosT], [r1])
                    tt(S, "dve", r2[:], pw[:], sinT[:], ALU.mult, [pw, sinT], [r2])
                    o = bfr.next()
                    tt(S, "pool", o[:], r1[:], r2[:], ALU.add, [r1, r2], [o])
                    for hh in range(4):
                        h = 4 * g + hh
                        dma(S, AQ[h * 96 + 64:h * 96 + 96, t0:t0 + 512], o[hh * 32:(hh + 1) * 32, :], [o], (), o)
                if sec < 5:
                    continue
                lora_norm(2944, 2, gkva, 256.0)
                for g in range(4):
                    ps = up_fm(Wkn, 2, g * 128, 128)
                    o = bfr.next()
                    cp(S, ("dve", "act")[g % 2], o[:], ps[:], [ps], [o])
                    dma(S, AKN[g * 128:(g + 1) * 128, t0:t0 + 512], o[:], [o], (), o)
                vaug = vaugr.next()
                for s in range(4):
                    ps = psM.next()
                    for c in range(2):
                        mm(S, ps[:, :], cqn[:, c, s * 128:(s + 1) * 128], Wv[:, c, :], c == 0, c == 1, [Wv, cqn], [ps])
                    cp(S, ("dve", "act")[s % 2], vaug[:, s, :].rearrange("p (h e) -> p h e", e=65)[:, :, 0:64],
                       ps[:].rearrange("p (h e) -> p h e", e=64), [ps], [vaug])
                dma(S, AV[t0:t0 + 512, :].rearrange("(s p) f -> p s f", p=128), vaug[:], [vaug], (), vaug)
                pk = fm_group(hT, 3200, 32)
                pks = fm_group(hT, 5280, 32)
                r1 = f32r.next()
                r2 = f32r.next()
                tt(S, "dve", r1[0:32, :], pk[0:32, :], cosT[0:32, :], ALU.mult, [pk, cosT], [r1])
                tt(S, "dve", r2[0:32, :], pks[0:32, :], sinT[0:32, :], ALU.mult, [pks, sinT], [r2])
                o = bfr.next()
                tt(S, "pool", o[0:32, :], r1[0:32, :], r2[0:32, :], ALU.add, [r1, r2], [o])
                dma(S, AKP[:, t0:t0 + 512], o[0:32, :], [o], (), o)
                if sec < 6:
                    continue
                for gi in range(2):
                    dst = (SGH, SGA)[gi]
                    for c in range(8):
                        ps = fm_group(hT, 3232 + gi * 1024 + c * 128, 128)
                        sg = bfr.next()
                        act(S, sg[:], ps[:], AF.Sigmoid, [ps], [sg])
                        dma(S, dst[c * 128:(c + 1) * 128, t0:t0 + 512], sg[:], [sg], (), sg)

        for d in (range(2) if upto >= 2 else ()):
            with S.phase():
                msk = S.sb([64, 64], F32, "msk")
                dma(S, msk[:], masks_d[d], (), [msk], msk)
                qTr = S.ring(2, [128, 4, 512], BF16, "qT")
                kTr = S.ring(2, [128, 4, 512], BF16, "kT")
                kdr = S.ring(2, [64, 8, 512], BF16, "kd")
                vr = S.ring(2, [64, 8, 512], BF16, "v")
                dcr = S.ring(2, [128, 4, 8], F32, "dc")
                Sf = S.sb([128, 4, 128], F32, "Sf")
                Sb = S.sb([128, 4, 128], BF16, "Sb")
                smr = S.ring(3, [64, 4, 64], BF16, "sm")
                otr = S.ring(2, [64, 8, 512], F32, "ot")
                ps_s = S.psring(2, "ps_s")
                ps_o = S.psring(2, "ps_o")
                ps_u = S.psring(2, "ps_u")
                if d == 1:
                    ofr = S.ring(2, [64, 8, 512], F32, "of")
                    ghr = S.ring(2, [64, 8, 512], BF16, "gh")
                    sqt = S.sb([64, 8, 512], F32, "sqt")
                    msr = S.ring(2, [64, 32], F32, "ms")
                    onb = S.ring(2, [64, 8, 512], BF16, "onb")
                    otT = S.ring(2, [128, 4, 512], BF16, "otT")
                    ghg = S.sb([64, 512], F32, "ghg")
                    dma(S, ghg[:], g_hg.partition_broadcast(64), (), [ghg], ghg)
                    psT2 = S.psring(2, "psT2")
                seq0 = 0
                for L in seq_lens:
                    ntile = L // 512
                    memset(S, "pool", Sf[:], 0.0, [Sf])
                    memset(S, "pool", Sb[:], 0.0, [Sb])
                    order = range(ntile) if d == 0 else range(ntile - 1, -1, -1)
                    for ti in order:
                        t0 = seq0 + ti * 512
                        n = t0 // 512
                        qT = qTr.next()
                        kT = kTr.next()
                        kd = kdr.next()
                        v = vr.next()
                        dc = dcr.next()
                        dma(S, qT[:], QTs[d][:, t0:t0 + 512].rearrange("(h k) t -> k h t", k=128), (), [qT], qT)
                        dma(S, kT[:], KTs[d][:, t0:t0 + 512].rearrange("(h k) t -> k h t", k=128), (), [kT], kT)
                        dma(S, kd[:], KDs[d][t0:t0 + 512, :].rearrange("(c j) f -> j c f", j=64), (), [kd], kd)
                        dma(S, v[:], VH[t0:t0 + 512, :].rearrange("(c j) f -> j c f", j=64), (), [v], v)
                        dma(S, dc[:], DECs[d][:, n * 8:(n + 1) * 8].rearrange("(h k) c -> k h c", k=128), (), [dc], dc)
                        ot = otr.next()
                        corder = list(range(8)) if d == 0 else list(range(7, -1, -1))

                        def scores(c, kT=kT, qT=qT):
                            cs = slice(c * 64, (c + 1) * 64)
                            pss = ps_s.next()
                            for h in range(4):
                                mm(S, pss[0:64, h * 64:(h + 1) * 64], kT[:, h, cs], qT[:, h, cs], True, True, [kT, qT], [pss])
                            sm = smr.next()
                            tt(S, "dve", sm[:], pss[0:64, 0:256].rearrange("p (h t) -> p h t", t=64),
                               msk[:].unsqueeze(1).to_broadcast([64, 4, 64]), ALU.mult, [pss, msk], [sm])
                            return sm

                        sm_next = scores(corder[0])
                        for ci, c in enumerate(corder):
                            cs = slice(c * 64, (c + 1) * 64)
                            sm = sm_next
                            if ci + 1 < 8:
                                sm_next = scores(corder[ci + 1])
                            psu = ps_u.next()
                            for h in range(4):
                                hs = slice(h * 128, (h + 1) * 128)
                                mm(S, psu[:, hs], kd[:, c, hs], v[:, c, hs], True, True, [kd, v], [psu])
                            pso = ps_o.next()
                            for h in range(4):
                                hs = slice(h * 128, (h + 1) * 128)
                                mm(S, pso[0:64, hs], sm[:, h, :], v[:, c, hs], True, False, [sm, v], [pso])
                                mm(S, pso[0:64, hs], qT[:, h, cs], Sb[:, h, :], False, True, [qT, Sb], [pso])
                            cp(S, "act", ot[:, c, :], pso[0:64, :], [pso], [ot])
                            tt(S, "dve", Sf[:], Sf[:], dc[:, :, c].unsqueeze(2).to_broadcast([128, 4, 128]), ALU.mult, [Sf, dc], [Sf])
                            tt(S, "dve", Sf[:], Sf[:], psu[:].rearrange("p (h e) -> p h e", e=128), ALU.add, [Sf, psu], [Sf])
                            cp(S, "act", Sb[:], Sf[:], [Sf], [Sb])
                        if d == 0:
                            dma(S, OF[t0:t0 + 512, :].rearrange("(c j) f -> j c f", j=64), ot[:], [ot], (), ot)
                        else:
                            of = ofr.next()
                            gh = ghr.next()
                            dma(S, of[:], OF[t0:t0 + 512, :].rearrange("(c j) f -> j c f", j=64), (), [of], of)
                            dma(S, gh[:], GH[t0:t0 + 512, :].rearrange("(c j) f -> j c f", j=64), (), [gh], gh)
                            tt(S, "pool", ot[:], ot[:], of[:], ALU.add, [ot, of], [ot])
                            tt(S, "dve", sqt[:], ot[:], ot[:], ALU.mult, [ot], [sqt])
                            ms = msr.next()
                            S.op("dve", lambda e, ms=ms: e.tensor_reduce(out=ms[:], in_=sqt[:].rearrange("p c (h e) -> p (c h) e", e=128),
                                                                       axis=AX.X, op=ALU.add), [sqt], [ms])
                            act(S, ms[:], ms[:], AF.Sqrt, [ms], [ms], scale=1.0 / 128, bias=EPS)
                            S.op("dve", lambda e, ms=ms: e.reciprocal(out=ms[:], in_=ms[:]), [ms], [ms])
                            ot4 = ot[:].rearrange("p c (h e) -> p (c h) e", e=128)
                            tt(S, "dve", ot4, ot4, ms[:].unsqueeze(2).to_broadcast([64, 32, 128]), ALU.mult, [ot, ms], [ot])
                            tt(S, "pool", ot[:], ot[:], ghg[:].unsqueeze(1).to_broadcast([64, 8, 512]), ALU.mult, [ot, ghg], [ot])
                            ob = onb.next()
                            tt(S, "dve", ob[:], ot[:], gh[:], ALU.mult, [ot, gh], [ob])
                            oT = otT.next()
                            for hc in range(4):
                                pt = psT2.next()
                                ptb = pt[:].bitcast(BF16)
                                for c in range(8):
                                    tr(S, ptb[:, c * 64:(c + 1) * 64], ob[:, c, hc * 128:(hc + 1) * 128], identb[0:64, 0:64], [ob, identb], [pt])
                                cp(S, "act", oT[:, hc, :], ptb[:, 0:512], [pt], [oT])
                            dma(S, OT[:, t0:t0 + 512].rearrange("(c p) t -> p c t", p=128), oT[:], [oT], (), oT)
                    seq0 += L

        for _ in ((1,) if upto >= 3 else ()):
          with S.phase():
            Lmax = max(seq_lens)
            KTr = S.ring(2, [96, Lmax], BF16, "KT")
            Vall = S.sb([128, Lmax // 128, 520], BF16, "Vall")
            QTr = S.ring(3, [96, 512], BF16, "QTt")
            pTr = S.ring(4, [128, 512], BF16, "pT")
            ps_s = S.psring(4, "a_s")
            ps_o = S.psring(2, "a_o")
            ps_b = S.psring(2, "a_b")
            osr = S.ring(2, [65, 512], F32, "osb")
            atr = S.ring(2, [64, 512], BF16, "at")
            scale = float(96 ** -0.5)
            lur = S.ring(2, [128, 1024], F32, "lu")
            lvr = S.ring(2, [128, 1024], F32, "lv")
            uvo = S.ring(2, [128, 2048], BF16, "uvo")
            conv_left = list(range(n_exp // 128)) if upto >= 5 else []

            def conv_some(k):
                for _ in range(k):
                    if not conv_left:
                        return
                    r = conv_left.pop(0)
                    lu = lur.next(); lv = lvr.next(); o = uvo.next()
                    dma(S, lu[:], pu[r * 128:(r + 1) * 128, :], (), [lu], lu)
                    dma(S, lv[:], pv[r * 128:(r + 1) * 128, :], (), [lv], lv)
                    cp(S, "dve", o[:, 0:1024], lu[:], [lu], [o])
                    cp(S, "pool", o[:, 1024:2048], lv[:], [lv], [o])
                    dma(S, UV[r * 128:(r + 1) * 128, :], o[:], [o], (), o)
            items = []
            seq0 = 0
            for si, L in enumerate(seq_lens):
                for h in range(8):
                    for qi in range(L // 512):
                        items.append((si, seq0, L, h, qi))
                seq0 += L
            loaded = {}

            def prefetch(i):
                si, seq0, L, h, qi = items[i]
                if (si, h) not in loaded:
                    KT = KTr.next()
                    dma(S, KT[0:64, 0:L], AKN[h * 64:(h + 1) * 64, seq0:seq0 + L], (), [KT], KT)
                    dma(S, KT[64:96, 0:L], AKP[:, seq0:seq0 + L], (), [KT], KT)
                    loaded[(si, h)] = KT
                QTt = QTr.next()
                q0 = seq0 + qi * 512
                dma(S, QTt[:], AQ[h * 96:(h + 1) * 96, q0:q0 + 512], (), [QTt], QTt)
                loaded[i] = QTt

            prefetch(0)
            cur_seq = -1
            for i in range(len(items)):
                si, seq0, L, h, qi = items[i]
                nk = L // 128
                if si != cur_seq:
                    dma(S, Vall[:, 0:nk, :], AV[seq0:seq0 + L, :].rearrange("(n j) f -> j n f", j=128), (), [Vall], Vall)
                    cur_seq = si
                if i + 1 < len(items):
                    prefetch(i + 1)
                conv_some(-(-(n_exp // 128) // len(items)))
                KT = loaded[(si, h)]
                QTt = loaded.pop(i)
                q0 = seq0 + qi * 512
                pso = ps_o.next()
                pend = []
                LOOK = 2

                def pvmm(j, pT, pso=pso, h=h, nk=nk):
                    mm(S, pso[0:65, :], Vall[:, j, h * 65:(h + 1) * 65], pT[:], j == 0, j == nk - 1, [Vall, pT], [pso])

                for j in range(nk):
                    pss = ps_s.next()
                    mm(S, pss[:, :], KT[0:96, j * 128:(j + 1) * 128], QTt[:], True, True, [KT, QTt], [pss])
                    pT = pTr.next()
                    act(S, pT[:], pss[:], AF.Exp, [pss], [pT], scale=scale)
                    pend.append((j, pT))
                    if len(pend) > LOOK:
                        pvmm(*pend.pop(0))
                while pend:
                    pvmm(*pend.pop(0))
                osb = osr.next()
                cp(S, "dve", osb[:], pso[0:65, :], [pso], [osb])
                S.op("dve", lambda e, osb=osb: e.reciprocal(out=osb[64:65, :], in_=osb[64:65, :]), [osb], [osb])
                psb = ps_b.next()
                mm(S, psb[0:64, :], onesf[64:65, 0:64], osb[64:65, :], True, True, [onesf, osb], [psb])
                at = atr.next()
                tt(S, "dve", at[:], osb[0:64, :], psb[0:64, :], ALU.mult, [osb, psb], [at])
                dma(S, AT[h * 64:(h + 1) * 64, q0:q0 + 512], at[:], [at], (), at)
            conv_some(len(conv_left))

        for _ in ((1,) if upto >= 4 else ()):
          with S.phase():
            stg_ring = S.ring(3, [128, 1024], F32, "stg")
            Wbh = S.sb([128, 4, 1024], BF16, "Wbh")
            Wba = S.sb([64, 8, 1024], BF16, "Wba")
            Wo = S.sb([128, 8, 1024], BF16, "Wo")
            k = 0
            for c in range(4):
                load_w_bf16(Wbh, Wbh[:, c, :], w_brh[c * 128:(c + 1) * 128, :], [128, 1024], stg_ring, k); k += 1
            for h in range(8):
                load_w_bf16(Wba, Wba[:, h, :], w_bra[h * 64:(h + 1) * 64, :], [64, 1024], stg_ring, k); k += 1
            for c in range(8):
                load_w_bf16(Wo, Wo[:, c, :], w_out[c * 128:(c + 1) * 128, :], [128, 1024], stg_ring, k); k += 1
            OTr = S.ring(2, [128, 4, 512], BF16, "OTt")
            ATr = S.ring(2, [64, 8, 512], BF16, "ATt")
            sghr = S.ring(2, [128, 8, 512], BF16, "sgh")
            sgar = S.ring(2, [128, 8, 512], BF16, "sga")
            mgr = S.ring(2, [128, 8, 512], BF16, "mg")
            t1r = S.ring(3, [128, 512], F32, "t1")
            t2r = S.ring(3, [128, 512], F32, "t2")
            xr = S.ring(3, [128, 1024], F32, "x4")
            x1r = S.ring(3, [128, 1024], F32, "x1")
            psh = S.psring(2, "psh")
            psa = S.psring(2, "psa")
            psd = S.psring(4, "psd")
            for n in range(NT):
                t0 = n * 512
                OTt = OTr.next(); ATt = ATr.next(); sgh = sghr.next(); sga = sgar.next()
                dma(S, OTt[:], OT[:, t0:t0 + 512].rearrange("(c p) t -> p c t", p=128), (), [OTt], OTt)
                dma(S, ATt[:], AT[:, t0:t0 + 512].rearrange("(h e) t -> e h t", e=64), (), [ATt], ATt)
                dma(S, sgh[:], SGH[:, t0:t0 + 512].rearrange("(c p) t -> p c t", p=128), (), [sgh], sgh)
                dma(S, sga[:], SGA[:, t0:t0 + 512].rearrange("(c p) t -> p c t", p=128), (), [sga], sga)
                mg = mgr.next()
                for m in range(8):
                    ph = psh.next()
                    for c in range(4):
                        mm(S, ph[:, :], Wbh[:, c, m * 128:(m + 1) * 128], OTt[:, c, :], c == 0, c == 3, [Wbh, OTt], [ph])
                    pa = psa.next()
                    for h in range(8):
                        mm(S, pa[:, :], Wba[:, h, m * 128:(m + 1) * 128], ATt[:, h, :], h == 0, h == 7, [Wba, ATt], [pa])
                    t1 = t1r.next(); t2 = t2r.next()
                    tt(S, "dve", t1[:], ph[:], sgh[:, m, :], ALU.mult, [ph, sgh], [t1])
                    tt(S, "dve", t2[:], pa[:], sga[:, m, :], ALU.mult, [pa, sga], [t2])
                    tt(S, "pool", mg[:, m, :], t1[:], t2[:], ALU.add, [t1, t2], [mg])
                for s in range(4):
                    xt = xr.next()
                    dma(S, xt[:], x[t0 + s * 128:t0 + (s + 1) * 128, :], (), [xt], xt)
                    x1 = x1r.next()
                    for hf in range(2):
                        pd = psd.next()
                        for c in range(8):
                            mm(S, pd[:, :], mg[:, c, s * 128:(s + 1) * 128], Wo[:, c, hf * 512:(hf + 1) * 512], c == 0, c == 7, [mg, Wo], [pd])
                        tt(S, "dve", x1[:, hf * 512:(hf + 1) * 512], xt[:, hf * 512:(hf + 1) * 512], pd[:], ALU.add, [xt, pd], [x1])
                    dma(S, X1[t0 + s * 128:t0 + (s + 1) * 128, :], x1[:], [x1], (), x1)

        for _ in ((1,) if upto >= 5 else ()):
          with S.phase():
            stg_ring = S.ring(3, [128, 1024], F32, "stg")
            Wpq = S.sb([128, 8, 2048], BF16, "Wpq")
            k = 0
            for c in range(8):
                load_w_bf16(Wpq, Wpq[:, c, :], w_pq[c * 128:(c + 1) * 128, :], [128, 2048], stg_ring, k); k += 1
            skT = S.sb([128, 16, 128], BF16, "skT")
            skb = S.sb([128, 128], BF16, "skb")
            psT = S.psring(2, "psT")
            for g in range(16):
                stg = stg_ring.next()
                dma(S, stg[:, 0:128], skeys[g], (), [stg], stg)
                cp(S, "dve", skb[:], stg[:, 0:128], [stg], [skb])
                pt = psT.next()
                ptb = pt[:].bitcast(BF16)
                tr(S, ptb[:, 0:128], skb[:], identb[:], [skb, identb], [pt])
                cp(S, "act", skT[:, g, :], ptb[:, 0:128], [pt], [skT])
            gffn_c = S.sb([128, 8], F32, "gffn_c")
            dma(S, gffn_c[:], g_ffn.rearrange("(c p) -> p c", p=128), (), [gffn_c], gffn_c, slow=True)
            gffn_b = S.sb([128, 1024], F32, "gffn_b")
            dma(S, gffn_b[:], g_ffn.partition_broadcast(128), (), [gffn_b], gffn_b)
            gfin_b = S.sb([128, 1024], F32, "gfin_b")
            dma(S, gfin_b[:], g_fin.partition_broadcast(128), (), [gfin_b], gfin_b)
            iota16 = S.sb([128, 16], F32, "iota16")
            dma(S, iota16[:], iota_d[:, :], (), [iota16], iota16)

            x1r = S.ring(2, [128, 1024], F32, "x1")
            junk = S.sb([128, 1024], BF16, "junk")
            junkb = S.sb([128, 1024], BF16, "junkb")
            ssr = S.ring(2, [128, 1], F32, "ss")
            rsr = S.ring(2, [128, 1], F32, "rs")
            xnbr = S.ring(1, [128, 1024], BF16, "xnb")
            zr = S.ring(2, [128, 1024], BF16, "z")
            zTr = S.ring(1, [128, 8, 128], BF16, "zT")
            qTr = S.ring(1, [128, 16, 128], BF16, "qT")
            psq = S.psring(1, "psq")
            pssc = S.psring(1, "pssc")
            psacc = S.psring(4, "psacc")
            scr_ = S.ring(1, [128, 16, 128], F32, "sc")
            wkr = S.ring(2, [128, 128], F32, "wk")
            V1r = S.ring(2, [128, 16, 16], F32, "V1")
            I1r = S.ring(2, [128, 16, 16], U32, "I1")
            I1fr = S.ring(2, [128, 16, 16], F32, "I1f")
            candr = S.ring(1, [128, 8, 256], F32, "cand")
            wk2r = S.ring(2, [128, 256], F32, "wk2")
            tvr = S.ring(2, [128, 8, 16], F32, "tv")
            posr = S.ring(2, [128, 8, 16], U32, "pos")
            pir = S.ring(2, [128, 8, 16], U32, "pi")
            pjr = S.ring(2, [128, 8, 16], U32, "pj")
            fir = S.ring(2, [128, 8, 16], F32, "fi")
            fjr = S.ring(2, [128, 8, 16], F32, "fj")
            eqr = S.ring(1, [128, 8, 16, 16], BF16, "eq")
            eir = S.ring(2, [128, 8, 16], F32, "ei")
            ejr = S.ring(2, [128, 8, 16], F32, "ej")
            eidr = S.ring(2, [128, 128], I32, "eid")
            gwr = S.ring(2, [128, 8, 16], F32, "gw")
            zsr = S.ring(2, [128, 8], F32, "zs")
            dotr = S.ring(8, [128, 4], F32, "dots")
            actr = S.ring(8, [128, 4], F32, "actv")
            uvr = S.ring(20, [128, 2048], BF16, "uvg")
            dgr = S.ring(6, [128, 128], BF16, "dg")
            accr = S.ring(1, [128, 1024], F32, "acc")

            def prep(n, st):
                t0 = n * 128
                x1 = x1r.next()
                dma(S, x1[:], X1[t0:t0 + 128, :], (), [x1], x1)
                ss = ssr.next(); rs = rsr.next()
                act(S, junk[:], x1[:], AF.Square, [x1], [junk, ss], accum=ss[:, 0:1])
                rstd_from_ss(S, rs, ss, 1024.0, None)
                xnb = xnbr.next()
                act(S, xnb[:], x1[:], AF.Copy, [x1, rs], [xnb], scale=rs[:, 0:1])
                z = zr.next()
                stt(S, z[:], x1[:], rs[:, 0:1], gffn_b[:], ALU.mult, ALU.mult, [x1, rs, gffn_b], [z])
                pt = psT.next()
                ptb = pt[:].bitcast(BF16)
                for c in range(8):
                    tr(S, ptb[:, c * 128:(c + 1) * 128], xnb[:, c * 128:(c + 1) * 128], identb[:], [xnb, identb], [pt])
                zT = zTr.next()
                for c in range(8):
                    act(S, zT[:, c, :], ptb[:, c * 128:(c + 1) * 128], AF.Copy, [pt, gffn_c], [zT], scale=gffn_c[:, c:c + 1])
                yield
                qT = qTr.next()
                for gq in range(4):
                    pq = psq.next()
                    for gg in range(4):
                        g = gq * 4 + gg
                        for c in range(8):
                            mm(S, pq[:, gg * 128:(gg + 1) * 128], Wpq[:, c, g * 128:(g + 1) * 128], zT[:, c, :], c == 0, c == 7, [Wpq, zT], [pq])
                        yield
                    cp(S, "act", qT[:, gq * 4:(gq + 1) * 4, :], pq[:].rearrange("p (g t) -> p g t", t=128), [pq], [qT])
                sc = scr_.next()
                for gq in range(4):
                    pc = pssc.next()
                    for gg in range(4):
                        g = gq * 4 + gg
                        mm(S, pc[:, gg * 128:(gg + 1) * 128], qT[:, g, :], skT[:, g, :], True, True, [qT, skT], [pc])
                    cp(S, "act", sc[:, gq * 4:(gq + 1) * 4, :], pc[:].rearrange("p (g t) -> p g t", t=128), [pc], [sc])
                    yield
                yield
                V1 = V1r.next(); I1 = I1r.next()
                for g in range(16):
                    wk = wkr.next()
                    S.op("dve", lambda e, V1=V1, sc=sc, g=g: e.max(out=V1[:, g, 0:8], in_=sc[:, g, :]), [sc], [V1])
                    S.op("dve", lambda e, V1=V1, I1=I1, sc=sc, g=g: e.max_index(out=I1[:, g, 0:8], in_max=V1[:, g, 0:8], in_values=sc[:, g, :]), [sc, V1], [I1])
                    S.op("dve", lambda e, V1=V1, wk=wk, sc=sc, g=g: e.match_replace(out=wk[:], in_to_replace=V1[:, g, 0:8], in_values=sc[:, g, :], imm_value=-1e30), [sc, V1], [wk])
                    S.op("dve", lambda e, V1=V1, wk=wk, g=g: e.max(out=V1[:, g, 8:16], in_=wk[:]), [wk], [V1])
                    S.op("dve", lambda e, V1=V1, I1=I1, wk=wk, g=g: e.max_index(out=I1[:, g, 8:16], in_max=V1[:, g, 8:16], in_values=wk[:]), [wk, V1], [I1])
                    yield
                I1f = I1fr.next()
                cp(S, "dve", I1f[:], I1[:], [I1], [I1f])
                V1v = V1[:].rearrange("p (h two) k -> p h two k", two=2)
                I1v = I1f[:].rearrange("p (h two) k -> p h two k", two=2)
                cand = candr.next()
                tt(S, "dve", cand[:].rearrange("p h (i j) -> p h i j", j=16),
                   V1v[:, :, 0, :].unsqueeze(3).to_broadcast([128, 8, 16, 16]),
                   V1v[:, :, 1, :].unsqueeze(2).to_broadcast([128, 8, 16, 16]), ALU.add, [V1], [cand])
                yield
                tv = tvr.next(); pos = posr.next()
                for h in range(8):
                    wk2 = wk2r.next()
                    S.op("dve", lambda e, tv=tv, cand=cand, h=h: e.max(out=tv[:, h, 0:8], in_=cand[:, h, :]), [cand], [tv])
                    S.op("dve", lambda e, tv=tv, pos=pos, cand=cand, h=h: e.max_index(out=pos[:, h, 0:8], in_max=tv[:, h, 0:8], in_values=cand[:, h, :]), [cand, tv], [pos])
                    S.op("dve", lambda e, tv=tv, wk2=wk2, cand=cand, h=h: e.match_replace(out=wk2[:], in_to_replace=tv[:, h, 0:8], in_values=cand[:, h, :], imm_value=-1e30), [cand, tv], [wk2])
                    S.op("dve", lambda e, tv=tv, wk2=wk2, h=h: e.max(out=tv[:, h, 8:16], in_=wk2[:]), [wk2], [tv])
                    S.op("dve", lambda e, tv=tv, pos=pos, wk2=wk2, h=h: e.max_index(out=pos[:, h, 8:16], in_max=tv[:, h, 8:16], in_values=wk2[:]), [wk2, tv], [pos])
                    yield
                pi = pir.next(); pj = pjr.next(); fi = fir.next(); fj = fjr.next()
                S.op("dve", lambda e, pi=pi, pos=pos: e.tensor_single_scalar(out=pi[:], in_=pos[:], scalar=4, op=ALU.logical_shift_right), [pos], [pi])
                S.op("dve", lambda e, pj=pj, pos=pos: e.tensor_single_scalar(out=pj[:], in_=pos[:], scalar=15, op=ALU.bitwise_and), [pos], [pj])
                cp(S, "act", fi[:], pi[:], [pi], [fi])
                cp(S, "act", fj[:], pj[:], [pj], [fj])
                yield
                iob = iota16[:].unsqueeze(1).unsqueeze(1).to_broadcast([128, 8, 16, 16])
                ei = eir.next(); ej = ejr.next()
                for (ff, which, eo) in ((fi, 0, ei), (fj, 1, ej)):
                    eq = eqr.next()
                    tt(S, "dve", eq[:], ff[:].unsqueeze(3).to_broadcast([128, 8, 16, 16]), iob, ALU.is_equal, [ff, iota16], [eq])
                    yield
                    tt(S, "dve", eq[:], eq[:], I1v[:, :, which, :].unsqueeze(2).to_broadcast([128, 8, 16, 16]), ALU.mult, [eq, I1f], [eq])
                    yield
                    S.op("dve", lambda e, eo=eo, eq=eq: e.tensor_reduce(out=eo[:], in_=eq[:], axis=AX.X, op=ALU.add), [eq], [eo])
                    yield
                eid = eidr.next()
                stt(S, ei[:], ei[:], 128.0, ej[:], ALU.mult, ALU.add, [ei, ej], [ei])
                cp(S, "dve", eid[:], ei[:].rearrange("p h k -> p (h k)"), [ei], [eid])
                yield
                gw = gwr.next(); zs = zsr.next()
                tt(S, "dve", gw[:], tv[:], tv[:, :, 0:1].to_broadcast([128, 8, 16]), ALU.subtract, [tv], [gw])
                act(S, gw[:], gw[:], AF.Exp, [gw], [gw])
                S.op("dve", lambda e, zs=zs, gw=gw: e.tensor_reduce(out=zs[:], in_=gw[:], axis=AX.X, op=ALU.add), [gw], [zs])
                S.op("dve", lambda e, zs=zs: e.reciprocal(out=zs[:], in_=zs[:]), [zs], [zs])
                tt(S, "dve", gw[:], gw[:], zs[:].unsqueeze(2).to_broadcast([128, 8, 16]), ALU.mult, [gw, zs], [gw])
                st.update(x1=x1, z=z, eid=eid, gw=gw, t0=t0)
                yield

            def gather(st, nxt):
                x1 = st['x1']; z = st['z']; eid = st['eid']; gw = st['gw']; t0 = st['t0']
                gwf = gw[:].rearrange("p h k -> p (h k)")
                accA = psacc.next()
                accB = psacc.next()
                GB = 4
                for s0 in range(0, 128, GB):
                    rows = []
                    dots = dotr.next()
                    av = actr.next()
                    for sl in range(s0, s0 + GB):
                        uvg = uvr.next()
                        S.op("pool", lambda e, uvg=uvg, eid=eid, sl=sl: e.indirect_dma_start(
                            out=uvg[:], out_offset=None, in_=UV[:, :], in_offset=bass.IndirectOffsetOnAxis(ap=eid[:, sl:sl + 1], axis=0)),
                            [eid], [uvg], dma=uvg)
                        stt(S, junkb[:], uvg[:, 0:1024], 1.0, z[:], ALU.mult, ALU.mult, [z],
                            ([dots] if sl in (s0, s0 + GB - 1) else []), accum=dots[:, sl - s0:sl - s0 + 1], noreg=[uvg])
                        rows.append(uvg)
                    act(S, av[:], dots[:], AF.Gelu, [dots], [av])
                    for k_, sl in enumerate(range(s0, s0 + GB)):
                        uvg = rows[k_]
                        dg = dgr.next()
                        act(S, av[:, k_:k_ + 1], av[:, k_:k_ + 1], AF.Copy, [av, gw], [av], scale=gwf[:, sl:sl + 1])
                        act(S, dg[:], identb[:], AF.Copy, [identb, av], [dg], scale=av[:, k_:k_ + 1])
                        mm(S, accA[:, :], dg[:], uvg[:, 1024:1536], sl == 0, sl == 127, [dg, uvg], [accA])
                        mm(S, accB[:, :], dg[:], uvg[:, 1536:2048], sl == 0, sl == 127, [dg, uvg], [accB])
                    if nxt is not None:
                        next(nxt, None)
                        next(nxt, None)
                acc = accr.next()
                tt(S, "dve", acc[:, 0:512], x1[:, 0:512], accA[:, :], ALU.add, [x1, accA], [acc])
                tt(S, "dve", acc[:, 512:1024], x1[:, 512:1024], accB[:, :], ALU.add, [x1, accB], [acc])
                ss2 = ssr.next(); rs2 = rsr.next()
                act(S, junk[:], acc[:], AF.Square, [acc], [junk, ss2], accum=ss2[:, 0:1])
                rstd_from_ss(S, rs2, ss2, 1024.0, None)
                stt(S, acc[:], acc[:], rs2[:, 0:1], gfin_b[:], ALU.mult, ALU.mult, [acc, rs2, gfin_b], [acc])
                dma(S, y[t0:t0 + 128, :], acc[:], [acc], (), acc)


            ntile4 = TT // 128
            sts = [dict() for _ in range(ntile4)]
            for _ in prep(0, sts[0]):
                pass
            for n in range(ntile4):
                nxt = prep(n + 1, sts[n + 1]) if n + 1 < ntile4 else None
                gather(sts[n], nxt)
                if nxt is not None:
                    for _ in nxt:
                        pass

        S.barrier()
        S.emit()
    return nc


def _consts():
    half = 16
    inv = (1.0 / (10000.0 ** (np.arange(half, dtype=np.float32) / half))).astype(np.float32)
    ang = np.arange(8192, dtype=np.float32)[:, None] * inv[None, :]
    cos, sin = np.cos(ang).astype(np.float32).T, np.sin(ang).astype(np.float32).T
    cos32 = np.concatenate([cos, cos], 0)
    sin32 = np.concatenate([-sin, sin], 0)
    cos4 = np.ascontiguousarray(np.tile(cos32, (4, 1)))
    sin4 = np.ascontiguousarray(np.tile(sin32, (4, 1)))
    ident = np.eye(128, dtype=np.float32)
    s = np.arange(64)
    mU = (s[:, None] <= s[None, :]).astype(np.float32)
    mL = (s[:, None] >= s[None, :]).astype(np.float32)
    masks = np.stack([mU, mL], 0)
    iota16 = np.tile(np.arange(16, dtype=np.float32)[None, :], (128, 1))
    return dict(cos4=cos4, sin4=sin4, ident=ident, masks=masks, iota16=np.ascontiguousarray(iota16))


def _prep_weights(lb_logits, norm_mix, w_in, hg_out_norm, w_br_h, q_a_norm, w_uq, kv_a_norm, w_ukv,
                  w_br_a, w_out, norm_ffn, peer_wq, peer_sub_keys, peer_u, peer_v, norm_final):
    f = lambda a: np.ascontiguousarray(np.asarray(a, dtype=np.float32))
    w_in0 = f(w_in[0])
    kpe = w_in0[:, 3200:3232]
    kpe_sw = np.concatenate([kpe[:, 16:32], kpe[:, 0:16]], 1)
    w_in_ext = np.concatenate([w_in0, kpe_sw], 1)
    wuq = f(w_uq[0]).reshape(384, 8, 96)
    wq_n = wuq[:, :, 0:64].reshape(384, 512)
    wq_p = wuq[:, :, 64:96].reshape(384, 256)
    wq_s = np.concatenate([wuq[:, :, 80:96], wuq[:, :, 64:80]], 2).reshape(384, 256)
    wukv = f(w_ukv[0]).reshape(256, 8, 128)
    wk_n = wukv[:, :, 0:64].reshape(256, 512)
    wv = wukv[:, :, 64:128].reshape(256, 512)
    sk = f(peer_sub_keys[0])
    skeys = sk.transpose(1, 0, 2, 3).reshape(16, 128, 128)
    d = dict(w_in=w_in_ext, lbl=f(lb_logits), g_mix=f(norm_mix[0]), g_hg=f(hg_out_norm[0]), w_brh=f(w_br_h[0]),
             g_qa=f(q_a_norm[0]), wq_n=wq_n, wq_p=wq_p, wq_s=wq_s, g_kva=f(kv_a_norm[0]), wk_n=wk_n, wv=wv,
             w_bra=f(w_br_a[0]), w_out=f(w_out[0]), g_ffn=f(norm_ffn[0]), w_pq=f(peer_wq[0]), skeys=skeys,
             pu=f(peer_u[0]), pv=f(peer_v[0]), g_fin=f(norm_final))
    d = {k: np.ascontiguousarray(v) for k, v in d.items()}
    d.update(_consts())
    return d


def kernel(x_prompt, x_sample, lb_logits, norm_mix, w_in, hg_out_norm, w_br_h, q_a_norm, w_uq,
           kv_a_norm, w_ukv, w_br_a, w_out, norm_ffn, peer_wq, peer_sub_keys, peer_u, peer_v, norm_final):
    xp = np.asarray(x_prompt, dtype=np.float32)
    xs = np.asarray(x_sample, dtype=np.float32)
    B, SP, D = xp.shape
    BS, SS, _ = xs.shape
    n = 8
    ppc = B // n
    spc = BS // n
    seq_lens = [SP] * ppc + [SS] * spc
    wd = _prep_weights(lb_logits, norm_mix, w_in, hg_out_norm, w_br_h, q_a_norm, w_uq, kv_a_norm, w_ukv,
                       w_br_a, w_out, norm_ffn, peer_wq, peer_sub_keys, peer_u, peer_v, norm_final)
    nc = build(seq_lens)
    in_maps = []
    for c in range(n):
        parts = [xp[c * ppc + i] for i in range(ppc)] + [xs[c * spc + i] for i in range(spc)]
        m = dict(wd)
        m["x"] = np.ascontiguousarray(np.concatenate(parts, 0))
        in_maps.append(m)
    res = run_bass_kernel_spmd(nc, in_maps, core_ids=list(range(n)))
    yp = np.empty_like(xp)
    ys = np.empty_like(xs)
    for c in range(n):
        yc = res.results[c]["y"]
        o = 0
        for i in range(ppc):
            yp[c * ppc + i] = yc[o:o + SP]; o += SP
        for i in range(spc):
            ys[c * spc + i] = yc[o:o + SS]; o += SS
    return (yp, ys)
```

```python
import contextlib
import numpy as np
import concourse.bass as bass
import concourse.mybir as mybir
from concourse.bass_utils import run_bass_kernel_spmd

F32 = mybir.dt.float32
BF16 = mybir.dt.bfloat16
I32 = mybir.dt.int32
U32 = mybir.dt.uint32
ALU = mybir.AluOpType
AF = mybir.ActivationFunctionType
AX = mybir.AxisListType

EPS = 1e-6
EPOCH = 30000
DLIMIT = 30000
ENGS = ("pe", "act", "dve", "pool", "sp")


class Buf:
    _n = 0

    def __init__(self, name=""):
        Buf._n += 1
        self.id = Buf._n
        self.name = name
        self.w = None
        self.r = []
        self.psum = False


class T:
    def __init__(self, t, buf):
        self.t = t
        self.b = buf

    def __getitem__(self, k):
        return self.t[k]


class Ring:
    def __init__(self, tiles):
        self.tiles = tiles
        self.i = 0

    def next(self):
        t = self.tiles[self.i % len(self.tiles)]
        self.i += 1
        return t


class Sched:
    def __init__(self, nc, gstack):
        self.nc = nc
        self.gstack = gstack
        self.stack = gstack
        self.q = {e: [] for e in ENGS}
        self.count = {e: 0 for e in ENGS}
        self.seen = {e: {} for e in ENGS}
        self.sems = {}
        self.all_tokens = {}
        self.n_sb = 0
        self.dmasem = {}
        self.free_dsems = []
        self.n_dsem = 0

    def sb(self, shape, dtype, name="t"):
        self.n_sb += 1
        t = self.stack.enter_context(self.nc.sbuf_tensor(f"{name}_{self.n_sb}", list(shape), dtype))
        return T(t, Buf(name))

    def ring(self, n, shape, dtype, name="r"):
        return Ring([self.sb(shape, dtype, name) for _ in range(n)])

    def ps(self, shape, dtype, name="ps"):
        self.n_sb += 1
        t = self.stack.enter_context(self.nc.psum_tensor(f"{name}_{self.n_sb}", list(shape), dtype))
        b = Buf(name)
        b.psum = True
        return T(t, b)

    def psring(self, n, name="ps"):
        return Ring([self.ps([128, 512], F32, name) for _ in range(n)])

    def _dma_tok(self, buf):
        ent = self.dmasem.get(buf.id)
        if ent is None or ent[1] + 16 > DLIMIT:
            if self.free_dsems:
                j, base = self.free_dsems.pop()
            else:
                j, base = self.n_dsem, 0
                self.n_dsem += 1
            ent = [j, base]
            self.dmasem[buf.id] = ent
        ent[1] += 16
        return (("dsem", ent[0]), ent[1])

    def op(self, eng, fn, reads=(), writes=(), dma=None, noreg=()):
        reads = [x.b if isinstance(x, T) else x for x in reads]
        writes = [x.b if isinstance(x, T) else x for x in writes]
        noreg = [x.b if isinstance(x, T) else x for x in noreg]
        writes = writes + [r for r in reads if r.psum and r not in writes]
        reads = [r for r in reads if not r.psum]
        need = {}

        def add(tok):
            if tok is None:
                return
            k, v = tok
            if need.get(k, 0) < v:
                need[k] = v

        for r in reads:
            add(r.w)
        for r in noreg:
            add(r.w)
        for w in writes:
            add(w.w)
            for t in w.r:
                add(t)
        waits = []
        seen = self.seen[eng]
        for k, v in need.items():
            if eng == "pe" and k[0] == "eng" and k[1] == "pe":
                continue
            if seen.get(k, 0) >= v:
                continue
            seen[k] = v
            waits.append((k, v))
        if dma is not None:
            dma = dma.b if isinstance(dma, T) else dma
            tok = self._dma_tok(dma)
            inc = (tok[0], 16)
        else:
            idx = self.count[eng]
            self.count[eng] += 1
            key = ("eng", eng, idx // EPOCH)
            tok = (key, idx % EPOCH + 1)
            inc = (key, 1)
        key = tok[0]
        self.all_tokens[key] = max(self.all_tokens.get(key, 0), tok[1])
        self.q[eng].append((fn, waits, inc))
        for r in reads:
            r.r.append(tok)
        for w in writes:
            w.w = tok
            w.r = []
        return tok

    def barrier(self):
        for e in ENGS:
            waits = []
            seen = self.seen[e]
            for k, v in self.all_tokens.items():
                if seen.get(k, 0) >= v:
                    continue
                seen[k] = v
                waits.append((k, v))
            if waits:
                self.q[e].append((None, waits, None))
        for bid, (j, cnt) in self.dmasem.items():
            if cnt < DLIMIT - 4000:
                self.free_dsems.append((j, cnt))
        self.dmasem = {}

    @contextlib.contextmanager
    def phase(self):
        with contextlib.ExitStack() as st:
            self.stack = st
            yield
            self.barrier()
        self.stack = self.gstack

    def emit(self):
        nc = self.nc
        sems = {}
        for k in self.all_tokens.keys():
            nm = "s_" + "_".join(str(x) for x in k)
            sems[k] = self.gstack.enter_context(nc.semaphore(nm))
        q = self.q

        def run(engobj, lst):
            for fn, waits, inc in lst:
                for k, v in waits:
                    engobj.wait_ge(sems[k], v)
                if fn is not None:
                    ins = fn(engobj)
                    ins.then_inc(sems[inc[0]], inc[1])

        with nc.Block() as block:
            @block.tensor
            def _(e):
                run(e, q["pe"])

            @block.scalar
            def _(e):
                run(e, q["act"])

            @block.vector
            def _(e):
                run(e, q["dve"])

            @block.gpsimd
            def _(e):
                run(e, q["pool"])

            @block.sync
            def _(e):
                run(e, q["sp"])


def dma(S, out, in_, reads=(), writes=(), owner=None, eng="sp", slow=False):
    if slow:
        S.op(eng, lambda e: e.dma_start(out=out, in_=in_, allow_slow_non_contiguous=True), reads, writes, dma=owner)
    else:
        S.op(eng, lambda e: e.dma_start(out=out, in_=in_), reads, writes, dma=owner)


def mm(S, out, lhsT, rhs, start, stop, reads, writes):
    S.op("pe", lambda e: e.matmul(out, lhsT=lhsT, rhs=rhs, start=start, stop=stop), reads, writes)


def tr(S, out, in_, ident, reads, writes):
    S.op("pe", lambda e: e.transpose(out=out, in_=in_, identity=ident), reads, writes)


def act(S, out, in_, func, reads, writes, scale=None, bias=None, accum=None):
    kw = {}
    if scale is not None:
        kw["scale"] = scale
    if bias is not None:
        kw["bias"] = bias
    if accum is not None:
        kw["accum_out"] = accum
    S.op("act", lambda e: e.activation(out=out, in_=in_, func=func, **kw), reads, writes)


def tt(S, eng, out, in0, in1, op, reads, writes):
    S.op(eng, lambda e: e.tensor_tensor(out=out, in0=in0, in1=in1, op=op), reads, writes)


def ts(S, eng, out, in0, s1, s2, op0, op1, reads, writes):
    if op1 is None:
        S.op(eng, lambda e: e.tensor_scalar(out=out, in0=in0, scalar1=s1, scalar2=None, op0=op0), reads, writes)
    else:
        S.op(eng, lambda e: e.tensor_scalar(out=out, in0=in0, scalar1=s1, scalar2=s2, op0=op0, op1=op1), reads, writes)


def stt(S, out, in0, scalar, in1, op0, op1, reads, writes, accum=None, noreg=()):
    if accum is None:
        S.op("dve", lambda e: e.scalar_tensor_tensor(out=out, in0=in0, scalar=scalar, in1=in1, op0=op0, op1=op1), reads, writes, noreg=noreg)
    else:
        S.op("dve", lambda e: e.scalar_tensor_tensor(out=out, in0=in0, scalar=scalar, in1=in1, op0=op0, op1=op1, accum_out=accum), reads, writes, noreg=noreg)


def cp(S, eng, out, in_, reads, writes):
    if eng == "act":
        S.op("act", lambda e: e.activation(out=out, in_=in_, func=AF.Copy), reads, writes)
    else:
        S.op(eng, lambda e: e.tensor_copy(out=out, in_=in_), reads, writes)


def memset(S, eng, ap, val, writes):
    S.op(eng, lambda e: e.memset(ap, val), (), writes)


def rstd_from_ss(S, rstd, ss, n, reads_t):
    act(S, rstd[:], ss[:], AF.Sqrt, [ss], [rstd], scale=1.0 / n, bias=EPS)
    S.op("dve", lambda e: e.reciprocal(out=rstd[:], in_=rstd[:]), [rstd], [rstd])


def build(seq_lens, n_exp=16384, upto=9, sec=99):
    TT = sum(seq_lens)
    assert all(L % 512 == 0 for L in seq_lens)
    NT = TT // 512
    NCH = TT // 64
    nc = bass.Bass("TRN2", target_bir_lowering=False)

    def din(name, shape, dt=F32):
        return nc.dram_tensor(name, list(shape), dt, kind="ExternalInput").ap()

    def dscr(name, shape, dt):
        return nc.dram_tensor(name, list(shape), dt, kind="Internal").ap()

    x = din("x", [TT, 1024])
    w_in = din("w_in", [1024, 5312])
    lbl = din("lbl", [2, 2, 512])
    g_mix = din("g_mix", [1024])
    g_hg = din("g_hg", [512])
    w_brh = din("w_brh", [512, 1024])
    g_qa = din("g_qa", [384])
    wq_n = din("wq_n", [384, 512])
    wq_p = din("wq_p", [384, 256])
    wq_s = din("wq_s", [384, 256])
    g_kva = din("g_kva", [256])
    wk_n = din("wk_n", [256, 512])
    wv = din("wv", [256, 512])
    w_bra = din("w_bra", [512, 1024])
    w_out = din("w_out", [1024, 1024])
    g_ffn = din("g_ffn", [1024])
    w_pq = din("w_pq", [1024, 2048])
    skeys = din("skeys", [16, 128, 128])
    pu = din("pu", [n_exp, 1024])
    pv = din("pv", [n_exp, 1024])
    g_fin = din("g_fin", [1024])
    cos4 = din("cos4", [128, 8192])
    sin4 = din("sin4", [128, 8192])
    ident_d = din("ident", [128, 128])
    masks_d = din("masks", [2, 64, 64])
    iota_d = din("iota16", [128, 16])
    y = nc.dram_tensor("y", [TT, 1024], F32, kind="ExternalOutput").ap()

    QTs = [dscr(f"QT{d}", [512, TT], BF16) for d in range(2)]
    KTs = [dscr(f"KT{d}", [512, TT], BF16) for d in range(2)]
    KDs = [dscr(f"KD{d}", [TT, 512], BF16) for d in range(2)]
    DECs = [dscr(f"DEC{d}", [512, NCH], F32) for d in range(2)]
    VH = dscr("VH", [TT, 512], BF16)
    GH = dscr("GH", [TT, 512], BF16)
    AQ = dscr("AQ", [768, TT], BF16)
    AKN = dscr("AKN", [512, TT], BF16)
    AKP = dscr("AKP", [32, TT], BF16)
    AV = dscr("AV", [TT, 520], BF16)
    SGH = dscr("SGH", [1024, TT], BF16)
    SGA = dscr("SGA", [1024, TT], BF16)
    OF = dscr("OF", [TT, 512], F32)
    OT = dscr("OT", [512, TT], BF16)
    AT = dscr("AT", [512, TT], BF16)
    X1 = dscr("X1", [TT, 1024], F32)
    UV = dscr("UV", [n_exp, 2048], BF16)

    with contextlib.ExitStack() as gstack:
        S = Sched(nc, gstack)
        identb = S.sb([128, 128], BF16, "identb")
        onesb = S.sb([128, 128], BF16, "onesb")
        onesf = S.sb([128, 128], F32, "onesf")
        stage0 = S.sb([128, 128], F32, "stage0")
        dma(S, stage0[:], ident_d[:, :], (), [stage0], stage0)
        cp(S, "dve", identb[:], stage0[:], [stage0], [identb])
        memset(S, "pool", onesb[:], 1.0, [onesb])
        memset(S, "pool", onesf[:], 1.0, [onesf])

        def load_w_bf16(dst, dst_ap, src_ap, shape, stg_ring, i):
            P, N = shape
            for n0 in range(0, N, 1024):
                n1 = min(N, n0 + 1024)
                stg = stg_ring.next()
                v = stg[0:P, 0:n1 - n0]
                dma(S, v, src_ap[:, n0:n1], (), [stg], stg)
                cp(S, ("dve", "pool", "act")[(i + n0 // 1024) % 3], dst_ap[:, n0:n1], v, [stg], [dst])

        with S.phase():
            W = S.sb([128, 8, 5312], BF16, "W")
            w_in_v = w_in.rearrange("(c p) n -> p c n", p=128)
            Wqn = S.sb([128, 3, 512], BF16, "Wqn")
            Wqp = S.sb([128, 3, 256], BF16, "Wqp")
            Wqs = S.sb([128, 3, 256], BF16, "Wqs")
            Wkn = S.sb([128, 2, 512], BF16, "Wkn")
            Wv = S.sb([128, 2, 512], BF16, "Wv")
            def _load_p1_weights(stg_ring):
                k = 0
                for c in range(8):
                    load_w_bf16(W, W[:, c, :], w_in_v[:, c, :], [128, 5312], stg_ring, k)
                    k += 1
                for (dst, src, nchunk, ncol) in ((Wqn, wq_n, 3, 512), (Wqp, wq_p, 3, 256), (Wqs, wq_s, 3, 256),
                                                 (Wkn, wk_n, 2, 512), (Wv, wv, 2, 512)):
                    for c in range(nchunk):
                        load_w_bf16(dst, dst[:, c, :], src[c * 128:(c + 1) * 128, :], [128, ncol], stg_ring, k)
                        k += 1
            gmix = S.sb([128, 8], F32, "gmix")
            dma(S, gmix[:], g_mix.rearrange("(c p) -> p c", p=128), (), [gmix], gmix, slow=True)
            gqa = S.sb([128, 3], F32, "gqa")
            dma(S, gqa[:], g_qa.rearrange("(c p) -> p c", p=128), (), [gqa], gqa, slow=True)
            gkva = S.sb([128, 2], F32, "gkva")
            dma(S, gkva[:], g_kva.rearrange("(c p) -> p c", p=128), (), [gkva], gkva, slow=True)
            l0 = S.sb([128, 2, 4], F32, "l0")
            l1 = S.sb([128, 2, 4], F32, "l1")
            lb = S.sb([128, 2, 4], F32, "lb")
            oml = S.sb([128, 2, 4], F32, "oml")
            dma(S, l0[:], lbl[0].rearrange("d (h p) -> p d h", p=128), (), [l0], l0, slow=True)
            dma(S, l1[:], lbl[1].rearrange("d (h p) -> p d h", p=128), (), [l1], l1, slow=True)
            tt(S, "dve", l0[:], l0[:], l1[:], ALU.subtract, [l0, l1], [l0])
            act(S, lb[:], l0[:], AF.Sigmoid, [l0], [lb])
            ts(S, "dve", oml[:], lb[:], -1.0, 1.0, ALU.mult, ALU.add, [lb], [oml])

            xring = S.ring(2, [128, 1024], F32, "x")
            ssr = S.ring(2, [128, 1], F32, "ss")
            rsr = S.ring(2, [128, 1], F32, "rstd")
            xnr = S.ring(2, [128, 1024], BF16, "xn")
            hTr = S.ring(1, [128, 8, 512], BF16, "hT")
            psT = S.psring(2, "psT")
            psM = S.psring(4, "psM")
            psX = S.psring(2, "psX")
            f32r = S.ring(6, [128, 512], F32, "f32r")
            sqr = S.ring(2, [128, 512], F32, "sq")
            Ar = S.ring(2, [128, 512], F32, "A")
            logr = S.ring(2, [128, 512], F32, "logf")
            omfr = S.ring(2, [128, 512], F32, "omf")
            aendr = S.ring(2, [128, 9], F32, "aend")
            decr = S.ring(2, [128, 8], F32, "dec")
            bfr = S.ring(6, [128, 512], BF16, "bfr")
            kdtr = S.ring(2, [128, 512], BF16, "kdT")
            kdall = [S.ring(1, [128, 4, 512], BF16, "kdall") for _ in range(2)]
            vtr = S.ring(1, [128, 4, 512], BF16, "vt")
            gtr = S.ring(1, [128, 4, 512], BF16, "gt")
            cqg = S.sb([128, 3, 512], F32, "cqg")
            sqc = S.sb([128, 3, 512], BF16, "sqc")
            cqn = S.sb([128, 3, 512], BF16, "cqn")
            rsb = S.sb([128, 512], F32, "rsb")
            vaugr = S.ring(1, [128, 4, 520], BF16, "vaug")
            for t_ in vaugr.tiles:
                memset(S, "pool", t_[:], 1.0, [t_])
            cosr = S.ring(1, [128, 512], F32, "cos")
            sinr = S.ring(1, [128, 512], F32, "sin")
            ones512 = S.sb([128, 512], F32, "ones512")
            memset(S, "pool", ones512[:], 1.0, [ones512])
            with contextlib.ExitStack() as wst:
                _pst = S.stack
                S.stack = wst
                stg_ring = S.ring(3, [128, 1024], F32, "stg")
                S.stack = _pst
                _load_p1_weights(stg_ring)
            f32r2 = S.ring(6, [128, 512], F32, "f32r2")
            f32x = (f32r, f32r2)

            def fm_group(hT, col0, ncol):
                ps = psM.next()
                for c in range(8):
                    mm(S, ps[0:ncol, :], W[:, c, col0:col0 + ncol], hT[:, c, :], c == 0, c == 7, [W, hT], [ps])
                return ps

            def tm_group(hT, s, col0):
                ps = psM.next()
                for c in range(8):
                    mm(S, ps[:, :], hT[:, c, s * 128:(s + 1) * 128], W[:, c, col0:col0 + 512], c == 0, c == 7, [W, hT], [ps])
                return ps

            tile_pos = []
            for L in seq_lens:
                for p0 in range(0, L, 512):
                    tile_pos.append(p0)

            for n in range(NT if sec > 0 else 0):
                t0 = n * 512
                p0 = tile_pos[n]
                hT = hTr.next()
                for s in range(4):
                    xt = xring.next()
                    dma(S, xt[:], x[t0 + s * 128:t0 + (s + 1) * 128, :], (), [xt], xt)
                    ss = ssr.next()
                    rs = rsr.next()
                    xn = xnr.next()
                    act(S, xn[:], xt[:], AF.Square, [xt], [xn, ss], accum=ss[:, 0:1])
                    rstd_from_ss(S, rs, ss, 1024.0, None)
                    act(S, xn[:], xt[:], AF.Copy, [xt, rs], [xn], scale=rs[:, 0:1])
                    pt = psT.next()
                    ptb = pt[:].bitcast(BF16)
                    for c in range(8):
                        tr(S, ptb[:, c * 128:(c + 1) * 128], xn[:, c * 128:(c + 1) * 128], identb[:], [xn, identb], [pt])
                    tt(S, "dve", hT[:, :, s * 128:(s + 1) * 128], ptb.rearrange("p (c t) -> p c t", t=128),
                       gmix[:].unsqueeze(2).to_broadcast([128, 8, 128]), ALU.mult, [pt, gmix], [hT])
                if sec < 2:
                    continue
                cosT = cosr.next()
                sinT = sinr.next()
                dma(S, cosT[:], cos4[:, p0:p0 + 512], (), [cosT], cosT)
                dma(S, sinT[:], sin4[:, p0:p0 + 512], (), [sinT], sinT)
                kda = [kdall[0].next(), kdall[1].next()]
                pend1 = []
                for hc in range(4):
                    psq = fm_group(hT, hc * 128, 128)
                    sq = sqr.next()
                    act(S, sq[:], psq[:], AF.Silu, [psq], [sq])
                    def chain(d, sq=sq, hc=hc):
                        psf = fm_group(hT, 512 + d * 512 + hc * 128, 128)
                        sig = f32x[d].next()
                        nsig = f32x[d].next()
                        act(S, sig[:], psf[:], AF.Sigmoid, [psf], [sig])
                        act(S, nsig[:], psf[:], AF.Sigmoid, [psf], [nsig], scale=-1.0)
                        logf = logr.next()
                        act(S, logf[:], sig[:], AF.Ln, [sig, oml, lb], [logf], scale=oml[:, d, hc:hc + 1], bias=lb[:, d, hc:hc + 1])
                        omf = omfr.next()
                        ts(S, "dve", omf[:], nsig[:], oml[:, d, hc:hc + 1], None, ALU.mult, None, [nsig, oml], [omf])
                        yield
                        A = Ar.next()
                        S.op("dve", lambda e, A=A, logf=logf: e.tensor_tensor_scan(out=A[:], data0=ones512[:], data1=logf[:], initial=0.0,
                                                                                  op0=ALU.mult, op1=ALU.add), [ones512, logf], [A])
                        A3 = A[:].rearrange("p (c j) -> p c j", j=64)
                        aend = aendr.next()
                        memset(S, "pool", aend[:, 0:1], 0.0, [aend])
                        cp(S, "dve", aend[:, 1:9], A3[:, :, 63], [A], [aend])
                        dec = decr.next()
                        tt(S, "dve", dec[:], aend[:, 1:9], aend[:, 0:8], ALU.subtract, [aend], [dec])
                        act(S, dec[:], dec[:], AF.Exp, [dec], [dec])
                        dma(S, DECs[d][hc * 128:(hc + 1) * 128, n * 8:(n + 1) * 8], dec[:], [dec], (), dec)
                        yield
                        lo_b = aend[:, 0:8].unsqueeze(2).to_broadcast([128, 8, 64])
                        hi_b = aend[:, 1:9].unsqueeze(2).to_broadcast([128, 8, 64])
                        a = f32x[d].next()
                        a3 = a[:].rearrange("p (c j) -> p c j", j=64)
                        t1 = f32x[d].next()
                        t13 = t1[:].rearrange("p (c j) -> p c j", j=64)
                        if d == 0:
                            tt(S, "dve", a3, A3, lo_b, ALU.subtract, [A, aend], [a])
                            tt(S, "dve", t13, hi_b, A3, ALU.subtract, [A, aend], [t1])
                        else:
                            tt(S, "dve", t13, hi_b, A3, ALU.subtract, [A, aend], [t1])
                            tt(S, "pool", a[:], t1[:], logf[:], ALU.add, [t1, logf], [a])
                            tt(S, "pool", t1[:], A[:], logf[:], ALU.subtract, [A, logf], [t1])
                            tt(S, "dve", t13, t13, lo_b, ALU.subtract, [t1, aend], [t1])
                        yield
                        ea = f32x[d].next()
                        ena = f32x[d].next()
                        act(S, ea[:], a[:], AF.Exp, [a], [ea])
                        act(S, ena[:], a[:], AF.Exp, [a], [ena], scale=-1.0)
                        act(S, t1[:], t1[:], AF.Exp, [t1], [t1])
                        yield
                        qt = bfr.next()
                        kt = bfr.next()
                        tt(S, "dve", qt[:], sq[:], ea[:], ALU.mult, [sq, ea], [qt])
                        tt(S, "pool", kt[:], omf[:], ena[:], ALU.mult, [omf, ena], [kt])
                        dma(S, QTs[d][hc * 128:(hc + 1) * 128, t0:t0 + 512], qt[:], [qt], (), qt)
                        dma(S, KTs[d][hc * 128:(hc + 1) * 128, t0:t0 + 512], kt[:], [kt], (), kt)
                        kdT = kdtr.next()
                        tt(S, "dve", kdT[:], omf[:], t1[:], ALU.mult, [omf, t1], [kdT])
                        def fin_(kdT=kdT, d=d, hc=hc):
                            pt = psT.next()
                            ptb = pt[:].bitcast(BF16)
                            for s in range(4):
                                tr(S, ptb[:, s * 128:(s + 1) * 128], kdT[:, s * 128:(s + 1) * 128], identb[:], [kdT, identb], [pt])
                            cp(S, "act", kda[d][:, :, hc * 128:(hc + 1) * 128], ptb[:, 0:512].rearrange("p (s k) -> p s k", k=128), [pt], [kda[d]])
                        pend1.append(fin_)
                    gens = [chain(0), chain(1)]
                    first_ = True
                    while gens:
                        for g_ in list(gens):
                            try:
                                next(g_)
                            except StopIteration:
                                gens.remove(g_)
                        if first_:
                            for fn_ in pend1:
                                fn_()
                            pend1.clear()
                            first_ = False
                for fn_ in pend1:
                    fn_()
                pend1.clear()
                if sec < 3:
                    continue
                for d in range(2):
                    dma(S, KDs[d][t0:t0 + 512, :].rearrange("(s p) f -> p s f", p=128), kda[d][:], [kda[d]], (), kda[d])
                vt = vtr.next()
                gt = gtr.next()
                for s in range(4):
                    psv = tm_group(hT, s, 1536)
                    cp(S, "dve", vt[:, s, :], psv[:], [psv], [vt])
                    psg = tm_group(hT, s, 2048)
                    act(S, gt[:, s, :], psg[:], AF.Silu, [psg], [gt])
                dma(S, VH[t0:t0 + 512, :].rearrange("(s p) f -> p s f", p=128), vt[:], [vt], (), vt)
                dma(S, GH[t0:t0 + 512, :].rearrange("(s p) f -> p s f", p=128), gt[:], [gt], (), gt)

                if sec < 4:
                    continue
                def lora_norm(col0, nchunk, gain, D):
                    for c in range(nchunk):
                        ps = fm_group(hT, col0 + c * 128, 128)
                        act(S, sqc[:, c, :], ps[:], AF.Square, [ps], [sqc])
                        ts(S, "dve", cqg[:, c, :], ps[:], gain[:, c:c + 1], None, ALU.mult, None, [ps, gain], [cqg])
                    pss = psX.next()
                    for c in range(nchunk):
                        mm(S, pss[:, :], onesb[:], sqc[:, c, :], c == 0, c == nchunk - 1, [onesb, sqc], [pss])
                    act(S, rsb[:], pss[:], AF.Sqrt, [pss], [rsb], scale=1.0 / D, bias=EPS)
                    S.op("dve", lambda e: e.reciprocal(out=rsb[:], in_=rsb[:]), [rsb], [rsb])
                    for c in range(nchunk):
                        tt(S, ("dve", "pool")[c % 2], cqn[:, c, :], cqg[:, c, :], rsb[:], ALU.mult, [cqg, rsb], [cqn])

                def up_fm(Wt, nchunk, col0, ncol):
                    ps = psM.next()
                    for c in range(nchunk):
                        mm(S, ps[0:ncol, :], Wt[:, c, col0:col0 + ncol], cqn[:, c, :], c == 0, c == nchunk - 1, [Wt, cqn], [ps])
                    return ps

                lora_norm(2560, 3, gqa, 384.0)
                for g in range(4):
                    ps = up_fm(Wqn, 3, g * 128, 128)
                    o = bfr.next()
                    cp(S, ("dve", "act")[g % 2], o[:], ps[:], [ps], [o])
                    for hh in range(2):
                        h = 2 * g + hh
                        dma(S, AQ[h * 96:h * 96 + 64, t0:t0 + 512], o[hh * 64:(hh + 1) * 64, :], [o], (), o)
                for g in range(2):
                    pp = up_fm(Wqp, 3, g * 128, 128)
                    pw = up_fm(Wqs, 3, g * 128, 128)
                    r1 = f32r.next()
                    r2 = f32r.next()
                    tt(S, "dve", r1[:], pp[:], cosT[:], ALU.mult, [pp, cosT], [r1])
                    tt(S, "dve", r2[:], pw[:], sinT[:], ALU.mult, [pw, sinT], [r2])
                    o = bfr.next()
                    tt(S, "pool", o[:], r1[:], r2[:], ALU.add, [r1, r2], [o])
                    for hh in range(4):
                        h = 4 * g + hh
                        dma(S, AQ[h * 96 + 64:h * 96 + 96, t0:t0 + 512], o[hh * 32:(hh + 1) * 32, :], [o], (), o)
                if sec < 5:
                    continue
                lora_norm(2944, 2, gkva, 256.0)
                for g in range(4):
                    ps = up_fm(Wkn, 2, g * 128, 128)
                    o = bfr.next()
                    cp(S, ("dve", "act")[g % 2], o[:], ps[:], [ps], [o])
                    dma(S, AKN[g * 128:(g + 1) * 128, t0:t0 + 512], o[:], [o], (), o)
                vaug = vaugr.next()
                for s in range(4):
                    ps = psM.next()
                    for c in range(2):
                        mm(S, ps[:, :], cqn[:, c, s * 128:(s + 1) * 128], Wv[:, c, :], c == 0, c == 1, [Wv, cqn], [ps])
                    cp(S, ("dve", "act")[s % 2], vaug[:, s, :].rearrange("p (h e) -> p h e", e=65)[:, :, 0:64],
                       ps[:].rearrange("p (h e) -> p h e", e=64), [ps], [vaug])
                dma(S, AV[t0:t0 + 512, :].rearrange("(s p) f -> p s f", p=128), vaug[:], [vaug], (), vaug)
                pk = fm_group(hT, 3200, 32)
                pks = fm_group(hT, 5280, 32)
                r1 = f32r.next()
                r2 = f32r.next()
                tt(S, "dve", r1[0:32, :], pk[0:32, :], cosT[0:32, :], ALU.mult, [pk, cosT], [r1])
                tt(S, "dve", r2[0:32, :], pks[0:32, :], sinT[0:32, :], ALU.mult, [pks, sinT], [r2])
                o = bfr.next()
                tt(S, "pool", o[0:32, :], r1[0:32, :], r2[0:32, :], ALU.add, [r1, r2], [o])
                dma(S, AKP[:, t0:t0 + 512], o[0:32, :], [o], (), o)
                if sec < 6:
                    continue
                for gi in range(2):
                    dst = (SGH, SGA)[gi]
                    for c in range(8):
                        ps = fm_group(hT, 3232 + gi * 1024 + c * 128, 128)
                        sg = bfr.next()
                        act(S, sg[:], ps[:], AF.Sigmoid, [ps], [sg])
                        dma(S, dst[c * 128:(c + 1) * 128, t0:t0 + 512], sg[:], [sg], (), sg)

        for d in (range(2) if upto >= 2 else ()):
            with S.phase():
                msk = S.sb([64, 64], F32, "msk")
                dma(S, msk[:], masks_d[d], (), [msk], msk)
                qTr = S.ring(2, [128, 4, 512], BF16, "qT")
                kTr = S.ring(2, [128, 4, 512], BF16, "kT")
                kdr = S.ring(2, [64, 8, 512], BF16, "kd")
                vr = S.ring(2, [64, 8, 512], BF16, "v")
                dcr = S.ring(2, [128, 4, 8], F32, "dc")
                Sf = S.sb([128, 4, 128], F32, "Sf")
                Sb = S.sb([128, 4, 128], BF16, "Sb")
                smr = S.ring(3, [64, 4, 64], BF16, "sm")
                otr = S.ring(2, [64, 8, 512], F32, "ot")
                ps_s = S.psring(2, "ps_s")
                ps_o = S.psring(2, "ps_o")
                ps_u = S.psring(2, "ps_u")
                if d == 1:
                    ofr = S.ring(2, [64, 8, 512], F32, "of")
                    ghr = S.ring(2, [64, 8, 512], BF16, "gh")
                    sqt = S.sb([64, 8, 512], F32, "sqt")
                    msr = S.ring(2, [64, 32], F32, "ms")
                    onb = S.ring(2, [64, 8, 512], BF16, "onb")
                    otT = S.ring(2, [128, 4, 512], BF16, "otT")
                    ghg = S.sb([64, 512], F32, "ghg")
                    dma(S, ghg[:], g_hg.partition_broadcast(64), (), [ghg], ghg)
                    psT2 = S.psring(2, "psT2")
                seq0 = 0
                for L in seq_lens:
                    ntile = L // 512
                    memset(S, "pool", Sf[:], 0.0, [Sf])
                    memset(S, "pool", Sb[:], 0.0, [Sb])
                    order = range(ntile) if d == 0 else range(ntile - 1, -1, -1)
                    for ti in order:
                        t0 = seq0 + ti * 512
                        n = t0 // 512
                        qT = qTr.next()
                        kT = kTr.next()
                        kd = kdr.next()
                        v = vr.next()
                        dc = dcr.next()
                        dma(S, qT[:], QTs[d][:, t0:t0 + 512].rearrange("(h k) t -> k h t", k=128), (), [qT], qT)
                        dma(S, kT[:], KTs[d][:, t0:t0 + 512].rearrange("(h k) t -> k h t", k=128), (), [kT], kT)
                        dma(S, kd[:], KDs[d][t0:t0 + 512, :].rearrange("(c j) f -> j c f", j=64), (), [kd], kd)
                        dma(S, v[:], VH[t0:t0 + 512, :].rearrange("(c j) f -> j c f", j=64), (), [v], v)
                        dma(S, dc[:], DECs[d][:, n * 8:(n + 1) * 8].rearrange("(h k) c -> k h c", k=128), (), [dc], dc)
                        ot = otr.next()
                        corder = list(range(8)) if d == 0 else list(range(7, -1, -1))

                        def scores(c, kT=kT, qT=qT):
                            cs = slice(c * 64, (c + 1) * 64)
                            pss = ps_s.next()
                            for h in range(4):
                                mm(S, pss[0:64, h * 64:(h + 1) * 64], kT[:, h, cs], qT[:, h, cs], True, True, [kT, qT], [pss])
                            sm = smr.next()
                            tt(S, "dve", sm[:], pss[0:64, 0:256].rearrange("p (h t) -> p h t", t=64),
                               msk[:].unsqueeze(1).to_broadcast([64, 4, 64]), ALU.mult, [pss, msk], [sm])
                            return sm

                        sm_next = scores(corder[0])
                        for ci, c in enumerate(corder):
                            cs = slice(c * 64, (c + 1) * 64)
                            sm = sm_next
                            if ci + 1 < 8:
                                sm_next = scores(corder[ci + 1])
                            psu = ps_u.next()
                            for h in range(4):
                                hs = slice(h * 128, (h + 1) * 128)
                                mm(S, psu[:, hs], kd[:, c, hs], v[:, c, hs], True, True, [kd, v], [psu])
                            pso = ps_o.next()
                            for h in range(4):
                                hs = slice(h * 128, (h + 1) * 128)
                                mm(S, pso[0:64, hs], sm[:, h, :], v[:, c, hs], True, False, [sm, v], [pso])
                                mm(S, pso[0:64, hs], qT[:, h, cs], Sb[:, h, :], False, True, [qT, Sb], [pso])
                            cp(S, "act", ot[:, c, :], pso[0:64, :], [pso], [ot])
                            tt(S, "dve", Sf[:], Sf[:], dc[:, :, c].unsqueeze(2).to_broadcast([128, 4, 128]), ALU.mult, [Sf, dc], [Sf])
                            tt(S, "dve", Sf[:], Sf[:], psu[:].rearrange("p (h e) -> p h e", e=128), ALU.add, [Sf, psu], [Sf])
                            cp(S, "act", Sb[:], Sf[:], [Sf], [Sb])
                        if d == 0:
                            dma(S, OF[t0:t0 + 512, :].rearrange("(c j) f -> j c f", j=64), ot[:], [ot], (), ot, eng="pool")
                        else:
                            of = ofr.next()
                            gh = ghr.next()
                            dma(S, of[:], OF[t0:t0 + 512, :].rearrange("(c j) f -> j c f", j=64), (), [of], of)
                            dma(S, gh[:], GH[t0:t0 + 512, :].rearrange("(c j) f -> j c f", j=64), (), [gh], gh)
                            tt(S, "pool", ot[:], ot[:], of[:], ALU.add, [ot, of], [ot])
                            tt(S, "dve", sqt[:], ot[:], ot[:], ALU.mult, [ot], [sqt])
                            ms = msr.next()
                            S.op("dve", lambda e, ms=ms: e.tensor_reduce(out=ms[:], in_=sqt[:].rearrange("p c (h e) -> p (c h) e", e=128),
                                                                       axis=AX.X, op=ALU.add), [sqt], [ms])
                            act(S, ms[:], ms[:], AF.Sqrt, [ms], [ms], scale=1.0 / 128, bias=EPS)
                            S.op("dve", lambda e, ms=ms: e.reciprocal(out=ms[:], in_=ms[:]), [ms], [ms])
                            ot4 = ot[:].rearrange("p c (h e) -> p (c h) e", e=128)
                            tt(S, "dve", ot4, ot4, ms[:].unsqueeze(2).to_broadcast([64, 32, 128]), ALU.mult, [ot, ms], [ot])
                            tt(S, "pool", ot[:], ot[:], ghg[:].unsqueeze(1).to_broadcast([64, 8, 512]), ALU.mult, [ot, ghg], [ot])
                            ob = onb.next()
                            tt(S, "dve", ob[:], ot[:], gh[:], ALU.mult, [ot, gh], [ob])
                            oT = otT.next()
                            for hc in range(4):
                                pt = psT2.next()
                                ptb = pt[:].bitcast(BF16)
                                for c in range(8):
                                    tr(S, ptb[:, c * 64:(c + 1) * 64], ob[:, c, hc * 128:(hc + 1) * 128], identb[0:64, 0:64], [ob, identb], [pt])
                                cp(S, "act", oT[:, hc, :], ptb[:, 0:512], [pt], [oT])
                            dma(S, OT[:, t0:t0 + 512].rearrange("(c p) t -> p c t", p=128), oT[:], [oT], (), oT, eng="pool")
                    seq0 += L

        for _ in ((1,) if upto >= 3 else ()):
          with S.phase():
            Lmax = max(seq_lens)
            KTr = S.ring(2, [96, Lmax], BF16, "KT")
            Vall = S.sb([128, Lmax // 128, 520], BF16, "Vall")
            QTr = S.ring(3, [96, 512], BF16, "QTt")
            pTr = S.ring(4, [128, 512], BF16, "pT")
            ps_s = S.psring(4, "a_s")
            ps_o = S.psring(2, "a_o")
            ps_b = S.psring(2, "a_b")
            osr = S.ring(2, [65, 512], F32, "osb")
            atr = S.ring(2, [64, 512], BF16, "at")
            scale = float(96 ** -0.5)
            lur = S.ring(2, [128, 1024], F32, "lu")
            lvr = S.ring(2, [128, 1024], F32, "lv")
            uvo = S.ring(2, [128, 2048], BF16, "uvo")
            conv_left = list(range(n_exp // 128)) if upto >= 5 else []

            def conv_some(k):
                for _ in range(k):
                    if not conv_left:
                        return
                    r = conv_left.pop(0)
                    lu = lur.next(); lv = lvr.next(); o = uvo.next()
                    dma(S, lu[:], pu[r * 128:(r + 1) * 128, :], (), [lu], lu)
                    dma(S, lv[:], pv[r * 128:(r + 1) * 128, :], (), [lv], lv)
                    cp(S, "dve", o[:, 0:1024], lu[:], [lu], [o])
                    cp(S, "pool", o[:, 1024:2048], lv[:], [lv], [o])
                    dma(S, UV[r * 128:(r + 1) * 128, :], o[:], [o], (), o)
            items = []
            seq0 = 0
            for si, L in enumerate(seq_lens):
                for h in range(8):
                    for qi in range(L // 512):
                        items.append((si, seq0, L, h, qi))
                seq0 += L
            loaded = {}

            def prefetch(i):
                si, seq0, L, h, qi = items[i]
                if (si, h) not in loaded:
                    KT = KTr.next()
                    dma(S, KT[0:64, 0:L], AKN[h * 64:(h + 1) * 64, seq0:seq0 + L], (), [KT], KT)
                    dma(S, KT[64:96, 0:L], AKP[:, seq0:seq0 + L], (), [KT], KT)
                    loaded[(si, h)] = KT
                QTt = QTr.next()
                q0 = seq0 + qi * 512
                dma(S, QTt[:], AQ[h * 96:(h + 1) * 96, q0:q0 + 512], (), [QTt], QTt)
                loaded[i] = QTt

            prefetch(0)
            cur_seq = -1
            for i in range(len(items)):
                si, seq0, L, h, qi = items[i]
                nk = L // 128
                if si != cur_seq:
                    dma(S, Vall[:, 0:nk, :], AV[seq0:seq0 + L, :].rearrange("(n j) f -> j n f", j=128), (), [Vall], Vall)
                    cur_seq = si
                if i + 1 < len(items):
                    prefetch(i + 1)
                conv_some(-(-(n_exp // 128) // len(items)))
                KT = loaded[(si, h)]
                QTt = loaded.pop(i)
                q0 = seq0 + qi * 512
                pso = ps_o.next()
                pend = []
                LOOK = 2

                def pvmm(j, pT, pso=pso, h=h, nk=nk):
                    mm(S, pso[0:65, :], Vall[:, j, h * 65:(h + 1) * 65], pT[:], j == 0, j == nk - 1, [Vall, pT], [pso])

                for j in range(nk):
                    pss = ps_s.next()
                    mm(S, pss[:, :], KT[0:96, j * 128:(j + 1) * 128], QTt[:], True, True, [KT, QTt], [pss])
                    pT = pTr.next()
                    act(S, pT[:], pss[:], AF.Exp, [pss], [pT], scale=scale)
                    pend.append((j, pT))
                    if len(pend) > LOOK:
                        pvmm(*pend.pop(0))
                while pend:
                    pvmm(*pend.pop(0))
                osb = osr.next()
                cp(S, "dve", osb[:], pso[0:65, :], [pso], [osb])
                S.op("dve", lambda e, osb=osb: e.reciprocal(out=osb[64:65, :], in_=osb[64:65, :]), [osb], [osb])
                psb = ps_b.next()
                mm(S, psb[0:64, :], onesf[64:65, 0:64], osb[64:65, :], True, True, [onesf, osb], [psb])
                at = atr.next()
                tt(S, "dve", at[:], osb[0:64, :], psb[0:64, :], ALU.mult, [osb, psb], [at])
                dma(S, AT[h * 64:(h + 1) * 64, q0:q0 + 512], at[:], [at], (), at)
            conv_some(len(conv_left))

        for _ in ((1,) if upto >= 4 else ()):
          with S.phase():
            stg_ring = S.ring(3, [128, 1024], F32, "stg")
            Wbh = S.sb([128, 4, 1024], BF16, "Wbh")
            Wba = S.sb([64, 8, 1024], BF16, "Wba")
            Wo = S.sb([128, 8, 1024], BF16, "Wo")
            k = 0
            for c in range(4):
                load_w_bf16(Wbh, Wbh[:, c, :], w_brh[c * 128:(c + 1) * 128, :], [128, 1024], stg_ring, k); k += 1
            for h in range(8):
                load_w_bf16(Wba, Wba[:, h, :], w_bra[h * 64:(h + 1) * 64, :], [64, 1024], stg_ring, k); k += 1
            for c in range(8):
                load_w_bf16(Wo, Wo[:, c, :], w_out[c * 128:(c + 1) * 128, :], [128, 1024], stg_ring, k); k += 1
            OTr = S.ring(2, [128, 4, 512], BF16, "OTt")
            ATr = S.ring(2, [64, 8, 512], BF16, "ATt")
            sghr = S.ring(2, [128, 8, 512], BF16, "sgh")
            sgar = S.ring(2, [128, 8, 512], BF16, "sga")
            mgr = S.ring(2, [128, 8, 512], BF16, "mg")
            t1r = S.ring(3, [128, 512], F32, "t1")
            t2r = S.ring(3, [128, 512], F32, "t2")
            xr = S.ring(3, [128, 1024], F32, "x4")
            x1r = S.ring(3, [128, 1024], F32, "x1")
            psh = S.psring(2, "psh")
            psa = S.psring(2, "psa")
            psd = S.psring(4, "psd")
            for n in range(NT):
                t0 = n * 512
                OTt = OTr.next(); ATt = ATr.next(); sgh = sghr.next(); sga = sgar.next()
                dma(S, OTt[:], OT[:, t0:t0 + 512].rearrange("(c p) t -> p c t", p=128), (), [OTt], OTt)
                dma(S, ATt[:], AT[:, t0:t0 + 512].rearrange("(h e) t -> e h t", e=64), (), [ATt], ATt)
                dma(S, sgh[:], SGH[:, t0:t0 + 512].rearrange("(c p) t -> p c t", p=128), (), [sgh], sgh)
                dma(S, sga[:], SGA[:, t0:t0 + 512].rearrange("(c p) t -> p c t", p=128), (), [sga], sga)
                mg = mgr.next()
                for m in range(8):
                    ph = psh.next()
                    for c in range(4):
                        mm(S, ph[:, :], Wbh[:, c, m * 128:(m + 1) * 128], OTt[:, c, :], c == 0, c == 3, [Wbh, OTt], [ph])
                    pa = psa.next()
                    for h in range(8):
                        mm(S, pa[:, :], Wba[:, h, m * 128:(m + 1) * 128], ATt[:, h, :], h == 0, h == 7, [Wba, ATt], [pa])
                    t1 = t1r.next(); t2 = t2r.next()
                    tt(S, "dve", t1[:], ph[:], sgh[:, m, :], ALU.mult, [ph, sgh], [t1])
                    tt(S, "dve", t2[:], pa[:], sga[:, m, :], ALU.mult, [pa, sga], [t2])
                    tt(S, "pool", mg[:, m, :], t1[:], t2[:], ALU.add, [t1, t2], [mg])
                for s in range(4):
                    xt = xr.next()
                    dma(S, xt[:], x[t0 + s * 128:t0 + (s + 1) * 128, :], (), [xt], xt)
                    x1 = x1r.next()
                    for hf in range(2):
                        pd = psd.next()
                        for c in range(8):
                            mm(S, pd[:, :], mg[:, c, s * 128:(s + 1) * 128], Wo[:, c, hf * 512:(hf + 1) * 512], c == 0, c == 7, [mg, Wo], [pd])
                        tt(S, "dve", x1[:, hf * 512:(hf + 1) * 512], xt[:, hf * 512:(hf + 1) * 512], pd[:], ALU.add, [xt, pd], [x1])
                    dma(S, X1[t0 + s * 128:t0 + (s + 1) * 128, :], x1[:], [x1], (), x1, eng="pool")

        for _ in ((1,) if upto >= 5 else ()):
          with S.phase():
            stg_ring = S.ring(3, [128, 1024], F32, "stg")
            Wpq = S.sb([128, 8, 2048], BF16, "Wpq")
            k = 0
            for c in range(8):
                load_w_bf16(Wpq, Wpq[:, c, :], w_pq[c * 128:(c + 1) * 128, :], [128, 2048], stg_ring, k); k += 1
            skT = S.sb([128, 16, 128], BF16, "skT")
            skb = S.sb([128, 128], BF16, "skb")
            psT = S.psring(2, "psT")
            for g in range(16):
                stg = stg_ring.next()
                dma(S, stg[:, 0:128], skeys[g], (), [stg], stg)
                cp(S, "dve", skb[:], stg[:, 0:128], [stg], [skb])
                pt = psT.next()
                ptb = pt[:].bitcast(BF16)
                tr(S, ptb[:, 0:128], skb[:], identb[:], [skb, identb], [pt])
                cp(S, "act", skT[:, g, :], ptb[:, 0:128], [pt], [skT])
            gffn_c = S.sb([128, 8], F32, "gffn_c")
            dma(S, gffn_c[:], g_ffn.rearrange("(c p) -> p c", p=128), (), [gffn_c], gffn_c, slow=True)
            gffn_b = S.sb([128, 1024], F32, "gffn_b")
            dma(S, gffn_b[:], g_ffn.partition_broadcast(128), (), [gffn_b], gffn_b)
            gfin_b = S.sb([128, 1024], F32, "gfin_b")
            dma(S, gfin_b[:], g_fin.partition_broadcast(128), (), [gfin_b], gfin_b)
            iota16 = S.sb([128, 16], F32, "iota16")
            dma(S, iota16[:], iota_d[:, :], (), [iota16], iota16)

            x1r = S.ring(2, [128, 1024], F32, "x1")
            junk = S.sb([128, 1024], BF16, "junk")
            junkb = S.sb([128, 1024], BF16, "junkb")
            ssr = S.ring(2, [128, 1], F32, "ss")
            rsr = S.ring(2, [128, 1], F32, "rs")
            xnbr = S.ring(1, [128, 1024], BF16, "xnb")
            zr = S.ring(2, [128, 1024], BF16, "z")
            zTr = S.ring(1, [128, 8, 128], BF16, "zT")
            qTr = S.ring(1, [128, 16, 128], BF16, "qT")
            psq = S.psring(1, "psq")
            pssc = S.psring(1, "pssc")
            psacc = S.psring(4, "psacc")
            scr_ = S.ring(1, [128, 16, 128], F32, "sc")
            wkr = S.ring(2, [128, 128], F32, "wk")
            V1r = S.ring(2, [128, 16, 16], F32, "V1")
            I1r = S.ring(2, [128, 16, 16], U32, "I1")
            I1fr = S.ring(2, [128, 16, 16], F32, "I1f")
            candr = S.ring(1, [128, 8, 256], F32, "cand")
            wk2r = S.ring(2, [128, 256], F32, "wk2")
            tvr = S.ring(2, [128, 8, 16], F32, "tv")
            posr = S.ring(2, [128, 8, 16], U32, "pos")
            pir = S.ring(2, [128, 8, 16], U32, "pi")
            pjr = S.ring(2, [128, 8, 16], U32, "pj")
            fir = S.ring(2, [128, 8, 16], F32, "fi")
            fjr = S.ring(2, [128, 8, 16], F32, "fj")
            eqr = S.ring(1, [128, 8, 16, 16], BF16, "eq")
            eir = S.ring(2, [128, 8, 16], F32, "ei")
            ejr = S.ring(2, [128, 8, 16], F32, "ej")
            eidr = S.ring(2, [128, 128], I32, "eid")
            gwr = S.ring(2, [128, 8, 16], F32, "gw")
            zsr = S.ring(2, [128, 8], F32, "zs")
            dotr = S.ring(8, [128, 4], F32, "dots")
            actr = S.ring(8, [128, 4], F32, "actv")
            uvr = S.ring(20, [128, 2048], BF16, "uvg")
            dgr = S.ring(6, [128, 128], BF16, "dg")
            accr = S.ring(1, [128, 1024], F32, "acc")

            def prep(n, st):
                t0 = n * 128
                x1 = x1r.next()
                dma(S, x1[:], X1[t0:t0 + 128, :], (), [x1], x1)
                ss = ssr.next(); rs = rsr.next()
                act(S, junk[:], x1[:], AF.Square, [x1], [junk, ss], accum=ss[:, 0:1])
                rstd_from_ss(S, rs, ss, 1024.0, None)
                xnb = xnbr.next()
                act(S, xnb[:], x1[:], AF.Copy, [x1, rs], [xnb], scale=rs[:, 0:1])
                z = zr.next()
                stt(S, z[:], x1[:], rs[:, 0:1], gffn_b[:], ALU.mult, ALU.mult, [x1, rs, gffn_b], [z])
                pt = psT.next()
                ptb = pt[:].bitcast(BF16)
                for c in range(8):
                    tr(S, ptb[:, c * 128:(c + 1) * 128], xnb[:, c * 128:(c + 1) * 128], identb[:], [xnb, identb], [pt])
                zT = zTr.next()
                for c in range(8):
                    act(S, zT[:, c, :], ptb[:, c * 128:(c + 1) * 128], AF.Copy, [pt, gffn_c], [zT], scale=gffn_c[:, c:c + 1])
                yield
                qT = qTr.next()
                for gq in range(4):
                    pq = psq.next()
                    for gg in range(4):
                        g = gq * 4 + gg
                        for c in range(8):
                            mm(S, pq[:, gg * 128:(gg + 1) * 128], Wpq[:, c, g * 128:(g + 1) * 128], zT[:, c, :], c == 0, c == 7, [Wpq, zT], [pq])
                        yield
                    cp(S, "act", qT[:, gq * 4:(gq + 1) * 4, :], pq[:].rearrange("p (g t) -> p g t", t=128), [pq], [qT])
                sc = scr_.next()
                for gq in range(4):
                    pc = pssc.next()
                    for gg in range(4):
                        g = gq * 4 + gg
                        mm(S, pc[:, gg * 128:(gg + 1) * 128], qT[:, g, :], skT[:, g, :], True, True, [qT, skT], [pc])
                    cp(S, "act", sc[:, gq * 4:(gq + 1) * 4, :], pc[:].rearrange("p (g t) -> p g t", t=128), [pc], [sc])
                    yield
                yield
                V1 = V1r.next(); I1 = I1r.next()
                for g in range(16):
                    wk = wkr.next()
                    S.op("dve", lambda e, V1=V1, sc=sc, g=g: e.max(out=V1[:, g, 0:8], in_=sc[:, g, :]), [sc], [V1])
                    S.op("dve", lambda e, V1=V1, I1=I1, sc=sc, g=g: e.max_index(out=I1[:, g, 0:8], in_max=V1[:, g, 0:8], in_values=sc[:, g, :]), [sc, V1], [I1])
                    S.op("dve", lambda e, V1=V1, wk=wk, sc=sc, g=g: e.match_replace(out=wk[:], in_to_replace=V1[:, g, 0:8], in_values=sc[:, g, :], imm_value=-1e30), [sc, V1], [wk])
                    S.op("dve", lambda e, V1=V1, wk=wk, g=g: e.max(out=V1[:, g, 8:16], in_=wk[:]), [wk], [V1])
                    S.op("dve", lambda e, V1=V1, I1=I1, wk=wk, g=g: e.max_index(out=I1[:, g, 8:16], in_max=V1[:, g, 8:16], in_values=wk[:]), [wk, V1], [I1])
                    yield
                I1f = I1fr.next()
                cp(S, "dve", I1f[:], I1[:], [I1], [I1f])
                V1v = V1[:].rearrange("p (h two) k -> p h two k", two=2)
                I1v = I1f[:].rearrange("p (h two) k -> p h two k", two=2)
                cand = candr.next()
                tt(S, "dve", cand[:].rearrange("p h (i j) -> p h i j", j=16),
                   V1v[:, :, 0, :].unsqueeze(3).to_broadcast([128, 8, 16, 16]),
                   V1v[:, :, 1, :].unsqueeze(2).to_broadcast([128, 8, 16, 16]), ALU.add, [V1], [cand])
                yield
                tv = tvr.next(); pos = posr.next()
                for h in range(8):
                    wk2 = wk2r.next()
                    S.op("dve", lambda e, tv=tv, cand=cand, h=h: e.max(out=tv[:, h, 0:8], in_=cand[:, h, :]), [cand], [tv])
                    S.op("dve", lambda e, tv=tv, pos=pos, cand=cand, h=h: e.max_index(out=pos[:, h, 0:8], in_max=tv[:, h, 0:8], in_values=cand[:, h, :]), [cand, tv], [pos])
                    S.op("dve", lambda e, tv=tv, wk2=wk2, cand=cand, h=h: e.match_replace(out=wk2[:], in_to_replace=tv[:, h, 0:8], in_values=cand[:, h, :], imm_value=-1e30), [cand, tv], [wk2])
                    S.op("dve", lambda e, tv=tv, wk2=wk2, h=h: e.max(out=tv[:, h, 8:16], in_=wk2[:]), [wk2], [tv])
                    S.op("dve", lambda e, tv=tv, pos=pos, wk2=wk2, h=h: e.max_index(out=pos[:, h, 8:16], in_max=tv[:, h, 8:16], in_values=wk2[:]), [wk2, tv], [pos])
                    yield
                pi = pir.next(); pj = pjr.next(); fi = fir.next(); fj = fjr.next()
                S.op("dve", lambda e, pi=pi, pos=pos: e.tensor_single_scalar(out=pi[:], in_=pos[:], scalar=4, op=ALU.logical_shift_right), [pos], [pi])
                S.op("dve", lambda e, pj=pj, pos=pos: e.tensor_single_scalar(out=pj[:], in_=pos[:], scalar=15, op=ALU.bitwise_and), [pos], [pj])
                cp(S, "act", fi[:], pi[:], [pi], [fi])
                cp(S, "act", fj[:], pj[:], [pj], [fj])
                yield
                iob = iota16[:].unsqueeze(1).unsqueeze(1).to_broadcast([128, 8, 16, 16])
                ei = eir.next(); ej = ejr.next()
                for (ff, which, eo) in ((fi, 0, ei), (fj, 1, ej)):
                    eq = eqr.next()
                    tt(S, "dve", eq[:], ff[:].unsqueeze(3).to_broadcast([128, 8, 16, 16]), iob, ALU.is_equal, [ff, iota16], [eq])
                    yield
                    tt(S, "dve", eq[:], eq[:], I1v[:, :, which, :].unsqueeze(2).to_broadcast([128, 8, 16, 16]), ALU.mult, [eq, I1f], [eq])
                    yield
                    S.op("dve", lambda e, eo=eo, eq=eq: e.tensor_reduce(out=eo[:], in_=eq[:], axis=AX.X, op=ALU.add), [eq], [eo])
                    yield
                eid = eidr.next()
                stt(S, ei[:], ei[:], 128.0, ej[:], ALU.mult, ALU.add, [ei, ej], [ei])
                cp(S, "dve", eid[:], ei[:].rearrange("p h k -> p (h k)"), [ei], [eid])
                yield
                gw = gwr.next(); zs = zsr.next()
                tt(S, "dve", gw[:], tv[:], tv[:, :, 0:1].to_broadcast([128, 8, 16]), ALU.subtract, [tv], [gw])
                act(S, gw[:], gw[:], AF.Exp, [gw], [gw])
                S.op("dve", lambda e, zs=zs, gw=gw: e.tensor_reduce(out=zs[:], in_=gw[:], axis=AX.X, op=ALU.add), [gw], [zs])
                S.op("dve", lambda e, zs=zs: e.reciprocal(out=zs[:], in_=zs[:]), [zs], [zs])
                tt(S, "dve", gw[:], gw[:], zs[:].unsqueeze(2).to_broadcast([128, 8, 16]), ALU.mult, [gw, zs], [gw])
                st.update(x1=x1, z=z, eid=eid, gw=gw, t0=t0)
                yield

            def gather(st, nxt):
                x1 = st['x1']; z = st['z']; eid = st['eid']; gw = st['gw']; t0 = st['t0']
                gwf = gw[:].rearrange("p h k -> p (h k)")
                accA = psacc.next()
                accB = psacc.next()
                GB = 4
                for s0 in range(0, 128, GB):
                    rows = []
                    dots = dotr.next()
                    av = actr.next()
                    for sl in range(s0, s0 + GB):
                        uvg = uvr.next()
                        S.op("pool", lambda e, uvg=uvg, eid=eid, sl=sl: e.indirect_dma_start(
                            out=uvg[:], out_offset=None, in_=UV[:, :], in_offset=bass.IndirectOffsetOnAxis(ap=eid[:, sl:sl + 1], axis=0)),
                            [eid], [uvg], dma=uvg)
                        stt(S, junkb[:], uvg[:, 0:1024], 1.0, z[:], ALU.mult, ALU.mult, [z],
                            ([dots] if sl in (s0, s0 + GB - 1) else []), accum=dots[:, sl - s0:sl - s0 + 1], noreg=[uvg])
                        rows.append(uvg)
                    act(S, av[:], dots[:], AF.Gelu, [dots], [av])
                    for k_, sl in enumerate(range(s0, s0 + GB)):
                        uvg = rows[k_]
                        dg = dgr.next()
                        act(S, av[:, k_:k_ + 1], av[:, k_:k_ + 1], AF.Copy, [av, gw], [av], scale=gwf[:, sl:sl + 1])
                        act(S, dg[:], identb[:], AF.Copy, [identb, av], [dg], scale=av[:, k_:k_ + 1])
                        mm(S, accA[:, :], dg[:], uvg[:, 1024:1536], sl == 0, sl == 127, [dg, uvg], [accA])
                        mm(S, accB[:, :], dg[:], uvg[:, 1536:2048], sl == 0, sl == 127, [dg, uvg], [accB])
                    if nxt is not None:
                        next(nxt, None)
                        next(nxt, None)
                acc = accr.next()
                tt(S, "dve", acc[:, 0:512], x1[:, 0:512], accA[:, :], ALU.add, [x1, accA], [acc])
                tt(S, "dve", acc[:, 512:1024], x1[:, 512:1024], accB[:, :], ALU.add, [x1, accB], [acc])
                ss2 = ssr.next(); rs2 = rsr.next()
                act(S, junk[:], acc[:], AF.Square, [acc], [junk, ss2], accum=ss2[:, 0:1])
                rstd_from_ss(S, rs2, ss2, 1024.0, None)
                stt(S, acc[:], acc[:], rs2[:, 0:1], gfin_b[:], ALU.mult, ALU.mult, [acc, rs2, gfin_b], [acc])
                dma(S, y[t0:t0 + 128, :], acc[:], [acc], (), acc)


            ntile4 = TT // 128
            sts = [dict() for _ in range(ntile4)]
            for _ in prep(0, sts[0]):
                pass
            for n in range(ntile4):
                nxt = prep(n + 1, sts[n + 1]) if n + 1 < ntile4 else None
                gather(sts[n], nxt)
                if nxt is not None:
                    for _ in nxt:
                        pass

        S.barrier()
        S.emit()
    return nc


def _consts():
    half = 16
    inv = (1.0 / (10000.0 ** (np.arange(half, dtype=np.float32) / half))).astype(np.float32)
    ang = np.arange(8192, dtype=np.float32)[:, None] * inv[None, :]
    cos, sin = np.cos(ang).astype(np.float32).T, np.sin(ang).astype(np.float32).T
    cos32 = np.concatenate([cos, cos], 0)
    sin32 = np.concatenate([-sin, sin], 0)
    cos4 = np.ascontiguousarray(np.tile(cos32, (4, 1)))
    sin4 = np.ascontiguousarray(np.tile(sin32, (4, 1)))
    ident = np.eye(128, dtype=np.float32)
    s = np.arange(64)
    mU = (s[:, None] <= s[None, :]).astype(np.float32)
    mL = (s[:, None] >= s[None, :]).astype(np.float32)
    masks = np.stack([mU, mL], 0)
    iota16 = np.tile(np.arange(16, dtype=np.float32)[None, :], (128, 1))
    return dict(cos4=cos4, sin4=sin4, ident=ident, masks=masks, iota16=np.ascontiguousarray(iota16))


def _prep_weights(lb_logits, norm_mix, w_in, hg_out_norm, w_br_h, q_a_norm, w_uq, kv_a_norm, w_ukv,
                  w_br_a, w_out, norm_ffn, peer_wq, peer_sub_keys, peer_u, peer_v, norm_final):
    f = lambda a: np.ascontiguousarray(np.asarray(a, dtype=np.float32))
    w_in0 = f(w_in[0])
    kpe = w_in0[:, 3200:3232]
    kpe_sw = np.concatenate([kpe[:, 16:32], kpe[:, 0:16]], 1)
    w_in_ext = np.concatenate([w_in0, kpe_sw], 1)
    wuq = f(w_uq[0]).reshape(384, 8, 96)
    wq_n = wuq[:, :, 0:64].reshape(384, 512)
    wq_p = wuq[:, :, 64:96].reshape(384, 256)
    wq_s = np.concatenate([wuq[:, :, 80:96], wuq[:, :, 64:80]], 2).reshape(384, 256)
    wukv = f(w_ukv[0]).reshape(256, 8, 128)
    wk_n = wukv[:, :, 0:64].reshape(256, 512)
    wv = wukv[:, :, 64:128].reshape(256, 512)
    sk = f(peer_sub_keys[0])
    skeys = sk.transpose(1, 0, 2, 3).reshape(16, 128, 128)
    d = dict(w_in=w_in_ext, lbl=f(lb_logits), g_mix=f(norm_mix[0]), g_hg=f(hg_out_norm[0]), w_brh=f(w_br_h[0]),
             g_qa=f(q_a_norm[0]), wq_n=wq_n, wq_p=wq_p, wq_s=wq_s, g_kva=f(kv_a_norm[0]), wk_n=wk_n, wv=wv,
             w_bra=f(w_br_a[0]), w_out=f(w_out[0]), g_ffn=f(norm_ffn[0]), w_pq=f(peer_wq[0]), skeys=skeys,
             pu=f(peer_u[0]), pv=f(peer_v[0]), g_fin=f(norm_final))
    d = {k: np.ascontiguousarray(v) for k, v in d.items()}
    d.update(_consts())
    return d


def kernel(x_prompt, x_sample, lb_logits, norm_mix, w_in, hg_out_norm, w_br_h, q_a_norm, w_uq,
           kv_a_norm, w_ukv, w_br_a, w_out, norm_ffn, peer_wq, peer_sub_keys, peer_u, peer_v, norm_final):
    xp = np.asarray(x_prompt, dtype=np.float32)
    xs = np.asarray(x_sample, dtype=np.float32)
    B, SP, D = xp.shape
    BS, SS, _ = xs.shape
    n = 8
    ppc = B // n
    spc = BS // n
    seq_lens = [SP] * ppc + [SS] * spc
    wd = _prep_weights(lb_logits, norm_mix, w_in, hg_out_norm, w_br_h, q_a_norm, w_uq, kv_a_norm, w_ukv,
                       w_br_a, w_out, norm_ffn, peer_wq, peer_sub_keys, peer_u, peer_v, norm_final)
    nc = build(seq_lens)
    in_maps = []
    for c in range(n):
        parts = [xp[c * ppc + i] for i in range(ppc)] + [xs[c * spc + i] for i in range(spc)]
        m = dict(wd)
        m["x"] = np.ascontiguousarray(np.concatenate(parts, 0))
        in_maps.append(m)
    res = run_bass_kernel_spmd(nc, in_maps, core_ids=list(range(n)))
    yp = np.empty_like(xp)
    ys = np.empty_like(xs)
    for c in range(n):
        yc = res.results[c]["y"]
        o = 0
        for i in range(ppc):
            yp[c * ppc + i] = yc[o:o + SP]; o += SP
        for i in range(spc):
            ys[c * spc + i] = yc[o:o + SS]; o += SS
    return (yp, ys)
```
